# Optimizing a Trainium2 kernel written in Bass

```python
import math
import jax, jax.numpy as jnp
from jax import lax
import numpy as np

D_MODEL = 1024
BATCH = 2
SEQ = 16384
DEPTH = 1
DEC_BATCH = 16
DEC_SEQ = 64
PAST_LEN = 4096

CHUNK = 64
GLA_HEADS = 4
GLA_DK = 64
GLA_DV = 128
GLA_KEY = GLA_HEADS * GLA_DK
GLA_VAL = GLA_HEADS * GLA_DV
GLA_GATE_RANK = 16
GLA_GATE_TAU = 16.0
S5_WIDTH = 512
S5_GROUP = 16
S5_GROUPS = S5_WIDTH // S5_GROUP
S5_STATE = 64
D_FF = 2816
EPS = 1e-6
IN_SIZES = (GLA_KEY, GLA_KEY, GLA_VAL, GLA_VAL, GLA_GATE_RANK, S5_WIDTH, D_MODEL, D_MODEL)
IN_COLS = sum(IN_SIZES)

kernel_name = "hybrid_gla_s5_macaron_stream_step"


def _rmsnorm(x, g):
    xf = x.astype(jnp.float32)
    xf = xf * lax.rsqrt(jnp.mean(xf * xf, axis=-1, keepdims=True) + EPS)
    return (xf * g.astype(jnp.float32)).astype(x.dtype)


def _swiglu(x, w_gate, w_up, w_down):
    return (jax.nn.silu(x @ w_gate) * (x @ w_up)) @ w_down


def _heads(z, n):
    b, t, _ = z.shape
    return z.reshape(b, t, n, -1).transpose(0, 2, 1, 3)


def _gla_chunked(q, k, v, log_a, s0, chunk):
    bsz, nh, t, dk = q.shape
    dv = v.shape[-1]
    n = t // chunk

    def to_chunks(z):
        return jnp.moveaxis(z.reshape(bsz, nh, n, chunk, z.shape[-1]), 2, 0)

    causal = jnp.tril(jnp.ones((chunk, chunk), dtype=bool))[:, :, None]

    def step(s, inp):
        qi, ki, vi, ai = inp
        b = jnp.cumsum(ai, axis=2)
        diff = b[:, :, :, None, :] - b[:, :, None, :, :]
        decay = jnp.exp(jnp.where(causal, diff, -jnp.inf))
        scores = jnp.einsum('bhtd,bhsd,bhtsd->bhts', qi, ki, decay)
        o = jnp.einsum('bhts,bhse->bhte', scores, vi) + jnp.einsum('bhtd,bhde->bhte', qi * jnp.exp(b), s)
        b_last = b[:, :, -1:, :]
        s_new = jnp.exp(b_last[:, :, 0, :])[..., None] * s + jnp.einsum('bhsd,bhse->bhde', ki * jnp.exp(b_last - b), vi)
        return s_new, o

    s_fin, oc = lax.scan(step, s0, (to_chunks(q), to_chunks(k), to_chunks(v), to_chunks(log_a)))
    o = jnp.moveaxis(oc, 0, 2).reshape(bsz, nh, t, dv)
    return o, s_fin


def _s5_scan(u, lam_re, lam_im, log_dt, b_re, b_im, c_re, c_im, d_skip, x0):
    f32 = jnp.float32
    bsz, t, _ = u.shape
    uf = u.astype(f32)
    ug = uf.reshape(bsz, t, S5_GROUPS, S5_GROUP)
    lr, li = lam_re.astype(f32), lam_im.astype(f32)
    dt = jnp.exp(log_dt.astype(f32))[:, None]
    mag = jnp.exp(lr * dt)
    ab_re, ab_im = mag * jnp.cos(li * dt), mag * jnp.sin(li * dt)
    nr, ni = ab_re - 1.0, ab_im
    den = lr * lr + li * li
    f_re = (nr * lr + ni * li) / den
    f_im = (ni * lr - nr * li) / den
    br, bi = b_re.astype(f32), b_im.astype(f32)
    bb_re = f_re[..., None] * br - f_im[..., None] * bi
    bb_im = f_re[..., None] * bi + f_im[..., None] * br
    bu_re = jnp.einsum('btgh,gph->btgp', ug, bb_re)
    bu_im = jnp.einsum('btgh,gph->btgp', ug, bb_im)
    x0r, x0i = x0[..., 0].astype(f32), x0[..., 1].astype(f32)
    bu_re = bu_re.at[:, 0].add(ab_re * x0r - ab_im * x0i)
    bu_im = bu_im.at[:, 0].add(ab_re * x0i + ab_im * x0r)
    a_re = jnp.broadcast_to(ab_re, bu_re.shape)
    a_im = jnp.broadcast_to(ab_im, bu_im.shape)

    def combine(e1, e2):
        a1r, a1i, b1r, b1i = e1
        a2r, a2i, b2r, b2i = e2
        return (a1r * a2r - a1i * a2i,
                a1r * a2i + a1i * a2r,
                a2r * b1r - a2i * b1i + b2r,
                a2r * b1i + a2i * b1r + b2i)

    _, _, xr, xi = lax.associative_scan(combine, (a_re, a_im, bu_re, bu_im), axis=1)
    y = jnp.einsum('gjp,btgp->btgj', c_re.astype(f32), xr) - jnp.einsum('gjp,btgp->btgj', c_im.astype(f32), xi)
    y = y.reshape(bsz, t, S5_WIDTH) + d_skip.astype(f32) * uf
    x_last = jnp.stack([xr[:, -1], xi[:, -1]], axis=-1)
    return y, x_last


def _layer(x, s_gla, s_s5, chunk, p):
    f32 = jnp.float32
    bsz, t, _ = x.shape
    h = x + 0.5 * _swiglu(_rmsnorm(x, p['g_ffn1']), p['w_ffn1_gate'], p['w_ffn1_up'], p['w_ffn1_down'])
    u = _rmsnorm(h, p['g_mix'])
    z = u @ p['w_in']
    splits = [sum(IN_SIZES[:i + 1]) for i in range(len(IN_SIZES) - 1)]
    q, k, v, r, ga, us5, gm_gla, gm_s5 = jnp.split(z, splits, axis=-1)
    log_a = jax.nn.log_sigmoid((ga @ p['w_gate_up'] + p['b_gate']).astype(f32)) / GLA_GATE_TAU
    qh = _heads(q.astype(f32), GLA_HEADS) * (GLA_DK ** -0.5)
    kh = _heads(k.astype(f32), GLA_HEADS)
    vh = _heads(v.astype(f32), GLA_HEADS)
    ah = _heads(log_a, GLA_HEADS)
    o, s_gla_new = _gla_chunked(qh, kh, vh, ah, s_gla.astype(f32), chunk)
    o = o.transpose(0, 2, 1, 3)
    o = o * lax.rsqrt(jnp.mean(o * o, axis=-1, keepdims=True) + EPS)
    o = o.reshape(bsz, t, GLA_VAL) * p['g_gla_head'].astype(f32)
    y_gla = (o * jax.nn.silu(r.astype(f32))).astype(h.dtype) @ p['w_gla_out']
    ys5, s_s5_new = _s5_scan(us5, p['s5_lam_re'], p['s5_lam_im'], p['s5_log_dt'], p['s5_b_re'], p['s5_b_im'],
                             p['s5_c_re'], p['s5_c_im'], p['s5_d'], s_s5)
    g5 = jax.nn.gelu(ys5).astype(h.dtype)
    y_s5 = ((g5 @ p['w_glu_a']) * jax.nn.sigmoid(g5 @ p['w_glu_b'])) @ p['w_s5_out']
    m = jax.nn.sigmoid(gm_gla) * y_gla + jax.nn.sigmoid(gm_s5) * y_s5
    h = h + m @ p['w_out']
    h = h + 0.5 * _swiglu(_rmsnorm(h, p['g_ffn2']), p['w_ffn2_gate'], p['w_ffn2_up'], p['w_ffn2_down'])
    return h, s_gla_new, s_s5_new


def setup_inputs(seed: int = 0) -> dict:
    key = jax.random.key(seed)
    ks = jax.random.split(key, 40)
    f32 = jnp.float32

    def nrm(k, shape, scale):
        return jax.random.normal(k, shape, f32) * scale

    def gain(k, n):
        return 1.0 + 0.05 * jax.random.normal(k, (n,), f32)

    lam_im_base = math.pi * jnp.arange(S5_STATE, dtype=f32)[None, :]
    return {
        'x_prompt': nrm(ks[0], (BATCH, SEQ, D_MODEL), 1.0),
        'x_sample': nrm(ks[1], (DEC_BATCH, DEC_SEQ, D_MODEL), 1.0),
        'state_gla': nrm(ks[2], (DEC_BATCH, GLA_HEADS, GLA_DK, GLA_DV), 0.5),
        'state_s5': nrm(ks[3], (DEC_BATCH, S5_GROUPS, S5_STATE, 2), 0.5),
        'g_ffn1': gain(ks[4], D_MODEL),
        'w_ffn1_gate': nrm(ks[5], (D_MODEL, D_FF), D_MODEL ** -0.5),
        'w_ffn1_up': nrm(ks[6], (D_MODEL, D_FF), D_MODEL ** -0.5),
        'w_ffn1_down': nrm(ks[7], (D_FF, D_MODEL), D_FF ** -0.5),
        'g_mix': gain(ks[8], D_MODEL),
        'w_in': nrm(ks[9], (D_MODEL, IN_COLS), D_MODEL ** -0.5),
        'w_gate_up': nrm(ks[10], (GLA_GATE_RANK, GLA_KEY), GLA_GATE_RANK ** -0.5),
        'b_gate': nrm(ks[11], (GLA_KEY,), 0.1),
        'g_gla_head': gain(ks[12], GLA_VAL),
        'w_gla_out': nrm(ks[13], (GLA_VAL, D_MODEL), GLA_VAL ** -0.5),
        's5_lam_re': -0.5 + nrm(ks[14], (S5_GROUPS, S5_STATE), 0.01),
        's5_lam_im': lam_im_base + nrm(ks[15], (S5_GROUPS, S5_STATE), 0.01),
        's5_log_dt': jax.random.uniform(ks[16], (S5_GROUPS,), f32, math.log(1e-3), math.log(1e-1)),
        's5_b_re': nrm(ks[17], (S5_GROUPS, S5_STATE, S5_GROUP), (0.5 / S5_GROUP) ** 0.5),
        's5_b_im': nrm(ks[18], (S5_GROUPS, S5_STATE, S5_GROUP), (0.5 / S5_GROUP) ** 0.5),
        's5_c_re': nrm(ks[19], (S5_GROUPS, S5_GROUP, S5_STATE), (0.5 / S5_STATE) ** 0.5),
        's5_c_im': nrm(ks[20], (S5_GROUPS, S5_GROUP, S5_STATE), (0.5 / S5_STATE) ** 0.5),
        's5_d': nrm(ks[21], (S5_WIDTH,), 1.0),
        'w_glu_a': nrm(ks[22], (S5_WIDTH, S5_WIDTH), S5_WIDTH ** -0.5),
        'w_glu_b': nrm(ks[23], (S5_WIDTH, S5_WIDTH), S5_WIDTH ** -0.5),
        'w_s5_out': nrm(ks[24], (S5_WIDTH, D_MODEL), S5_WIDTH ** -0.5),
        'w_out': nrm(ks[25], (D_MODEL, D_MODEL), D_MODEL ** -0.5),
        'g_ffn2': gain(ks[26], D_MODEL),
        'w_ffn2_gate': nrm(ks[27], (D_MODEL, D_FF), D_MODEL ** -0.5),
        'w_ffn2_up': nrm(ks[28], (D_MODEL, D_FF), D_MODEL ** -0.5),
        'w_ffn2_down': nrm(ks[29], (D_FF, D_MODEL), D_FF ** -0.5),
        'g_final': gain(ks[30], D_MODEL),
    }


def reference(x_prompt, x_sample, state_gla, state_s5, g_ffn1, w_ffn1_gate, w_ffn1_up, w_ffn1_down,
              g_mix, w_in, w_gate_up, b_gate, g_gla_head, w_gla_out, s5_lam_re, s5_lam_im, s5_log_dt,
              s5_b_re, s5_b_im, s5_c_re, s5_c_im, s5_d, w_glu_a, w_glu_b, w_s5_out, w_out,
              g_ffn2, w_ffn2_gate, w_ffn2_up, w_ffn2_down, g_final):
    p = {
        'g_ffn1': g_ffn1, 'w_ffn1_gate': w_ffn1_gate, 'w_ffn1_up': w_ffn1_up, 'w_ffn1_down': w_ffn1_down,
        'g_mix': g_mix, 'w_in': w_in, 'w_gate_up': w_gate_up, 'b_gate': b_gate,
        'g_gla_head': g_gla_head, 'w_gla_out': w_gla_out,
        's5_lam_re': s5_lam_re, 's5_lam_im': s5_lam_im, 's5_log_dt': s5_log_dt,
        's5_b_re': s5_b_re, 's5_b_im': s5_b_im, 's5_c_re': s5_c_re, 's5_c_im': s5_c_im, 's5_d': s5_d,
        'w_glu_a': w_glu_a, 'w_glu_b': w_glu_b, 'w_s5_out': w_s5_out, 'w_out': w_out,
        'g_ffn2': g_ffn2, 'w_ffn2_gate': w_ffn2_gate, 'w_ffn2_up': w_ffn2_up, 'w_ffn2_down': w_ffn2_down,
    }
    bp = x_prompt.shape[0]
    hp, gla_p, s5_p = x_prompt, jnp.zeros((bp, GLA_HEADS, GLA_DK, GLA_DV), jnp.float32), jnp.zeros((bp, S5_GROUPS, S5_STATE, 2), jnp.float32)
    hs, gla_s, s5_s = x_sample, state_gla, state_s5
    for _ in range(DEPTH):
        hp, gla_p, s5_p = _layer(hp, gla_p, s5_p, CHUNK, p)
        hs, gla_s, s5_s = _layer(hs, gla_s, s5_s, x_sample.shape[1], p)
    y_prompt = _rmsnorm(hp, g_final)
    y_sample = _rmsnorm(hs, g_final)
    return (y_prompt, y_sample, gla_p, s5_p, gla_s, s5_s)
```

```python
import math
from contextlib import ExitStack
import numpy as np
import concourse.bass as bass
import concourse.mybir as mybir
from concourse.bass_utils import run_bass_kernel_spmd

F32 = mybir.dt.float32
BF16 = mybir.dt.bfloat16
I32 = mybir.dt.int32
AF = mybir.ActivationFunctionType
ALU = mybir.AluOpType

D = 1024
KT = 8
FF = 2816
FT = 22
INC = 4112
EPS = 1e-6
NCORES = 8
NO_COLL = False
PREFIX_TILES = 24
WARM_COLL = False
TWO_PI = 2.0 * math.pi


class Buf:
    __slots__ = ("name", "last_w", "readers", "sem", "dcount")
    epoch_op = None

    def __init__(self, name):
        self.name = name
        self.last_w = Buf.epoch_op
        self.readers = []
        self.sem = None
        self.dcount = 0


class Op:
    __slots__ = ("eng", "fn", "deps", "is_dma", "needs_inc", "val", "sem", "inc")

    def __init__(self, eng, fn, is_dma):
        self.eng = eng
        self.fn = fn
        self.deps = {}
        self.is_dma = is_dma
        self.needs_inc = False
        self.val = None
        self.sem = None
        self.inc = 16


class Prog:
    ENGS = ("pe", "act", "dve", "pool", "sp")

    def __init__(self, nc):
        self.nc = nc
        self.ops = {e: [] for e in self.ENGS}
        self.all_ops = []
        self.dma_bufs = []

    def _dep(self, op, reads, writes):
        for r in reads:
            if r.last_w is not None:
                op.deps[r.last_w] = True
        for w in writes:
            if w.last_w is not None and w.last_w not in op.deps:
                op.deps[w.last_w] = False
            for rd in w.readers:
                if rd not in op.deps:
                    op.deps[rd] = False
        for r in reads:
            if not op.is_dma:
                r.readers = [x for x in r.readers if x.is_dma or x.eng != op.eng]
            r.readers.append(op)
        for w in writes:
            w.last_w = op
            w.readers = []

    def op(self, eng, fn, reads=(), writes=()):
        o = Op(eng, fn, False)
        self._dep(o, reads, writes)
        self.ops[eng].append(o)
        self.all_ops.append(o)
        return o

    def dma(self, out, in_, reads=(), writes=(), sbuf=None, eng="sp", slow=False):
        if slow:
            fn = lambda e: e.dma_start(out=out, in_=in_, allow_slow_non_contiguous=True)
        else:
            fn = lambda e: e.dma_start(out=out, in_=in_)
        o = Op(eng, fn, True)
        self._dep(o, reads, writes)
        if sbuf not in self.dma_bufs:
            self.dma_bufs.append(sbuf)
        sbuf.dcount += 16
        o.sem = sbuf
        o.val = sbuf.dcount
        self.ops[eng].append(o)
        self.all_ops.append(o)
        return o

    def coll(self, kind, src, dst, groups, reads, writes, sbuf, inc=1):
        fn = lambda e: e.collective_compute(kind, ALU.bypass, replica_groups=groups, ins=[src], outs=[dst])
        o = Op("pool", fn, True)
        self._dep(o, reads, writes)
        if sbuf not in self.dma_bufs:
            self.dma_bufs.append(sbuf)
        sbuf.dcount += inc
        o.sem = sbuf
        o.val = sbuf.dcount
        o.inc = inc
        self.ops["pool"].append(o)
        self.all_ops.append(o)
        return o

    def mm(self, out, lhsT, rhs, start, stop, reads, writes):
        return self.op("pe", lambda e: e.matmul(out, lhsT=lhsT, rhs=rhs, start=start, stop=stop),
                       reads, writes)

    def tr(self, out, in_, ident, reads, writes):
        return self.op("pe", lambda e: e.transpose(out=out, in_=in_, identity=ident), reads, writes)

    def act(self, out, in_, func, reads, writes, bias=None, scale=None, accum=None):
        kw = {}
        if bias is not None:
            kw["bias"] = bias
        if scale is not None:
            kw["scale"] = scale
        if accum is not None:
            kw["accum_out"] = accum
        return self.op("act", lambda e: e.activation(out=out, in_=in_, func=func, **kw), reads, writes)

    def tt(self, eng, out, in0, in1, op, reads, writes):
        return self.op(eng, lambda e: e.tensor_tensor(out=out, in0=in0, in1=in1, op=op), reads, writes)

    def ts(self, eng, out, in0, s1, op0, reads, writes, s2=None, op1=None):
        if op1 is None:
            return self.op(eng, lambda e: e.tensor_scalar(out=out, in0=in0, scalar1=s1, scalar2=None, op0=op0),
                           reads, writes)
        return self.op(eng, lambda e: e.tensor_scalar(out=out, in0=in0, scalar1=s1, scalar2=s2, op0=op0, op1=op1),
                       reads, writes)

    def stt(self, eng, out, in0, scalar, in1, op0, op1, reads, writes):
        return self.op(eng, lambda e: e.scalar_tensor_tensor(out=out, in0=in0, scalar=scalar, in1=in1,
                                                             op0=op0, op1=op1), reads, writes)

    def cp(self, eng, out, in_, reads, writes):
        if eng == "act":
            return self.act(out, in_, AF.Copy, reads, writes)
        return self.op(eng, lambda e: e.tensor_copy(out=out, in_=in_), reads, writes)

    def memset(self, eng, ap, val, writes):
        return self.op(eng, lambda e: e.memset(ap, val), (), writes)

    def fence(self, ap, fbuf, bufs):
        o = self.op("dve", lambda e: e.memset(ap, 0.0), (), list(bufs) + [fbuf])
        Buf.epoch_op = o
        return o

    def recip(self, out, in_, reads, writes):
        return self.op("dve", lambda e: e.reciprocal(out=out, in_=in_), reads, writes)

    @staticmethod
    def _need(o, d, raw):
        if d.is_dma:
            if o.is_dma and not raw:
                return False
            return True
        if d.eng == o.eng:
            if d.eng in ("pe", "pool"):
                return False
            return raw
        return True

    def emit(self, stack):
        nc = self.nc
        for o in self.all_ops:
            for d, raw in o.deps.items():
                if self._need(o, d, raw) and not d.is_dma:
                    d.needs_inc = True
        esem = {}
        for e in self.ENGS:
            esem[e] = stack.enter_context(nc.semaphore("s_" + e))
            c = 0
            for o in self.ops[e]:
                if not o.is_dma:
                    if o.needs_inc:
                        c += 1
                        o.val = c
                    o.sem = e
        for b in self.dma_bufs:
            b.sem = stack.enter_context(nc.semaphore("d_" + b.name))
        block = stack.enter_context(nc.Block())

        def replay(ename, eng):
            waited = {}
            for o in self.ops[ename]:
                for d, raw in o.deps.items():
                    if not self._need(o, d, raw):
                        continue
                    if d.is_dma:
                        key, sem = id(d.sem), d.sem.sem
                    else:
                        key, sem = d.sem, esem[d.sem]
                    if waited.get(key, 0) >= d.val:
                        continue
                    waited[key] = d.val
                    eng.wait_ge(sem, d.val)
                ins = o.fn(eng)
                if o.is_dma:
                    ins.then_inc(o.sem.sem, o.inc)
                elif o.needs_inc:
                    ins.then_inc(esem[ename], 1)
            if ename == "sp":
                for b in self.dma_bufs:
                    if waited.get(id(b), 0) < b.dcount:
                        eng.wait_ge(b.sem, b.dcount)

        @block.tensor
        def _(eng):
            replay("pe", eng)

        @block.scalar
        def _(eng):
            replay("act", eng)

        @block.vector
        def _(eng):
            replay("dve", eng)

        @block.gpsimd
        def _(eng):
            replay("pool", eng)

        @block.sync
        def _(eng):
            replay("sp", eng)


W_SPECS = [
    ("w_ffn1_gate", 1024, FF, "g_ffn1"), ("w_ffn1_up", 1024, FF, "g_ffn1"), ("w_ffn1_down", FF, 1024, None),
    ("w_in", 1024, INC, "g_mix"), ("w_gla_out", 512, 1024, "g_gla_head"), ("w_glu_a", 512, 512, None),
    ("w_glu_b", 512, 512, None), ("w_s5_out", 512, 1024, None), ("w_out", 1024, 1024, None),
    ("w_ffn2_gate", 1024, FF, "g_ffn2"), ("w_ffn2_up", 1024, FF, "g_ffn2"), ("w_ffn2_down", FF, 1024, None),
]
GAINS = [("g_ffn1", 8), ("g_mix", 8), ("g_ffn2", 8), ("g_gla_head", 4)]


def build(ntiles, balanced=True):
    Buf.epoch_op = None
    nc = bass.Bass("TRN2", target_bir_lowering=False)
    NPRE = PREFIX_TILES if balanced == "prefix" else 0
    NP = ntiles * 512
    NPIN = (ntiles + NPRE) * 512

    def din(name, shape, dt=F32):
        return nc.dram_tensor(name, list(shape), dt, kind="ExternalInput").ap()

    def dout(name, shape):
        return nc.dram_tensor(name, list(shape), F32, kind="ExternalOutput").ap()

    xp = din("xp", [NPIN, D])
    xs = din("xs", [128, D])
    sgla = din("sgla", [2, 128, 2, 128])
    ss5 = din("ss5", [64, 2, 32, 2])
    wd = {}
    ws = {}
    for name, K, N, _g in W_SPECS:
        wd[name] = din(name, [128, K // 128, N])
        ws[name] = nc.dram_tensor("scr_" + name, [128, K // 128, N], BF16).ap()
    gd = {name: din(name, [128, n]) for name, n in GAINS}
    g_final_d = din("g_final", [128, D])
    wgu_d = din("w_gate_up", [16, 256])
    bgate_d = din("b_gate", [1, 256])
    lamre_d = din("lam_re", [64, 32])
    lamim_d = din("lam_im", [64, 32])
    logdt_d = din("log_dt", [64, 32])
    bre_d = din("b_re", [64, 32, 16])
    bim_d = din("b_im", [64, 32, 16])
    cre_d = din("c_re", [64, 32, 16])
    cim_d = din("c_im", [64, 32, 16])
    dcol_d = din("dcol", [128, 32])
    ident_d = din("ident", [128, 128])
    triinc_d = din("tri_inc", [128, 128])
    trirev_d = din("tri_rev", [128, 128])
    cmask_d = din("cmask", [128, 128])
    s5mask_d = din("s5mask", [128, 128])
    flags_d = din("flags", [128, 8])
    hscr = nc.dram_tensor("hscr", [max(NP, 128) if balanced is True else 128, D], F32).ap()
    exsrc = nc.dram_tensor("exsrc", [128, 384], F32).ap()
    exdst = nc.dram_tensor("exdst", [8 * 128, 384], F32).ap()

    yp = dout("yp", [NP, D])
    ys = dout("ys", [128, D])
    glap_o = dout("glap", [128, 2, 128])
    s5p_o = dout("s5p", [64, 2, 32])
    glas_o = dout("glas", [2, 128, 2, 128])
    s5s_o = dout("s5s", [2, 64, 2, 32])

    with ExitStack() as st:
        P = Prog(nc)
        cnt = [0]

        def sb(shape, dt=F32, name=None):
            cnt[0] += 1
            nm = (name or "t") + "_%d" % cnt[0]
            return st.enter_context(nc.sbuf_tensor(nm, list(shape), dt)), Buf(nm)

        banks = []
        for i in range(6):
            t = st.enter_context(nc.psum_tensor("pb%d" % i, [128, 512], F32))
            banks.append((t, Buf("pb%d" % i)))
        bbanks = []
        for i in range(2):
            t = st.enter_context(nc.psum_tensor("pbb%d" % i, [128, 1024], BF16))
            bbanks.append((t, Buf("pbb%d" % i)))
        bk = [0]

        def nb():
            bk[0] = (bk[0] + 1) % 6
            return banks[bk[0]]

        bbk = [0]

        def nbb():
            bbk[0] = (bbk[0] + 1) % 2
            return bbanks[bbk[0]]

        identf, b_identf = sb([128, 128], F32, "identf")
        identb, b_identb = sb([128, 128], BF16, "identb")
        triinc, b_triinc = sb([128, 128], F32, "triinc")
        trirev, b_trirev = sb([128, 128], F32, "trirev")
        cmask, b_cmask = sb([128, 128], F32, "cmask")
        s5mask, b_s5mask = sb([128, 128], F32, "s5mask")
        gfin, b_gfin = sb([128, D], F32, "gfin")
        wgu, b_wgu = sb([16, 256], F32, "wgu")
        bgate, b_bgate = sb([1, 256], F32, "bgate")
        ones1, b_ones1 = sb([1, 128], F32, "ones1")
        dcol, b_dcol = sb([128, 32], F32, "dcol")
        T0, b_T0 = sb([128, 32, 128], BF16, "T0")
        Wm, b_Wm = sb([128, 32, 2, 64], BF16, "Wm")
        Vm, b_Vm = sb([64, 32, 2, 128], BF16, "Vm")
        A1, b_A1 = sb([64, 2, 32], F32, "A1")
        A2, b_A2 = sb([64, 2, 32], F32, "A2")
        AB1, b_AB1 = sb([64, 2, 32], F32, "AB1")
        AB2, b_AB2 = sb([64, 2, 32], F32, "AB2")
        flags, b_flags = sb([128, 8], F32, "flags")
        Dtot, b_Dtot = sb([128, 2], F32, "Dtot")
        exs, b_exs = sb([128, 384], F32, "exs")
        exg, b_exg = sb([128, 384], F32, "exg")
        gcols = {}
        for name, n in GAINS:
            gcols[name] = sb([128, n], F32, name)

        for t_, b_, d_ in [(identf, b_identf, ident_d), (triinc, b_triinc, triinc_d), (trirev, b_trirev, trirev_d),
                           (cmask, b_cmask, cmask_d), (s5mask, b_s5mask, s5mask_d), (gfin, b_gfin, g_final_d),
                           (wgu, b_wgu, wgu_d), (bgate, b_bgate, bgate_d), (dcol, b_dcol, dcol_d), (flags, b_flags, flags_d)]:
            P.dma(t_[:], d_, writes=[b_], sbuf=b_)
        for name, n in GAINS:
            P.dma(gcols[name][0][:], gd[name], writes=[gcols[name][1]], sbuf=gcols[name][1])
        if balanced is True and not NO_COLL and WARM_COLL:
            wsrc = nc.dram_tensor("wsrc", [128, 64], F32).ap()
            wdst = nc.dram_tensor("wdst", [8 * 128, 64], F32).ap()
            b_wsrc, b_wdst, b_wcc = Buf("wsrc"), Buf("wdst"), Buf("wcc")
            P.dma(wsrc, ident_d[:, 0:64], writes=[b_wsrc], sbuf=b_identf)
            P.coll("AllGather", wsrc, wdst, [list(range(NCORES))], reads=[b_wsrc], writes=[b_wdst], sbuf=b_wcc)
        P.cp("dve", identb[:], identf[:], [b_identf], [b_identb])
        P.memset("dve", ones1[:], 1.0, [b_ones1])

        prep_bufs = []
        with ExitStack() as pst:
            def psb(shape, dt=F32, name=None):
                cnt[0] += 1
                nm = (name or "p") + "_%d" % cnt[0]
                b = Buf(nm)
                prep_bufs.append(b)
                return pst.enter_context(nc.sbuf_tensor(nm, list(shape), dt)), b

            CH = 2816
            stg = [psb([128, CH], F32, "stg") for _ in range(2)]
            stb = [psb([128, CH], BF16, "stb") for _ in range(2)]
            ci = 0
            cast_engs = ["dve", "pool", "act"]
            scr_bufs = {name: Buf("scr_" + name) for name, _, _, _ in W_SPECS}
            for name, K, N, gk in W_SPECS:
                for kt in range(K // 128):
                    for c0 in range(0, N, CH):
                        cw = min(CH, N - c0)
                        s_t, s_b = stg[ci % 2]
                        o_t, o_b = stb[ci % 2]
                        P.dma(s_t[:, 0:cw], wd[name][:, kt, c0:c0 + cw], writes=[s_b], sbuf=s_b)
                        eng = cast_engs[ci % 3]
                        if gk is None:
                            P.cp(eng, o_t[:, 0:cw], s_t[:, 0:cw], [s_b], [o_b])
                        else:
                            gt, gb = gcols[gk]
                            if eng == "act":
                                P.act(o_t[:, 0:cw], s_t[:, 0:cw], AF.Copy, [s_b, gb], [o_b], scale=gt[:, kt:kt + 1])
                            else:
                                P.ts(eng, o_t[:, 0:cw], s_t[:, 0:cw], gt[:, kt:kt + 1], ALU.mult, [s_b, gb], [o_b])
                        P.dma(ws[name][:, kt, c0:c0 + cw], o_t[:, 0:cw], reads=[o_b], writes=[scr_bufs[name]], sbuf=o_b)
                        ci += 1

        fence_t, fence_b = sb([128, 1], F32, "fence")
        P.fence(fence_t[:], fence_b, prep_bufs + list(scr_bufs.values()))
        prep_bufs = []
        with ExitStack() as pst:
            def psb(shape, dt=F32, name=None):
                cnt[0] += 1
                nm = (name or "p") + "_%d" % cnt[0]
                b = Buf(nm)
                prep_bufs.append(b)
                return pst.enter_context(nc.sbuf_tensor(nm, list(shape), dt)), b

            def small(shape, name):
                return psb(shape, F32, name)

            lr, b_lr = small([64, 32], "lr")
            li, b_li = small([64, 32], "li")
            ldt, b_ldt = small([64, 32], "ldt")
            P.dma(lr[:], lamre_d, writes=[b_lr], sbuf=b_lr)
            P.dma(li[:], lamim_d, writes=[b_li], sbuf=b_li)
            P.dma(ldt[:], logdt_d, writes=[b_ldt], sbuf=b_ldt)
            Bre, b_Bre = small([64, 32, 16], "Bre")
            Bim, b_Bim = small([64, 32, 16], "Bim")
            Cre, b_Cre = small([64, 32, 16], "Cre")
            Cim, b_Cim = small([64, 32, 16], "Cim")
            for t_, b_, d_ in [(Bre, b_Bre, bre_d), (Bim, b_Bim, bim_d), (Cre, b_Cre, cre_d), (Cim, b_Cim, cim_d)]:
                P.dma(t_[:], d_, writes=[b_], sbuf=b_)
            dt_, b_dt = small([64, 32], "dt")
            P.act(dt_[:], ldt[:], AF.Exp, [b_ldt], [b_dt])
            aa, b_aa = small([64, 32], "aa")
            th, b_th = small([64, 32], "th")
            P.tt("dve", aa[:], lr[:], dt_[:], ALU.mult, [b_lr, b_dt], [b_aa])
            P.tt("dve", th[:], li[:], dt_[:], ALU.mult, [b_li, b_dt], [b_th])
            mag, b_mag = small([64, 32], "mag")
            P.act(mag[:], aa[:], AF.Exp, [b_aa], [b_mag])
            ki, b_ki = psb([64, 32], I32, "ki")
            kf, b_kf = small([64, 32], "kf")
            P.ts("dve", ki[:], th[:], 1.0 / TWO_PI, ALU.mult, [b_th], [b_ki])
            P.cp("dve", kf[:], ki[:], [b_ki], [b_kf])
            C1 = 6.28125
            C2 = TWO_PI - C1
            thr, b_thr = small([64, 32], "thr")
            P.stt("dve", thr[:], kf[:], -C1, th[:], ALU.mult, ALU.add, [b_kf, b_th], [b_thr])
            P.stt("dve", thr[:], kf[:], -C2, thr[:], ALU.mult, ALU.add, [b_kf, b_thr], [b_thr])
            sn, b_sn = small([64, 32], "sn")
            cs, b_cs = small([64, 32], "cs")
            ab, b_ab = small([64, 32], "ab")
            P.act(sn[:], thr[:], AF.Sin, [b_thr], [b_sn])
            P.act(ab[:], thr[:], AF.Abs, [b_thr], [b_ab])
            halfpi, b_halfpi = small([64, 1], "halfpi")
            P.memset("dve", halfpi[:], math.pi / 2.0, [b_halfpi])
            P.act(cs[:], ab[:], AF.Sin, [b_ab, b_halfpi], [b_cs], bias=halfpi[:], scale=-1.0)
            LP, b_LP = small([64, 2, 9, 32], "LP")
            P.memset("dve", LP[:, 0, 0, :], 1.0, [b_LP])
            P.memset("dve", LP[:, 1, 0, :], 0.0, [b_LP])
            P.tt("dve", LP[:, 0, 1, :], mag[:], cs[:], ALU.mult, [b_mag, b_cs], [b_LP])
            P.tt("dve", LP[:, 1, 1, :], mag[:], sn[:], ALU.mult, [b_mag, b_sn], [b_LP])
            tA, b_tA = small([64, 32], "tA")
            tB, b_tB = small([64, 32], "tB")

            def cmul(o_re, o_im, a_re, a_im, c_re, c_im, rd, wr, shape_t=None):
                t1, bt1 = shape_t[0]
                t2, bt2 = shape_t[1]
                P.tt("dve", t1, a_re, c_re, ALU.mult, rd, [bt1])
                P.tt("dve", t2, a_im, c_im, ALU.mult, rd, [bt2])
                P.tt("dve", o_re, t1, t2, ALU.subtract, [bt1, bt2], wr)
                P.tt("dve", t1, a_re, c_im, ALU.mult, rd, [bt1])
                P.tt("dve", t2, a_im, c_re, ALU.mult, rd, [bt2])
                P.tt("dve", o_im, t1, t2, ALU.add, [bt1, bt2], wr)

            for tau in range(2, 9):
                cmul(LP[:, 0, tau, :], LP[:, 1, tau, :], LP[:, 0, tau - 1, :], LP[:, 1, tau - 1, :],
                     LP[:, 0, 1, :], LP[:, 1, 1, :], [b_LP], [b_LP], [(tA[:], b_tA), (tB[:], b_tB)])
            P.cp("dve", A1[:, 0, :], LP[:, 0, 8, :], [b_LP], [b_A1])
            P.cp("dve", A1[:, 1, :], LP[:, 0, 8, :], [b_LP], [b_A1])
            P.ts("dve", A2[:, 0, :], LP[:, 1, 8, :], -1.0, ALU.mult, [b_LP], [b_A2])
            P.cp("dve", A2[:, 1, :], LP[:, 1, 8, :], [b_LP], [b_A2])
            nsq = int(round(math.log2(max(ntiles, 1) * 64)))
            assert 2 ** nsq == max(ntiles, 1) * 64
            LB, b_LB = small([64, 2, 32], "LB")
            P.cp("dve", LB[:, 0, :], LP[:, 0, 8, :], [b_LP], [b_LB])
            P.cp("dve", LB[:, 1, :], LP[:, 1, 8, :], [b_LP], [b_LB])
            for _ in range(nsq):
                P.tt("dve", tA[:], LB[:, 0, :], LB[:, 0, :], ALU.mult, [b_LB], [b_tA])
                P.tt("dve", tB[:], LB[:, 1, :], LB[:, 1, :], ALU.mult, [b_LB], [b_tB])
                P.stt("dve", LB[:, 1, :], LB[:, 0, :], 2.0, LB[:, 1, :], ALU.mult, ALU.mult, [b_LB], [b_LB])
                P.tt("dve", LB[:, 0, :], tA[:], tB[:], ALU.subtract, [b_tA, b_tB], [b_LB])
            P.cp("dve", AB1[:, 0, :], LB[:, 0, :], [b_LB], [b_AB1])
            P.cp("dve", AB1[:, 1, :], LB[:, 0, :], [b_LB], [b_AB1])
            P.ts("dve", AB2[:, 0, :], LB[:, 1, :], -1.0, ALU.mult, [b_LB], [b_AB2])
            P.cp("dve", AB2[:, 1, :], LB[:, 1, :], [b_LB], [b_AB2])
            inv, b_inv = small([64, 2, 32], "inv")
            den, b_den = small([64, 32], "den")
            P.tt("dve", tA[:], LP[:, 0, 8, :], LP[:, 0, 8, :], ALU.mult, [b_LP], [b_tA])
            P.tt("dve", tB[:], LP[:, 1, 8, :], LP[:, 1, 8, :], ALU.mult, [b_LP], [b_tB])
            P.tt("dve", den[:], tA[:], tB[:], ALU.add, [b_tA, b_tB], [b_den])
            P.recip(den[:], den[:], [b_den], [b_den])
            P.tt("dve", inv[:, 0, :], LP[:, 0, 8, :], den[:], ALU.mult, [b_LP, b_den], [b_inv])
            P.stt("dve", inv[:, 1, :], LP[:, 1, 8, :], -1.0, den[:], ALU.mult, ALU.mult, [b_LP, b_den], [b_inv])
            fre, b_fre = small([64, 32], "fre")
            fim, b_fim = small([64, 32], "fim")
            nr, b_nr = small([64, 32], "nr")
            P.ts("dve", nr[:], LP[:, 0, 1, :], -1.0, ALU.add, [b_LP], [b_nr])
            P.tt("dve", tA[:], lr[:], lr[:], ALU.mult, [b_lr], [b_tA])
            P.tt("dve", tB[:], li[:], li[:], ALU.mult, [b_li], [b_tB])
            P.tt("dve", den[:], tA[:], tB[:], ALU.add, [b_tA, b_tB], [b_den])
            P.recip(den[:], den[:], [b_den], [b_den])
            P.tt("dve", tA[:], nr[:], lr[:], ALU.mult, [b_nr, b_lr], [b_tA])
            P.tt("dve", tB[:], LP[:, 1, 1, :], li[:], ALU.mult, [b_LP, b_li], [b_tB])
            P.tt("dve", fre[:], tA[:], tB[:], ALU.add, [b_tA, b_tB], [b_fre])
            P.tt("dve", fre[:], fre[:], den[:], ALU.mult, [b_fre, b_den], [b_fre])
            P.tt("dve", tA[:], LP[:, 1, 1, :], lr[:], ALU.mult, [b_LP, b_lr], [b_tA])
            P.tt("dve", tB[:], nr[:], li[:], ALU.mult, [b_nr, b_li], [b_tB])
            P.tt("dve", fim[:], tA[:], tB[:], ALU.subtract, [b_tA, b_tB], [b_fim])
            P.tt("dve", fim[:], fim[:], den[:], ALU.mult, [b_fim, b_den], [b_fim])

            def bc(ap2d):
                return ap2d.unsqueeze(2).to_broadcast([64, 32, 16])

            u1, b_u1 = small([64, 32, 16], "u1")
            u2, b_u2 = small([64, 32, 16], "u2")
            utmp = [(u1[:], b_u1), (u2[:], b_u2)]
            bbre, b_bbre = small([64, 32, 16], "bbre")
            bbim, b_bbim = small([64, 32, 16], "bbim")
            cmul(bbre[:], bbim[:], Bre[:], Bim[:], bc(fre[:]), bc(fim[:]), [b_Bre, b_Bim, b_fre, b_fim],
                 [b_bbre, b_bbim], utmp)
            BL, b_BL = small([64, 2, 32, 8, 16], "BL")
            for s in range(8):
                tau = 7 - s
                cmul(BL[:, 0, :, s, :], BL[:, 1, :, s, :], bbre[:], bbim[:], bc(LP[:, 0, tau, :]), bc(LP[:, 1, tau, :]),
                     [b_bbre, b_bbim, b_LP], [b_BL], utmp)
            CL, b_CL = small([64, 2, 32, 8, 16], "CL")
            for t in range(8):
                cmul(CL[:, 0, :, t, :], CL[:, 1, :, t, :], Cre[:], Cim[:], bc(LP[:, 0, t + 1, :]), bc(LP[:, 1, t + 1, :]),
                     [b_Cre, b_Cim, b_LP], [b_CL], utmp)
            P.cp("dve", Vm[:, :, 0, :].rearrange("p g (t j) -> p g t j", t=8), CL[:, 0, :, :, :], [b_CL], [b_Vm])
            P.ts("dve", Vm[:, :, 1, :].rearrange("p g (t j) -> p g t j", t=8), CL[:, 1, :, :, :], -1.0, ALU.mult,
                 [b_CL], [b_Vm])
            CLp, b_CLp = small([64, 2, 32, 8, 16], "CLp")
            v1, b_v1 = small([64, 32, 8, 16], "v1")
            v2, b_v2 = small([64, 32, 8, 16], "v2")

            def bc4(ap2d):
                return ap2d.unsqueeze(2).unsqueeze(3).to_broadcast([64, 32, 8, 16])

            cmul(CLp[:, 0], CLp[:, 1], CL[:, 0], CL[:, 1], bc4(inv[:, 0, :]), bc4(inv[:, 1, :]), [b_CL, b_inv],
                 [b_CLp], [(v1[:], b_v1), (v2[:], b_v2)])
            P.ts("dve", CLp[:, 1], CLp[:, 1], -1.0, ALU.mult, [b_CLp], [b_CLp])
            tmpT, b_tmpT = small([128, 128], "tmpT")
            for g in range(32):
                pt, pb = nb()
                P.mm(pt[:, 0:128], BL[:, 0, g].rearrange("p s h -> p (s h)"), CLp[:, 0, g].rearrange("p t j -> p (t j)"),
                     True, False, [b_BL, b_CLp], [pb])
                P.mm(pt[:, 0:128], BL[:, 1, g].rearrange("p s h -> p (s h)"), CLp[:, 1, g].rearrange("p t j -> p (t j)"),
                     False, True, [b_BL, b_CLp], [pb])
                P.tt("dve", tmpT[:], pt[:, 0:128], s5mask[:], ALU.mult, [pb, b_s5mask], [b_tmpT])
                P.stt("dve", T0[:, g, :], identf[:], dcol[:, g:g + 1], tmpT[:], ALU.mult, ALU.add,
                      [b_identf, b_dcol, b_tmpT], [b_T0])
                for slot in range(2):
                    pt2, pb2 = nb()
                    P.tr(pt2[:, 0:64], BL[:, slot, g].rearrange("p s h -> p (s h)"), identf[0:64, 0:64], [b_BL, b_identf], [pb2])
                    P.cp("act", Wm[:, g, slot, :], pt2[:, 0:64], [pb2], [b_Wm])
        P.fence(fence_t[:], fence_b, prep_bufs)
        main_bufs = []

        def msb(shape, dt=F32, name=None):
            t, b = sb(shape, dt, name)
            main_bufs.append(b)
            return t, b

        TM = 512
        xt, b_xt = msb([128, 4, D], F32, "xt")
        xn, b_xn = msb([128, D], BF16, "xn")
        xnT, b_xnT = msb([128, 8, TM], BF16, "xnT")
        actb, b_actb = msb([128, FT, TM], BF16, "actb")
        wblk = [msb([128, 4096], BF16, "wblk") for _ in range(2)]
        DN = 256
        wdn = [msb([128, FT, DN], BF16, "wdn") for _ in range(1)]
        sgt, b_sgt = msb([128, TM], F32, "sgt")
        junk, b_junk = msb([128, D], BF16, "junk")
        ssq, b_ssq = msb([128, 8], F32, "ssq")
        rstd, b_rstd = msb([128, 8], F32, "rstd")
        wga, b_wga = msb([128, 8, 16], BF16, "wga")
        gaT, b_gaT = msb([16, TM], F32, "gaT")
        Lb, b_Lb = msb([128, 1, 256], F32, "Lb")
        E1, b_E1 = msb([128, 2, TM], F32, "E1")
        E2, b_E2 = msb([128, 2, TM], F32, "E2")
        E3, b_E3 = msb([128, 1, 256], F32, "E3")
        qt, b_qt = msb([128, 2, TM], BF16, "qt")
        qa, b_qa = msb([128, 2, TM], BF16, "qa")
        qb, b_qb = msb([128, 2, TM], BF16, "qb")
        ktl, b_ktl = msb([128, 2, TM], BF16, "ktl")
        khat, b_khat = msb([128, 4, 256], BF16, "khat")
        vb, b_vb = msb([128, 4, 512], BF16, "vb")
        srb, b_srb = msb([128, 4, 512], BF16, "srb")
        gates, b_gates = msb([128, 4, 2048], BF16, "gates")
        ATb, b_ATb = msb([128, 4, 128], BF16, "ATb")
        onb, b_onb = msb([128, 512], BF16, "onb")
        mf, b_mf = msb([128, 512], F32, "mf")
        mbf, b_mbf = xn, b_xn
        g5T, b_g5T = msb([128, 4, TM], BF16, "g5T")
        onT, b_onT = g5T, b_g5T
        glu, b_glu = msb([128, 4, TM], BF16, "glu")
        Sf = [msb([128, 2, 128], F32, "Sf") for _ in range(2)]
        Sb = [msb([128, 2, 128], BF16, "Sb") for _ in range(2)]
        So = [msb([128, 2, 128], F32, "So") for _ in range(2)]
        dS, b_dS = msb([128, 2, 2], F32, "dS")
        actraw = actb[:].rearrange("p f t -> p (f t)")
        Uc = actraw[:, 0:4096].rearrange("p (g s h) -> p g s h", g=32, s=8)
        Gc = actraw[:, 0:4096].rearrange("p (t c) -> p t c", t=8)
        Ug = actraw[:, 4096:6144].rearrange("p (g c) -> p g c", g=32)
        Xbf = actraw[:, 6144:10240]
        Bst, b_Bst = msb([64, 2 * 32 * 65], F32, "Bst")
        st1, b_st1 = msb([64, 2, 32], F32, "st1")
        st2, b_st2 = msb([64, 2, 32], F32, "st2")
        xo, b_xo = msb([64, 2, 32], F32, "xo")
        xin, b_xin = msb([64, 2, 32, 2], F32, "xin")
        xo2 = [msb([64, 2, 32], F32, "xo2") for _ in range(2)]

        P.memset("pool", qa[:], 0.0, [b_qa])
        P.memset("pool", qb[:], 0.0, [b_qb])

        def norm_T(T, src_t, src_b):
            NT = T // 128
            for ts_ in range(NT):
                P.act(junk[:], src_t[:, ts_, :], AF.Square, [src_b], [b_junk, b_ssq], accum=ssq[:, ts_:ts_ + 1])
                P.act(rstd[:, ts_:ts_ + 1], ssq[:, ts_:ts_ + 1], AF.Sqrt, [b_ssq], [b_rstd], bias=EPS, scale=1.0 / D)
                P.recip(rstd[:, ts_:ts_ + 1], rstd[:, ts_:ts_ + 1], [b_rstd], [b_rstd])
                P.ts("dve", xn[:], src_t[:, ts_, :], rstd[:, ts_:ts_ + 1], ALU.mult, [src_b, b_rstd], [b_xn])
                bt, bb = nbb()
                for kt in range(8):
                    P.tr(bt[:, kt * 128:(kt + 1) * 128], xn[:, kt * 128:(kt + 1) * 128], identb[:], [b_xn, b_identb], [bb])
                P.cp("dve" if ts_ % 2 else "act", xnT[:, :, ts_ * 128:(ts_ + 1) * 128],
                     bt[:].rearrange("p (k t) -> p k t", k=8), [bb], [b_xnT])

        wi = [0]

        def load_blk(scr_name, view_fn, shape_fn):
            t, b = wblk[wi[0] % 2]
            wi[0] += 1
            v = shape_fn(t)
            P.dma(v, view_fn(ws[scr_name]), reads=[scr_bufs[scr_name]], writes=[b], sbuf=b)
            return v, b

        def ffn(T, h_t, h_b, wg, wu, wdn_name):
            NT = T // 128
            norm_T(T, h_t, h_b)
            gi = 0
            for c0 in range(0, FF, 512):
                cw = min(512, FF - c0)
                gv, gb = load_blk(wg, lambda a: a[:, :, c0:c0 + cw], lambda t: t[:, 0:8 * cw].rearrange("p (k n) -> p k n", k=8))
                uv, ub = load_blk(wu, lambda a: a[:, :, c0:c0 + cw], lambda t: t[:, 0:8 * cw].rearrange("p (k n) -> p k n", k=8))
                for f0 in range(0, cw, 128):
                    ft = (c0 + f0) // 128
                    pg, pgb = banks[(gi % 2) * 2]
                    pu, pub = banks[(gi % 2) * 2 + 1]
                    gi += 1
                    for kt in range(8):
                        P.mm(pg[:, 0:T], gv[:, kt, f0:f0 + 128], xnT[:, kt, 0:T], kt == 0, kt == 7, [gb, b_xnT], [pgb])
                    for kt in range(8):
                        P.mm(pu[:, 0:T], uv[:, kt, f0:f0 + 128], xnT[:, kt, 0:T], kt == 0, kt == 7, [ub, b_xnT], [pub])
                    P.act(sgt[:, 0:T], pg[:, 0:T], AF.Silu, [pgb], [b_sgt])
                    P.tt("dve", actb[:, ft, 0:T], sgt[:, 0:T], pu[:, 0:T], ALU.mult, [b_sgt, pub], [b_actb])
            for dq in range(D // DN):
                wt_, wb_ = wdn[0]
                P.dma(wt_[:], ws[wdn_name][:, :, dq * DN:(dq + 1) * DN], reads=[scr_bufs[wdn_name]], writes=[wb_], sbuf=wb_)
                for ts_ in range(NT):
                    po, pob = banks[4 + (ts_ % 2)]
                    for ft in range(FT):
                        P.mm(po[:, 0:DN], actb[:, ft, ts_ * 128:(ts_ + 1) * 128], wt_[:, ft, :], ft == 0, ft == FT - 1,
                             [b_actb, wb_], [pob])
                    P.stt("dve", h_t[:, ts_, dq * DN:(dq + 1) * DN], po[:, 0:DN], 0.5, h_t[:, ts_, dq * DN:(dq + 1) * DN],
                          ALU.mult, ALU.add, [pob, h_b], [h_b])

        def tok_proj(T, scr_name, c0, cw, consume):
            NT = T // 128
            wv, wb_ = load_blk(scr_name, lambda a: a[:, :, c0:c0 + cw], lambda t: t[:, 0:8 * cw].rearrange("p (k n) -> p k n", k=8))
            for ts_ in range(NT):
                pt, pb = nb()
                for kt in range(8):
                    P.mm(pt[:, 0:cw], xnT[:, kt, ts_ * 128:(ts_ + 1) * 128], wv[:, kt, :], kt == 0, kt == 7, [b_xnT, wb_], [pb])
                consume(ts_, pt[:, 0:cw], pb)

        def mixer(T, h_t, h_b, Q, s5_init, gla_mode, gla_out, s5_out, so=False):
            NT = T // 128
            NC = T // 8
            NCQ = NC // Q
            norm_T(T, h_t, h_b)
            P.dma(wga[:], ws["w_in"][:, :, 1536:1552], reads=[scr_bufs["w_in"]], writes=[b_wga], sbuf=b_wga)
            qkv, qkb = load_blk("w_in", lambda a: a[:, :, 0:512], lambda t: t[:, 0:4096].rearrange("p (k n) -> p k n", k=8))
            pt, pb = nb()
            for kt in range(8):
                P.mm(pt[0:16, 0:T], wga[:, kt, :], xnT[:, kt, 0:T], kt == 0, kt == 7, [b_wga, b_xnT], [pb])
            P.cp("dve", gaT[:, 0:T], pt[0:16, 0:T], [pb], [b_gaT])
            for ts_ in range(NT):
                pt, pb = nb()
                P.mm(pt[:, 0:256], gaT[:, ts_ * 128:(ts_ + 1) * 128], wgu[:], True, False, [b_gaT, b_wgu], [pb])
                P.mm(pt[:, 0:256], ones1[:], bgate[:], False, True, [b_ones1, b_bgate], [pb])
                P.act(mf[:, 0:256], pt[:, 0:256], AF.Exp, [pb], [b_mf], scale=-1.0)
                P.act(Lb[:, 0, :], mf[:, 0:256], AF.Ln, [b_mf], [b_Lb], bias=1.0)
                for pair in range(2):
                    pt2, pb2 = nb()
                    P.mm(pt2[:, 0:128], Lb[:, 0, pair * 128:(pair + 1) * 128], triinc[:], True, True, [b_Lb, b_triinc], [pb2])
                    P.act(E1[:, pair, ts_ * 128:(ts_ + 1) * 128], pt2[:, 0:128], AF.Exp, [pb2], [b_E1])
                    if not so:
                        P.act(E2[:, pair, ts_ * 128:(ts_ + 1) * 128], pt2[:, 0:128], AF.Exp, [pb2], [b_E2], scale=-1.0)
                pt3, pb3 = nb()
                P.mm(pt3[:, 0:256], trirev[:], Lb[:, 0, :], True, True, [b_trirev, b_Lb], [pb3])
                P.act(E3[:, 0, :], pt3[:, 0:256], AF.Exp, [pb3], [b_E3])
                pt4, pb4 = nb()
                for kt in range(8):
                    P.mm(pt4[:, 0:256], xnT[:, kt, ts_ * 128:(ts_ + 1) * 128], qkv[:, kt, 256:512], kt == 0, kt == 7, [b_xnT, qkb], [pb4])
                P.tt("dve", khat[:, ts_, :], pt4[:, 0:256], E3[:, 0, :], ALU.mult, [pb4, b_E3], [b_khat])
            for which in ([] if so else range(2)):
                for pair in range(2):
                    c0 = which * 256 + pair * 128
                    pt, pb = nb()
                    for kt in range(8):
                        P.mm(pt[:, 0:T], qkv[:, kt, c0:c0 + 128], xnT[:, kt, 0:T], kt == 0, kt == 7, [qkb, b_xnT], [pb])
                    if which == 0:
                        P.stt("dve", qt[:, pair, 0:T], E1[:, pair, 0:T], 0.125, pt[:, 0:T], ALU.mult, ALU.mult, [b_E1, pb], [b_qt])
                        qv = qt[:, pair, 0:T].rearrange("p (n c t) -> p n c t", c=2, t=64)
                        P.cp("pool", qa[:, pair, 0:T].rearrange("p (n c t) -> p n c t", c=2, t=64)[:, :, 0, :], qv[:, :, 0, :], [b_qt], [b_qa])
                        P.cp("pool", qb[:, pair, 0:T].rearrange("p (n c t) -> p n c t", c=2, t=64)[:, :, 1, :], qv[:, :, 1, :], [b_qt], [b_qb])
                    else:
                        P.tt("dve", ktl[:, pair, 0:T], E2[:, pair, 0:T], pt[:, 0:T], ALU.mult, [b_E2, pb], [b_ktl])
            tok_proj(T, "w_in", 512, 512, lambda ts_, p, pb: P.cp("act", vb[:, ts_, :], p, [pb], [b_vb]))
            if not so:
                tok_proj(T, "w_in", 1024, 512, lambda ts_, p, pb: P.act(srb[:, ts_, :], p, AF.Silu, [pb], [b_srb]))
            for i in ([] if so else range(4)):
                tok_proj(T, "w_in", 2064 + i * 512, 512,
                         lambda ts_, p, pb, i=i: P.act(gates[:, ts_, i * 512:(i + 1) * 512], p, AF.Sigmoid, [pb], [b_gates]))
            uv_, ub_ = load_blk("w_in", lambda a: a[:, :, 1552:2064], lambda t: t[:, 0:4096].rearrange("p (k n) -> p k n", k=8))
            for s_lo in range(8):
                pt, pb = nb()
                for kt in range(8):
                    P.mm(pt[0:NC, 0:512], xnT[:, kt, s_lo:T:8], uv_[:, kt, :], kt == 0, kt == 7, [b_xnT, ub_], [pb])
                P.cp("act" if s_lo % 2 else "dve", Uc[0:NC, :, s_lo, :], pt[0:NC, 0:512].rearrange("c (g h) -> c g h", g=32), [pb], [b_actb])
            for half in range(2):
                bt, bb = nbb()
                for gg in range(16):
                    g = half * 16 + gg
                    P.tr(bt[:, gg * NC:(gg + 1) * NC], Uc[0:NC, g].rearrange("c s h -> c (s h)"), identb[0:NC, 0:NC], [b_actb, b_identb], [bb])
                P.cp("dve", Ug[:, half * 16:(half + 1) * 16, 0:NC], bt[:, 0:16 * NC].rearrange("p (g c) -> p g c", g=16), [bb], [b_actb])
            Bv = Bst[:, 0:2 * 32 * Q * (NCQ + 1)].rearrange("p (s g q c) -> p s g q c", s=2, g=32, q=Q)
            for q in range(Q):
                P.cp("pool", Bv[:, :, :, q, 0], s5_init(q), [b_xo] if s5_init_reads is None else s5_init_reads, [b_Bst])
            for g0 in range(0, 32, 4):
                pt, pb = nb()
                pv = pt[0:64, 0:2 * 4 * NC].rearrange("p (s g c) -> p s g c", s=2, g=4)
                for gg in range(4):
                    for slot in range(2):
                        P.mm(pv[:, slot, gg, :], Wm[:, g0 + gg, slot, :], Ug[:, g0 + gg, 0:NC], True, True, [b_Wm, b_actb], [pb])
                for q in range(Q):
                    P.cp("dve" if (g0 // 4) % 2 else "act", Bv[:, :, g0:g0 + 4, q, 1:NCQ + 1], pv[:, :, :, q * NCQ:(q + 1) * NCQ], [pb], [b_Bst])
            for c in range(NCQ):
                for q in range(Q):
                    P.tt("pool", st1[:], A1[:], Bv[:, :, :, q, c], ALU.mult, [b_A1, b_Bst], [b_st1])
                    P.tt("pool", st2[:, 0, :], A2[:, 0, :], Bv[:, 1, :, q, c], ALU.mult, [b_A2, b_Bst], [b_st2])
                    P.tt("pool", st2[:, 1, :], A2[:, 1, :], Bv[:, 0, :, q, c], ALU.mult, [b_A2, b_Bst], [b_st2])
                    P.tt("pool", st1[:], st1[:], st2[:], ALU.add, [b_st1, b_st2], [b_st1])
                    P.tt("pool", Bv[:, :, :, q, c + 1], Bv[:, :, :, q, c + 1], st1[:], ALU.add, [b_Bst, b_st1], [b_Bst])
            for q in range(Q):
                s5_out(q, Bv[:, :, :, q, NCQ])
            Xv = Xbf[0:64, 0:2 * 32 * NC].rearrange("p (s g c) -> p s g c", s=2, g=32)
            for q in ([] if so else range(Q)):
                P.cp("dve", Xv[:, :, :, q * NCQ:(q + 1) * NCQ], Bv[:, :, :, q, 0:NCQ], [b_Bst], [b_actb])
            for g0 in ([] if so else range(0, 32, 4)):
                pt, pb = nb()
                for gg in range(4):
                    g = g0 + gg
                    o_ = pt[0:NC, gg * 128:(gg + 1) * 128]
                    P.mm(o_, Ug[:, g, 0:NC], T0[:, g, :], True, False, [b_actb, b_T0], [pb])
                    P.mm(o_, Xv[:, 0, g, :], Vm[:, g, 0, :], False, False, [b_actb, b_Vm], [pb])
                    P.mm(o_, Xv[:, 1, g, :], Vm[:, g, 1, :], False, True, [b_actb, b_Vm], [pb])
                P.act(Gc[0:NC, :, g0 * 16:(g0 + 4) * 16].rearrange("c t (g j) -> c t g j", g=4),
                      pt[0:NC, 0:512].rearrange("c (g t j) -> c t g j", g=4, t=8), AF.Gelu, [pb], [b_actb])
            for th_ in ([] if so else range(2)):
                bt, bb = nbb()
                for tl in range(4):
                    for ct in range(4):
                        i = tl * 4 + ct
                        P.tr(bt[:, i * NC:(i + 1) * NC], Gc[0:NC, th_ * 4 + tl, ct * 128:(ct + 1) * 128], identb[0:NC, 0:NC],
                             [b_actb, b_identb], [bb])
                P.cp("dve", g5T[:, :, 0:T].rearrange("p k (c t) -> p t k c", t=8)[:, th_ * 4:(th_ + 1) * 4],
                     bt[:, 0:16 * NC].rearrange("p (t k c) -> p t k c", t=4, k=4), [bb], [b_g5T])
            if so:
                for ts_ in range(NT):
                    for pair in range(2):
                        P.cp("pool", dS[:, pair, :], E1[:, pair, ts_ * 128 + 63:ts_ * 128 + 128:64], [b_E1], [b_dS])
                    P.tt("pool", Dtot[:], Dtot[:], dS[:, :, 0], ALU.mult, [b_Dtot, b_dS], [b_Dtot])
                    P.tt("pool", Dtot[:], Dtot[:], dS[:, :, 1], ALU.mult, [b_Dtot, b_dS], [b_Dtot])
                    for c2 in range(2):
                        ps_ = slice(c2 * 64, c2 * 64 + 64)
                        src, dst = (Sf[0], Sf[1]) if c2 == 0 else (Sf[1], Sf[0])
                        for pair in range(2):
                            pt, pb = nb()
                            P.mm(pt[:, 0:256], khat[ps_, ts_, pair * 128:(pair + 1) * 128], vb[ps_, ts_, pair * 256:(pair + 1) * 256],
                                 True, True, [b_khat, b_vb], [pb])
                            for hp in range(2):
                                rs = slice(hp * 64, hp * 64 + 64)
                                P.stt("dve", dst[0][rs, pair, :], src[0][rs, pair, :], dS[rs, pair, c2:c2 + 1],
                                      pt[rs, hp * 128:(hp + 1) * 128], ALU.mult, ALU.add, [src[1], b_dS, pb], [dst[1]])
                return
            wa_v, wa_b = load_blk("w_glu_a", lambda a: a, lambda t: t[:, 0:2048].rearrange("p (k n) -> p k n", k=4))
            wb_v, wb_b = load_blk("w_glu_b", lambda a: a, lambda t: t[:, 0:2048].rearrange("p (k n) -> p k n", k=4))
            for nt_ in range(4):
                pa, pab = nb()
                for ct in range(4):
                    P.mm(pa[:, 0:T], wa_v[:, ct, nt_ * 128:(nt_ + 1) * 128], g5T[:, ct, 0:T], ct == 0, ct == 3, [wa_b, b_g5T], [pab])
                pb_, pbb = nb()
                for ct in range(4):
                    P.mm(pb_[:, 0:T], wb_v[:, ct, nt_ * 128:(nt_ + 1) * 128], g5T[:, ct, 0:T], ct == 0, ct == 3, [wb_b, b_g5T], [pbb])
                P.act(sgt[:, 0:T], pb_[:, 0:T], AF.Sigmoid, [pbb], [b_sgt])
                P.tt("dve", glu[:, nt_, 0:T], sgt[:, 0:T], pa[:, 0:T], ALU.mult, [b_sgt, pab], [b_glu])
            for ts_ in range(NT):
                tsl = slice(ts_ * 128, (ts_ + 1) * 128)
                if gla_mode == "chain":
                    s0f, s0b = Sf[0], Sb[0]
                    s1f, s1b = Sf[1], Sb[1]
                else:
                    s0f, s0b = Sf[0], Sb[0]
                    s1f, s1b = Sf[1], Sb[1]
                for pair in range(2):
                    P.cp("pool", dS[:, pair, :], E1[:, pair, ts_ * 128 + 63:ts_ * 128 + 128:64], [b_E1], [b_dS])
                for h in range(4):
                    pair, hp = h // 2, h % 2
                    rs = slice(hp * 64, hp * 64 + 64)
                    pt, pb = nb()
                    P.mm(pt[:, 0:128], ktl[rs, pair, tsl], qt[rs, pair, tsl], True, True, [b_ktl, b_qt], [pb])
                    P.tt("dve", ATb[:, h, :], pt[:, 0:128], cmask[:], ALU.mult, [pb, b_cmask], [b_ATb])
                kv = []
                for c2 in range(2):
                    ps_ = slice(c2 * 64, c2 * 64 + 64)
                    row = []
                    for pair in range(2):
                        pt, pb = nb()
                        P.mm(pt[:, 0:256], khat[ps_, ts_, pair * 128:(pair + 1) * 128], vb[ps_, ts_, pair * 256:(pair + 1) * 256],
                             True, True, [b_khat, b_vb], [pb])
                        row.append((pt, pb))
                    kv.append(row)

                def upd(dst_t, dst_b, src_t, src_b, c2, bf_t=None, bf_b=None):
                    for pair in range(2):
                        pt, pb = kv[c2][pair]
                        for hp in range(2):
                            rs = slice(hp * 64, hp * 64 + 64)
                            P.stt("dve", dst_t[rs, pair, :], src_t[rs, pair, :], dS[rs, pair, c2:c2 + 1],
                                  pt[rs, hp * 128:(hp + 1) * 128], ALU.mult, ALU.add, [src_b, b_dS, pb], [dst_b])
                    if bf_t is not None:
                        P.cp("pool", bf_t[:], dst_t[:], [dst_b], [bf_b])

                if gla_mode == "chain":
                    upd(s1f[0], s1f[1], s0f[0], s0f[1], 0, s1b[0], s1b[1])
                else:
                    upd(So[0][0], So[0][1], s0f[0], s0f[1], 0)
                    upd(So[1][0], So[1][1], s1f[0], s1f[1], 1)
                po, pob = nb()
                for h in range(4):
                    pair, hp = h // 2, h % 2
                    rs = slice(hp * 64, hp * 64 + 64)
                    o_ = po[:, h * 128:(h + 1) * 128]
                    P.mm(o_, ATb[:, h, :], vb[:, ts_, h * 128:(h + 1) * 128], True, False, [b_ATb, b_vb], [pob])
                    P.mm(o_, qa[rs, pair, tsl], s0b[0][rs, pair, :], False, False, [b_qa, s0b[1]], [pob])
                    P.mm(o_, qb[rs, pair, tsl], s1b[0][rs, pair, :], False, True, [b_qb, s1b[1]], [pob])
                if gla_mode == "chain":
                    upd(s0f[0], s0f[1], s1f[0], s1f[1], 1, s0b[0], s0b[1])
                for h in range(4):
                    P.act(junk[:, 0:128], po[:, h * 128:(h + 1) * 128], AF.Square, [pob], [b_junk, b_ssq], accum=ssq[:, 4 + h:5 + h])
                P.act(rstd[:, 4:8], ssq[:, 4:8], AF.Sqrt, [b_ssq], [b_rstd], bias=EPS, scale=1.0 / 128)
                P.recip(rstd[:, 4:8], rstd[:, 4:8], [b_rstd], [b_rstd])
                for h in range(4):
                    P.stt("dve", onb[:, h * 128:(h + 1) * 128], po[:, h * 128:(h + 1) * 128], rstd[:, 4 + h:5 + h],
                          srb[:, ts_, h * 128:(h + 1) * 128], ALU.mult, ALU.mult, [pob, b_rstd, b_srb], [b_onb])
                bt, bb = nbb()
                for ct in range(4):
                    P.tr(bt[:, ct * 128:(ct + 1) * 128], onb[:, ct * 128:(ct + 1) * 128], identb[:], [b_onb, b_identb], [bb])
                P.cp("act", onT[:, :, tsl], bt[:, 0:512].rearrange("p (k t) -> p k t", k=4), [bb], [b_onT])
            if gla_mode == "chain":
                pass
            else:
                for q in range(2):
                    gla_out(q, So[q])
            wgo_v, wgo_b = load_blk("w_gla_out", lambda a: a, lambda t: t[:, 0:4096].rearrange("p (k n) -> p k n", k=4))
            wso_v, wso_b = load_blk("w_s5_out", lambda a: a, lambda t: t[:, 0:4096].rearrange("p (k n) -> p k n", k=4))
            for ts_ in range(NT):
                tsl = slice(ts_ * 128, (ts_ + 1) * 128)
                for half in range(2):
                    hs = slice(half * 512, (half + 1) * 512)
                    pg_, pgb_ = nb()
                    for ct in range(4):
                        P.mm(pg_[:, :], onT[:, ct, tsl], wgo_v[:, ct, hs], ct == 0, ct == 3, [b_onT, wgo_b], [pgb_])
                    ps2, psb2 = nb()
                    for ct in range(4):
                        P.mm(ps2[:, :], glu[:, ct, tsl], wso_v[:, ct, hs], ct == 0, ct == 3, [b_glu, wso_b], [psb2])
                    P.tt("dve", mf[:], gates[:, ts_, hs], pg_[:, :], ALU.mult, [b_gates, pgb_], [b_mf])
                    P.tt("dve", sgt[:, 0:512], gates[:, ts_, 1024 + half * 512:1024 + (half + 1) * 512], ps2[:, :], ALU.mult,
                         [b_gates, psb2], [b_sgt])
                    P.tt("pool", mbf[:, hs], mf[:], sgt[:, 0:512], ALU.add, [b_mf, b_sgt], [b_mbf])
                bt, bb = nbb()
                for kt in range(8):
                    P.tr(bt[:, kt * 128:(kt + 1) * 128], mbf[:, kt * 128:(kt + 1) * 128], identb[:], [b_mbf, b_identb], [bb])
                P.cp("act", xnT[:, :, tsl], bt[:].rearrange("p (k t) -> p k t", k=8), [bb], [b_xnT])
            for half in range(2):
                hs = slice(half * 512, (half + 1) * 512)
                wo_v, wo_b = load_blk("w_out", lambda a: a[:, :, half * 512:(half + 1) * 512],
                                      lambda t: t[:, 0:4096].rearrange("p (k n) -> p k n", k=8))
                for ts_ in range(NT):
                    pt, pb = nb()
                    for kt in range(8):
                        P.mm(pt[:, :], xnT[:, kt, ts_ * 128:(ts_ + 1) * 128], wo_v[:, kt, :], kt == 0, kt == 7, [b_xnT, wo_b], [pb])
                    P.tt("dve", h_t[:, ts_, hs], h_t[:, ts_, hs], pt[:, :], ALU.add, [h_b, pb], [h_b])

        def final_norm(T, h_t, h_b):
            NT = T // 128
            for ts_ in range(NT):
                P.act(junk[:], h_t[:, ts_, :], AF.Square, [h_b], [b_junk, b_ssq], accum=ssq[:, ts_:ts_ + 1])
                P.act(rstd[:, ts_:ts_ + 1], ssq[:, ts_:ts_ + 1], AF.Sqrt, [b_ssq], [b_rstd], bias=EPS, scale=1.0 / D)
                P.recip(rstd[:, ts_:ts_ + 1], rstd[:, ts_:ts_ + 1], [b_rstd], [b_rstd])
                P.stt("dve", h_t[:, ts_, :], h_t[:, ts_, :], rstd[:, ts_:ts_ + 1], gfin[:], ALU.mult, ALU.mult,
                      [h_b, b_rstd, b_gfin], [h_b])

        s5_init_reads = None
        def s5o(q, view):
            P.cp("pool", xo[:], view, [b_Bst], [b_xo])

        if ntiles > 0:
            P.memset("dve", Sf[0][0][:], 0.0, [Sf[0][1]])
            P.memset("dve", Sb[0][0][:], 0.0, [Sb[0][1]])
            P.memset("dve", xo[:], 0.0, [b_xo])
        hb = [Buf("hscr%d" % ti) for ti in range(ntiles)]
        if balanced is True and ntiles > 0:
            P.memset("dve", Dtot[:], 1.0, [b_Dtot])
            for ti in range(ntiles):
                P.dma(xt[:, :, :], xp[ti * 512:(ti + 1) * 512, :].rearrange("(n p) d -> p n d", p=128), writes=[b_xt], sbuf=b_xt)
                ffn(512, xt, b_xt, "w_ffn1_gate", "w_ffn1_up", "w_ffn1_down")
                P.dma(hscr[ti * 512:(ti + 1) * 512, :].rearrange("(n p) d -> p n d", p=128), xt[:, :, :], reads=[b_xt],
                      writes=[hb[ti]], sbuf=b_xt)
                mixer(512, xt, b_xt, 1, lambda q: xo[:], "chain", None, s5o, so=True)
            P.memset("dve", exs[:], 0.0, [b_exs])
            P.cp("dve", exs[:, 0:256], Sf[0][0][:].rearrange("p a e -> p (a e)"), [Sf[0][1]], [b_exs])
            P.cp("dve", exs[:, 256:258], Dtot[:], [b_Dtot], [b_exs])
            P.cp("dve", exs[0:64, 258:322], xo[:].rearrange("p s g -> p (s g)"), [b_xo], [b_exs])
            b_exsrc, b_exdst, b_cc = Buf("exsrc"), Buf("exdst"), Buf("cc")
            P.dma(exsrc, exs[:], reads=[b_exs], writes=[b_exsrc], sbuf=b_exs)
            if NO_COLL:
                P.dma(exdst[0:128, :], exsrc, reads=[b_exsrc], writes=[b_exdst], sbuf=b_cc)
            else:
                P.coll("AllGather", exsrc, exdst, [list(range(NCORES))], reads=[b_exsrc], writes=[b_exdst], sbuf=b_cc)
            accS, b_accS = So[0]
            tmpS, b_tmpS = So[1]
            accX, b_accX = xo2[0]
            tmpX, b_tmpX = xo2[1]
            P.memset("dve", accS[:], 0.0, [b_accS])
            P.memset("dve", accX[:], 0.0, [b_accX])
            for i in range(NCORES):
                P.dma(exg[:], exdst[i * 128:(i + 1) * 128, :], reads=[b_exdst], writes=[b_exg], sbuf=b_exg)
                fi = flags[:, i:i + 1]
                for pair in range(2):
                    P.stt("dve", tmpS[:, pair, :], accS[:, pair, :], exg[:, 256 + pair:257 + pair], exg[:, pair * 128:(pair + 1) * 128],
                          ALU.mult, ALU.add, [b_accS, b_exg], [b_tmpS])
                P.tt("dve", tmpS[:], tmpS[:], accS[:], ALU.subtract, [b_tmpS, b_accS], [b_tmpS])
                P.stt("dve", accS[:], tmpS[:], fi, accS[:], ALU.mult, ALU.add, [b_tmpS, b_flags, b_accS], [b_accS])
                xi = exg[0:64, 258:322].rearrange("p (s g) -> p s g", s=2)
                P.tt("dve", st1[:], AB1[:], accX[:], ALU.mult, [b_AB1, b_accX], [b_st1])
                P.tt("dve", st2[:, 0, :], AB2[:, 0, :], accX[:, 1, :], ALU.mult, [b_AB2, b_accX], [b_st2])
                P.tt("dve", st2[:, 1, :], AB2[:, 1, :], accX[:, 0, :], ALU.mult, [b_AB2, b_accX], [b_st2])
                P.tt("dve", st1[:], st1[:], st2[:], ALU.add, [b_st1, b_st2], [b_st1])
                P.tt("dve", st1[:], st1[:], xi, ALU.add, [b_st1, b_exg], [b_st1])
                P.tt("dve", tmpX[:], st1[:], accX[:], ALU.subtract, [b_st1, b_accX], [b_tmpX])
                P.stt("dve", accX[:], tmpX[:], fi[0:64, :], accX[:], ALU.mult, ALU.add, [b_tmpX, b_flags, b_accX], [b_accX])
            P.cp("dve", Sf[0][0][:], accS[:], [b_accS], [Sf[0][1]])
            P.cp("dve", Sb[0][0][:], accS[:], [b_accS], [Sb[0][1]])
            P.cp("dve", xo[:], accX[:], [b_accX], [b_xo])
        if balanced == "prefix":
            for ti in range(NPRE):
                P.dma(xt[:, :, :], xp[ti * 512:(ti + 1) * 512, :].rearrange("(n p) d -> p n d", p=128), writes=[b_xt], sbuf=b_xt)
                ffn(512, xt, b_xt, "w_ffn1_gate", "w_ffn1_up", "w_ffn1_down")
                mixer(512, xt, b_xt, 1, lambda q: xo[:], "chain", None, s5o, so=True)
            P.cp("dve", Sb[0][0][:], Sf[0][0][:], [Sf[0][1]], [Sb[0][1]])
        for ti in range(ntiles):
            if balanced is True:
                P.dma(xt[:, :, :], hscr[ti * 512:(ti + 1) * 512, :].rearrange("(n p) d -> p n d", p=128), reads=[hb[ti]],
                      writes=[b_xt], sbuf=b_xt)
            else:
                t0_ = (NPRE + ti) * 512
                P.dma(xt[:, :, :], xp[t0_:t0_ + 512, :].rearrange("(n p) d -> p n d", p=128), writes=[b_xt], sbuf=b_xt)
                ffn(512, xt, b_xt, "w_ffn1_gate", "w_ffn1_up", "w_ffn1_down")
            mixer(512, xt, b_xt, 1, lambda q: xo[:], "chain", None, s5o)
            ffn(512, xt, b_xt, "w_ffn2_gate", "w_ffn2_up", "w_ffn2_down")
            final_norm(512, xt, b_xt)
            P.dma(yp[ti * 512:(ti + 1) * 512, :].rearrange("(n p) d -> p n d", p=128), xt[:, :, :], reads=[b_xt], sbuf=b_xt)
        if ntiles > 0:
            P.dma(glap_o, Sf[0][0][:], reads=[Sf[0][1]], sbuf=Sf[0][1])
            P.dma(s5p_o, xo[:], reads=[b_xo], sbuf=b_xo)
        P.dma(xt[:, 0, :], xs, writes=[b_xt], sbuf=b_xt)
        for q in range(2):
            P.dma(Sf[q][0][:], sgla[q], writes=[Sf[q][1]], sbuf=Sf[q][1])
            P.cp("pool", Sb[q][0][:], Sf[q][0][:], [Sf[q][1]], [Sb[q][1]])
        P.dma(xin[:], ss5, writes=[b_xin], sbuf=b_xin)
        s5_init_reads = [b_xin]
        ffn(128, xt, b_xt, "w_ffn1_gate", "w_ffn1_up", "w_ffn1_down")

        def s5o_s(q, view):
            P.cp("pool", xo2[q][0][:], view, [b_Bst], [xo2[q][1]])
            P.dma(s5s_o[q], xo2[q][0][:], reads=[xo2[q][1]], sbuf=xo2[q][1])

        def glao_s(q, so):
            P.dma(glas_o[q], so[0][:], reads=[so[1]], sbuf=so[1])

        mixer(128, xt, b_xt, 2, lambda q: xin[:, :, :, q], "indep", glao_s, s5o_s)
        ffn(128, xt, b_xt, "w_ffn2_gate", "w_ffn2_up", "w_ffn2_down")
        final_norm(128, xt, b_xt)
        P.dma(ys, xt[:, 0, :], reads=[b_xt], sbuf=b_xt)
        P.emit(st)
    return nc


def _consts():
    idx = np.arange(128)
    same = (idx[:, None] // 64) == (idx[None, :] // 64)
    tri_inc = np.where(same & (idx[:, None] <= idx[None, :]), -1.0 / 16.0, 0.0).astype(np.float32)
    tri_rev = np.where(same & (idx[:, None] > idx[None, :]), -1.0 / 16.0, 0.0).astype(np.float32)
    cmask = np.where(same & (idx[:, None] <= idx[None, :]), 1.0, 0.0).astype(np.float32)
    s5mask = np.where((idx[None, :] // 16) >= (idx[:, None] // 16), 1.0, 0.0).astype(np.float32)
    return dict(ident=np.eye(128, dtype=np.float32), tri_inc=tri_inc, tri_rev=tri_rev, cmask=cmask, s5mask=s5mask)


def _lay_w(w):
    K, N = w.shape
    return np.ascontiguousarray(w.reshape(K // 128, 128, N).transpose(1, 0, 2))


def make_in_maps(inp, ntiles, seq_of_core, seg_of_core=None, prefix_tiles=0):
    if seg_of_core is None:
        seg_of_core = [0] * NCORES
    c = _consts()
    shared = dict(c)
    for name, K, N, _g in W_SPECS:
        shared[name] = _lay_w(np.asarray(inp[name], dtype=np.float32))
    for name, n in GAINS:
        shared[name] = np.ascontiguousarray(np.asarray(inp[name], np.float32).reshape(n, 128).T)
    shared["g_final"] = np.ascontiguousarray(np.broadcast_to(np.asarray(inp["g_final"], np.float32)[None, :], (128, D)))
    shared["w_gate_up"] = np.ascontiguousarray(inp["w_gate_up"], dtype=np.float32)
    shared["b_gate"] = np.ascontiguousarray(np.asarray(inp["b_gate"], np.float32)[None, :])
    shared["lam_re"] = np.ascontiguousarray(np.asarray(inp["s5_lam_re"], np.float32).T)
    shared["lam_im"] = np.ascontiguousarray(np.asarray(inp["s5_lam_im"], np.float32).T)
    shared["log_dt"] = np.ascontiguousarray(np.broadcast_to(np.asarray(inp["s5_log_dt"], np.float32)[None, :], (64, 32)))
    shared["b_re"] = np.ascontiguousarray(np.asarray(inp["s5_b_re"], np.float32).transpose(1, 0, 2))
    shared["b_im"] = np.ascontiguousarray(np.asarray(inp["s5_b_im"], np.float32).transpose(1, 0, 2))
    shared["c_re"] = np.ascontiguousarray(np.asarray(inp["s5_c_re"], np.float32).transpose(2, 0, 1))
    shared["c_im"] = np.ascontiguousarray(np.asarray(inp["s5_c_im"], np.float32).transpose(2, 0, 1))
    d = np.asarray(inp["s5_d"], np.float32).reshape(32, 16).T
    shared["dcol"] = np.ascontiguousarray(np.tile(d, (8, 1)))
    xp = np.asarray(inp["x_prompt"], np.float32)
    xs = np.asarray(inp["x_sample"], np.float32)
    sg = np.asarray(inp["state_gla"], np.float32)
    s5 = np.asarray(inp["state_s5"], np.float32)
    maps = []
    for core in range(NCORES):
        m = dict(shared)
        n0 = seg_of_core[core] * ntiles * 512
        if prefix_tiles:
            npre = prefix_tiles * 512
            buf = np.zeros((npre + ntiles * 512, D), np.float32)
            lo = max(0, n0 - npre)
            buf[npre - (n0 - lo):] = xp[seq_of_core[core]][lo:n0 + ntiles * 512]
            m["xp"] = buf
        else:
            m["xp"] = np.ascontiguousarray(xp[seq_of_core[core]][n0:n0 + ntiles * 512])
        fl = np.zeros((128, 8), np.float32)
        for i in range(NCORES):
            if seq_of_core[i] == seq_of_core[core] and seg_of_core[i] < seg_of_core[core]:
                fl[:, i] = 1.0
        m["flags"] = fl
        m["xs"] = np.ascontiguousarray(xs[2 * core:2 * core + 2].reshape(128, D))
        g2 = sg[2 * core:2 * core + 2]
        m["sgla"] = np.ascontiguousarray(g2.reshape(2, 2, 2, 64, 128).transpose(0, 2, 3, 1, 4).reshape(2, 128, 2, 128))
        s2 = s5[2 * core:2 * core + 2]
        m["ss5"] = np.ascontiguousarray(s2.transpose(2, 3, 1, 0))
        maps.append(m)
    return maps


def _unlay_gla(a):
    return a.reshape(2, 64, 2, 128).transpose(2, 0, 1, 3).reshape(4, 64, 128)


def _unlay_s5(a):
    return a.transpose(2, 0, 1)


_NC_CACHE = {}


def run(inp, ntiles, seq_of_core, seg_of_core=None, balanced=True):
    key = (ntiles, balanced)
    if key not in _NC_CACHE:
        _NC_CACHE[key] = build(ntiles, balanced)
    nc = _NC_CACHE[key]
    maps = make_in_maps(inp, ntiles, seq_of_core, seg_of_core, PREFIX_TILES if balanced == "prefix" else 0)
    res = run_bass_kernel_spmd(nc, maps, core_ids=list(range(NCORES)))
    return res.results


def kernel(**inputs):
    seq_of_core = [c // 4 for c in range(NCORES)]
    seg_of_core = [c % 4 for c in range(NCORES)]
    ntiles = 8
    r = run(inputs, ntiles, seq_of_core, seg_of_core, "prefix")
    y_prompt = np.stack([np.concatenate([r[q * 4 + j]["yp"] for j in range(4)]) for q in range(2)]).astype(np.float32)
    y_sample = np.concatenate([r[c]["ys"].reshape(2, 64, D) for c in range(NCORES)]).astype(np.float32)
    gla_p = np.stack([_unlay_gla(r[3]["glap"]), _unlay_gla(r[7]["glap"])]).astype(np.float32)
    s5_p = np.stack([_unlay_s5(r[3]["s5p"]), _unlay_s5(r[7]["s5p"])]).astype(np.float32)
    gla_s = np.stack([_unlay_gla(r[c]["glas"][q]) for c in range(NCORES) for q in range(2)]).astype(np.float32)
    s5_s = np.stack([_unlay_s5(r[c]["s5s"][q]) for c in range(NCORES) for q in range(2)]).astype(np.float32)
    return (y_prompt, y_sample, gla_p, s5_p, gla_s, s5_s)
```

```python
import math
from contextlib import ExitStack
import numpy as np
import concourse.bass as bass
import concourse.mybir as mybir
from concourse.bass_utils import run_bass_kernel_spmd

F32 = mybir.dt.float32
BF16 = mybir.dt.bfloat16
I32 = mybir.dt.int32
AF = mybir.ActivationFunctionType
ALU = mybir.AluOpType

D = 1024
KT = 8
FF = 2816
FT = 22
INC = 4112
EPS = 1e-6
NCORES = 8
NO_COLL = False
PREFIX_TILES = 24
WARM_COLL = False
TWO_PI = 2.0 * math.pi


class Buf:
    __slots__ = ("name", "last_w", "readers", "sem", "dcount")
    epoch_op = None

    def __init__(self, name):
        self.name = name
        self.last_w = Buf.epoch_op
        self.readers = []
        self.sem = None
        self.dcount = 0


class Op:
    __slots__ = ("eng", "fn", "deps", "is_dma", "needs_inc", "val", "sem", "inc")

    def __init__(self, eng, fn, is_dma):
        self.eng = eng
        self.fn = fn
        self.deps = {}
        self.is_dma = is_dma
        self.needs_inc = False
        self.val = None
        self.sem = None
        self.inc = 16


class Prog:
    ENGS = ("pe", "act", "dve", "pool", "sp")

    def __init__(self, nc):
        self.nc = nc
        self.ops = {e: [] for e in self.ENGS}
        self.all_ops = []
        self.dma_bufs = []

    def _dep(self, op, reads, writes):
        for r in reads:
            if r.last_w is not None:
                op.deps[r.last_w] = True
        for w in writes:
            if w.last_w is not None and w.last_w not in op.deps:
                op.deps[w.last_w] = False
            for rd in w.readers:
                if rd not in op.deps:
                    op.deps[rd] = False
        for r in reads:
            if not op.is_dma:
                r.readers = [x for x in r.readers if x.is_dma or x.eng != op.eng]
            r.readers.append(op)
        for w in writes:
            w.last_w = op
            w.readers = []

    def op(self, eng, fn, reads=(), writes=()):
        o = Op(eng, fn, False)
        self._dep(o, reads, writes)
        self.ops[eng].append(o)
        self.all_ops.append(o)
        return o

    def dma(self, out, in_, reads=(), writes=(), sbuf=None, eng="sp", slow=False):
        if slow:
            fn = lambda e: e.dma_start(out=out, in_=in_, allow_slow_non_contiguous=True)
        else:
            fn = lambda e: e.dma_start(out=out, in_=in_)
        o = Op(eng, fn, True)
        self._dep(o, reads, writes)
        if sbuf not in self.dma_bufs:
            self.dma_bufs.append(sbuf)
        sbuf.dcount += 16
        o.sem = sbuf
        o.val = sbuf.dcount
        self.ops[eng].append(o)
        self.all_ops.append(o)
        return o

    def coll(self, kind, src, dst, groups, reads, writes, sbuf, inc=1):
        fn = lambda e: e.collective_compute(kind, ALU.bypass, replica_groups=groups, ins=[src], outs=[dst])
        o = Op("pool", fn, True)
        self._dep(o, reads, writes)
        if sbuf not in self.dma_bufs:
            self.dma_bufs.append(sbuf)
        sbuf.dcount += inc
        o.sem = sbuf
        o.val = sbuf.dcount
        o.inc = inc
        self.ops["pool"].append(o)
        self.all_ops.append(o)
        return o

    def mm(self, out, lhsT, rhs, start, stop, reads, writes):
        return self.op("pe", lambda e: e.matmul(out, lhsT=lhsT, rhs=rhs, start=start, stop=stop),
                       reads, writes)

    def tr(self, out, in_, ident, reads, writes):
        return self.op("pe", lambda e: e.transpose(out=out, in_=in_, identity=ident), reads, writes)

    def act(self, out, in_, func, reads, writes, bias=None, scale=None, accum=None):
        kw = {}
        if bias is not None:
            kw["bias"] = bias
        if scale is not None:
            kw["scale"] = scale
        if accum is not None:
            kw["accum_out"] = accum
        return self.op("act", lambda e: e.activation(out=out, in_=in_, func=func, **kw), reads, writes)

    def tt(self, eng, out, in0, in1, op, reads, writes):
        return self.op(eng, lambda e: e.tensor_tensor(out=out, in0=in0, in1=in1, op=op), reads, writes)

    def ts(self, eng, out, in0, s1, op0, reads, writes, s2=None, op1=None):
        if op1 is None:
            return self.op(eng, lambda e: e.tensor_scalar(out=out, in0=in0, scalar1=s1, scalar2=None, op0=op0),
                           reads, writes)
        return self.op(eng, lambda e: e.tensor_scalar(out=out, in0=in0, scalar1=s1, scalar2=s2, op0=op0, op1=op1),
                       reads, writes)

    def stt(self, eng, out, in0, scalar, in1, op0, op1, reads, writes):
        return self.op(eng, lambda e: e.scalar_tensor_tensor(out=out, in0=in0, scalar=scalar, in1=in1,
                                                             op0=op0, op1=op1), reads, writes)

    def cp(self, eng, out, in_, reads, writes):
        if eng == "act":
            return self.act(out, in_, AF.Copy, reads, writes)
        return self.op(eng, lambda e: e.tensor_copy(out=out, in_=in_), reads, writes)

    def memset(self, eng, ap, val, writes):
        return self.op(eng, lambda e: e.memset(ap, val), (), writes)

    def fence(self, ap, fbuf, bufs):
        o = self.op("dve", lambda e: e.memset(ap, 0.0), (), list(bufs) + [fbuf])
        Buf.epoch_op = o
        return o

    def recip(self, out, in_, reads, writes):
        return self.op("dve", lambda e: e.reciprocal(out=out, in_=in_), reads, writes)

    @staticmethod
    def _need(o, d, raw):
        if d.is_dma:
            if o.is_dma and not raw:
                return False
            return True
        if d.eng == o.eng:
            if d.eng in ("pe", "pool"):
                return False
            return raw
        return True

    def emit(self, stack):
        nc = self.nc
        for o in self.all_ops:
            for d, raw in o.deps.items():
                if self._need(o, d, raw) and not d.is_dma:
                    d.needs_inc = True
        esem = {}
        for e in self.ENGS:
            esem[e] = stack.enter_context(nc.semaphore("s_" + e))
            c = 0
            for o in self.ops[e]:
                if not o.is_dma:
                    if o.needs_inc:
                        c += 1
                        o.val = c
                    o.sem = e
        for b in self.dma_bufs:
            b.sem = stack.enter_context(nc.semaphore("d_" + b.name))
        block = stack.enter_context(nc.Block())

        def replay(ename, eng):
            waited = {}
            for o in self.ops[ename]:
                for d, raw in o.deps.items():
                    if not self._need(o, d, raw):
                        continue
                    if d.is_dma:
                        key, sem = id(d.sem), d.sem.sem
                    else:
                        key, sem = d.sem, esem[d.sem]
                    if waited.get(key, 0) >= d.val:
                        continue
                    waited[key] = d.val
                    eng.wait_ge(sem, d.val)
                ins = o.fn(eng)
                if o.is_dma:
                    ins.then_inc(o.sem.sem, o.inc)
                elif o.needs_inc:
                    ins.then_inc(esem[ename], 1)
            if ename == "sp":
                for b in self.dma_bufs:
                    if waited.get(id(b), 0) < b.dcount:
                        eng.wait_ge(b.sem, b.dcount)

        @block.tensor
        def _(eng):
            replay("pe", eng)

        @block.scalar
        def _(eng):
            replay("act", eng)

        @block.vector
        def _(eng):
            replay("dve", eng)

        @block.gpsimd
        def _(eng):
            replay("pool", eng)

        @block.sync
        def _(eng):
            replay("sp", eng)


W_SPECS = [
    ("w_ffn1_gate", 1024, FF, "g_ffn1"), ("w_ffn1_up", 1024, FF, "g_ffn1"), ("w_ffn1_down", FF, 1024, None),
    ("w_in", 1024, INC, "g_mix"), ("w_gla_out", 512, 1024, "g_gla_head"), ("w_glu_a", 512, 512, None),
    ("w_glu_b", 512, 512, None), ("w_s5_out", 512, 1024, None), ("w_out", 1024, 1024, None),
    ("w_ffn2_gate", 1024, FF, "g_ffn2"), ("w_ffn2_up", 1024, FF, "g_ffn2"), ("w_ffn2_down", FF, 1024, None),
]
GAINS = [("g_ffn1", 8), ("g_mix", 8), ("g_ffn2", 8), ("g_gla_head", 4)]


def build(ntiles, balanced=True):
    Buf.epoch_op = None
    nc = bass.Bass("TRN2", target_bir_lowering=False)
    NPRE = PREFIX_TILES if balanced == "prefix" else 0
    NP = ntiles * 512
    NPIN = (ntiles + NPRE) * 512

    def din(name, shape, dt=F32):
        return nc.dram_tensor(name, list(shape), dt, kind="ExternalInput").ap()

    def dout(name, shape):
        return nc.dram_tensor(name, list(shape), F32, kind="ExternalOutput").ap()

    xp = din("xp", [NPIN, D])
    xs = din("xs", [128, D])
    sgla = din("sgla", [2, 128, 2, 128])
    ss5 = din("ss5", [64, 2, 32, 2])
    wd = {}
    ws = {}
    for name, K, N, _g in W_SPECS:
        wd[name] = din(name, [128, K // 128, N])
        ws[name] = nc.dram_tensor("scr_" + name, [128, K // 128, N], BF16).ap()
    gd = {name: din(name, [128, n]) for name, n in GAINS}
    g_final_d = din("g_final", [128, D])
    wgu_d = din("w_gate_up", [16, 256])
    bgate_d = din("b_gate", [1, 256])
    lamre_d = din("lam_re", [64, 32])
    lamim_d = din("lam_im", [64, 32])
    logdt_d = din("log_dt", [64, 32])
    bre_d = din("b_re", [64, 32, 16])
    bim_d = din("b_im", [64, 32, 16])
    cre_d = din("c_re", [64, 32, 16])
    cim_d = din("c_im", [64, 32, 16])
    dcol_d = din("dcol", [128, 32])
    ident_d = din("ident", [128, 128])
    triinc_d = din("tri_inc", [128, 128])
    trirev_d = din("tri_rev", [128, 128])
    cmask_d = din("cmask", [128, 128])
    s5mask_d = din("s5mask", [128, 128])
    flags_d = din("flags", [128, 8])
    hscr = nc.dram_tensor("hscr", [max(NP, 128) if balanced is True else 128, D], F32).ap()
    exsrc = nc.dram_tensor("exsrc", [128, 384], F32).ap()
    exdst = nc.dram_tensor("exdst", [8 * 128, 384], F32).ap()

    yp = dout("yp", [NP, D])
    ys = dout("ys", [128, D])
    glap_o = dout("glap", [128, 2, 128])
    s5p_o = dout("s5p", [64, 2, 32])
    glas_o = dout("glas", [2, 128, 2, 128])
    s5s_o = dout("s5s", [2, 64, 2, 32])

    with ExitStack() as st:
        P = Prog(nc)
        cnt = [0]

        def sb(shape, dt=F32, name=None):
            cnt[0] += 1
            nm = (name or "t") + "_%d" % cnt[0]
            return st.enter_context(nc.sbuf_tensor(nm, list(shape), dt)), Buf(nm)

        banks = []
        for i in range(6):
            t = st.enter_context(nc.psum_tensor("pb%d" % i, [128, 512], F32))
            banks.append((t, Buf("pb%d" % i)))
        bbanks = []
        for i in range(2):
            t = st.enter_context(nc.psum_tensor("pbb%d" % i, [128, 1024], BF16))
            bbanks.append((t, Buf("pbb%d" % i)))
        bk = [0]

        def nb():
            bk[0] = (bk[0] + 1) % 6
            return banks[bk[0]]

        bbk = [0]

        def nbb():
            bbk[0] = (bbk[0] + 1) % 2
            return bbanks[bbk[0]]

        identf, b_identf = sb([128, 128], F32, "identf")
        identb, b_identb = sb([128, 128], BF16, "identb")
        triinc, b_triinc = sb([128, 128], F32, "triinc")
        trirev, b_trirev = sb([128, 128], F32, "trirev")
        cmask, b_cmask = sb([128, 128], F32, "cmask")
        s5mask, b_s5mask = sb([128, 128], F32, "s5mask")
        gfin, b_gfin = sb([128, D], F32, "gfin")
        wgu, b_wgu = sb([16, 256], F32, "wgu")
        bgate, b_bgate = sb([1, 256], F32, "bgate")
        ones1, b_ones1 = sb([1, 128], F32, "ones1")
        dcol, b_dcol = sb([128, 32], F32, "dcol")
        T0, b_T0 = sb([128, 32, 128], BF16, "T0")
        Wm, b_Wm = sb([128, 32, 2, 64], BF16, "Wm")
        Vm, b_Vm = sb([64, 32, 2, 128], BF16, "Vm")
        A1, b_A1 = sb([64, 2, 32], F32, "A1")
        A2, b_A2 = sb([64, 2, 32], F32, "A2")
        AB1, b_AB1 = sb([64, 2, 32], F32, "AB1")
        AB2, b_AB2 = sb([64, 2, 32], F32, "AB2")
        flags, b_flags = sb([128, 8], F32, "flags")
        Dtot, b_Dtot = sb([128, 2], F32, "Dtot")
        exs, b_exs = sb([128, 384], F32, "exs")
        exg, b_exg = sb([128, 384], F32, "exg")
        gcols = {}
        for name, n in GAINS:
            gcols[name] = sb([128, n], F32, name)

        for t_, b_, d_ in [(identf, b_identf, ident_d), (triinc, b_triinc, triinc_d), (trirev, b_trirev, trirev_d),
                           (cmask, b_cmask, cmask_d), (s5mask, b_s5mask, s5mask_d), (gfin, b_gfin, g_final_d),
                           (wgu, b_wgu, wgu_d), (bgate, b_bgate, bgate_d), (dcol, b_dcol, dcol_d), (flags, b_flags, flags_d)]:
            P.dma(t_[:], d_, writes=[b_], sbuf=b_)
        for name, n in GAINS:
            P.dma(gcols[name][0][:], gd[name], writes=[gcols[name][1]], sbuf=gcols[name][1])
        if balanced is True and not NO_COLL and WARM_COLL:
            wsrc = nc.dram_tensor("wsrc", [128, 64], F32).ap()
            wdst = nc.dram_tensor("wdst", [8 * 128, 64], F32).ap()
            b_wsrc, b_wdst, b_wcc = Buf("wsrc"), Buf("wdst"), Buf("wcc")
            P.dma(wsrc, ident_d[:, 0:64], writes=[b_wsrc], sbuf=b_identf)
            P.coll("AllGather", wsrc, wdst, [list(range(NCORES))], reads=[b_wsrc], writes=[b_wdst], sbuf=b_wcc)
        P.cp("dve", identb[:], identf[:], [b_identf], [b_identb])
        P.memset("dve", ones1[:], 1.0, [b_ones1])

        prep_bufs = []
        with ExitStack() as pst:
            def psb(shape, dt=F32, name=None):
                cnt[0] += 1
                nm = (name or "p") + "_%d" % cnt[0]
                b = Buf(nm)
                prep_bufs.append(b)
                return pst.enter_context(nc.sbuf_tensor(nm, list(shape), dt)), b

            CH = 2816
            stg = [psb([128, CH], F32, "stg") for _ in range(4)]
            stb = [psb([128, CH], BF16, "stb") for _ in range(4)]
            ci = 0
            cast_engs = ["dve", "pool", "act"]
            scr_bufs = {name: Buf("scr_" + name) for name, _, _, _ in W_SPECS}
            for name, K, N, gk in W_SPECS:
                for kt in range(K // 128):
                    for c0 in range(0, N, CH):
                        cw = min(CH, N - c0)
                        s_t, s_b = stg[ci % 4]
                        o_t, o_b = stb[ci % 4]
                        P.dma(s_t[:, 0:cw], wd[name][:, kt, c0:c0 + cw], writes=[s_b], sbuf=s_b)
                        eng = cast_engs[ci % 3]
                        if gk is None:
                            P.cp(eng, o_t[:, 0:cw], s_t[:, 0:cw], [s_b], [o_b])
                        else:
                            gt, gb = gcols[gk]
                            if eng == "act":
                                P.act(o_t[:, 0:cw], s_t[:, 0:cw], AF.Copy, [s_b, gb], [o_b], scale=gt[:, kt:kt + 1])
                            else:
                                P.ts(eng, o_t[:, 0:cw], s_t[:, 0:cw], gt[:, kt:kt + 1], ALU.mult, [s_b, gb], [o_b])
                        P.dma(ws[name][:, kt, c0:c0 + cw], o_t[:, 0:cw], reads=[o_b], writes=[scr_bufs[name]], sbuf=o_b)
                        ci += 1

        fence_t, fence_b = sb([128, 1], F32, "fence")
        P.fence(fence_t[:], fence_b, prep_bufs + list(scr_bufs.values()))
        prep_bufs = []
        with ExitStack() as pst:
            def psb(shape, dt=F32, name=None):
                cnt[0] += 1
                nm = (name or "p") + "_%d" % cnt[0]
                b = Buf(nm)
                prep_bufs.append(b)
                return pst.enter_context(nc.sbuf_tensor(nm, list(shape), dt)), b

            def small(shape, name):
                return psb(shape, F32, name)

            lr, b_lr = small([64, 32], "lr")
            li, b_li = small([64, 32], "li")
            ldt, b_ldt = small([64, 32], "ldt")
            P.dma(lr[:], lamre_d, writes=[b_lr], sbuf=b_lr)
            P.dma(li[:], lamim_d, writes=[b_li], sbuf=b_li)
            P.dma(ldt[:], logdt_d, writes=[b_ldt], sbuf=b_ldt)
            Bre, b_Bre = small([64, 32, 16], "Bre")
            Bim, b_Bim = small([64, 32, 16], "Bim")
            Cre, b_Cre = small([64, 32, 16], "Cre")
            Cim, b_Cim = small([64, 32, 16], "Cim")
            for t_, b_, d_ in [(Bre, b_Bre, bre_d), (Bim, b_Bim, bim_d), (Cre, b_Cre, cre_d), (Cim, b_Cim, cim_d)]:
                P.dma(t_[:], d_, writes=[b_], sbuf=b_)
            dt_, b_dt = small([64, 32], "dt")
            P.act(dt_[:], ldt[:], AF.Exp, [b_ldt], [b_dt])
            aa, b_aa = small([64, 32], "aa")
            th, b_th = small([64, 32], "th")
            P.tt("dve", aa[:], lr[:], dt_[:], ALU.mult, [b_lr, b_dt], [b_aa])
            P.tt("dve", th[:], li[:], dt_[:], ALU.mult, [b_li, b_dt], [b_th])
            mag, b_mag = small([64, 32], "mag")
            P.act(mag[:], aa[:], AF.Exp, [b_aa], [b_mag])
            ki, b_ki = psb([64, 32], I32, "ki")
            kf, b_kf = small([64, 32], "kf")
            P.ts("dve", ki[:], th[:], 1.0 / TWO_PI, ALU.mult, [b_th], [b_ki])
            P.cp("dve", kf[:], ki[:], [b_ki], [b_kf])
            C1 = 6.28125
            C2 = TWO_PI - C1
            thr, b_thr = small([64, 32], "thr")
            P.stt("dve", thr[:], kf[:], -C1, th[:], ALU.mult, ALU.add, [b_kf, b_th], [b_thr])
            P.stt("dve", thr[:], kf[:], -C2, thr[:], ALU.mult, ALU.add, [b_kf, b_thr], [b_thr])
            sn, b_sn = small([64, 32], "sn")
            cs, b_cs = small([64, 32], "cs")
            ab, b_ab = small([64, 32], "ab")
            P.act(sn[:], thr[:], AF.Sin, [b_thr], [b_sn])
            P.act(ab[:], thr[:], AF.Abs, [b_thr], [b_ab])
            halfpi, b_halfpi = small([64, 1], "halfpi")
            P.memset("dve", halfpi[:], math.pi / 2.0, [b_halfpi])
            P.act(cs[:], ab[:], AF.Sin, [b_ab, b_halfpi], [b_cs], bias=halfpi[:], scale=-1.0)
            LP, b_LP = small([64, 2, 9, 32], "LP")
            P.memset("dve", LP[:, 0, 0, :], 1.0, [b_LP])
            P.memset("dve", LP[:, 1, 0, :], 0.0, [b_LP])
            P.tt("dve", LP[:, 0, 1, :], mag[:], cs[:], ALU.mult, [b_mag, b_cs], [b_LP])
            P.tt("dve", LP[:, 1, 1, :], mag[:], sn[:], ALU.mult, [b_mag, b_sn], [b_LP])
            tA, b_tA = small([64, 32], "tA")
            tB, b_tB = small([64, 32], "tB")

            def cmul(o_re, o_im, a_re, a_im, c_re, c_im, rd, wr, shape_t=None):
                t1, bt1 = shape_t[0]
                t2, bt2 = shape_t[1]
                P.tt("dve", t1, a_re, c_re, ALU.mult, rd, [bt1])
                P.tt("dve", t2, a_im, c_im, ALU.mult, rd, [bt2])
                P.tt("dve", o_re, t1, t2, ALU.subtract, [bt1, bt2], wr)
                P.tt("dve", t1, a_re, c_im, ALU.mult, rd, [bt1])
                P.tt("dve", t2, a_im, c_re, ALU.mult, rd, [bt2])
                P.tt("dve", o_im, t1, t2, ALU.add, [bt1, bt2], wr)

            for tau in range(2, 9):
                cmul(LP[:, 0, tau, :], LP[:, 1, tau, :], LP[:, 0, tau - 1, :], LP[:, 1, tau - 1, :],
                     LP[:, 0, 1, :], LP[:, 1, 1, :], [b_LP], [b_LP], [(tA[:], b_tA), (tB[:], b_tB)])
            P.cp("dve", A1[:, 0, :], LP[:, 0, 8, :], [b_LP], [b_A1])
            P.cp("dve", A1[:, 1, :], LP[:, 0, 8, :], [b_LP], [b_A1])
            P.ts("dve", A2[:, 0, :], LP[:, 1, 8, :], -1.0, ALU.mult, [b_LP], [b_A2])
            P.cp("dve", A2[:, 1, :], LP[:, 1, 8, :], [b_LP], [b_A2])
            nsq = int(round(math.log2(max(ntiles, 1) * 64)))
            assert 2 ** nsq == max(ntiles, 1) * 64
            LB, b_LB = small([64, 2, 32], "LB")
            P.cp("dve", LB[:, 0, :], LP[:, 0, 8, :], [b_LP], [b_LB])
            P.cp("dve", LB[:, 1, :], LP[:, 1, 8, :], [b_LP], [b_LB])
            for _ in range(nsq):
                P.tt("dve", tA[:], LB[:, 0, :], LB[:, 0, :], ALU.mult, [b_LB], [b_tA])
                P.tt("dve", tB[:], LB[:, 1, :], LB[:, 1, :], ALU.mult, [b_LB], [b_tB])
                P.stt("dve", LB[:, 1, :], LB[:, 0, :], 2.0, LB[:, 1, :], ALU.mult, ALU.mult, [b_LB], [b_LB])
                P.tt("dve", LB[:, 0, :], tA[:], tB[:], ALU.subtract, [b_tA, b_tB], [b_LB])
            P.cp("dve", AB1[:, 0, :], LB[:, 0, :], [b_LB], [b_AB1])
            P.cp("dve", AB1[:, 1, :], LB[:, 0, :], [b_LB], [b_AB1])
            P.ts("dve", AB2[:, 0, :], LB[:, 1, :], -1.0, ALU.mult, [b_LB], [b_AB2])
            P.cp("dve", AB2[:, 1, :], LB[:, 1, :], [b_LB], [b_AB2])
            inv, b_inv = small([64, 2, 32], "inv")
            den, b_den = small([64, 32], "den")
            P.tt("dve", tA[:], LP[:, 0, 8, :], LP[:, 0, 8, :], ALU.mult, [b_LP], [b_tA])
            P.tt("dve", tB[:], LP[:, 1, 8, :], LP[:, 1, 8, :], ALU.mult, [b_LP], [b_tB])
            P.tt("dve", den[:], tA[:], tB[:], ALU.add, [b_tA, b_tB], [b_den])
            P.recip(den[:], den[:], [b_den], [b_den])
            P.tt("dve", inv[:, 0, :], LP[:, 0, 8, :], den[:], ALU.mult, [b_LP, b_den], [b_inv])
            P.stt("dve", inv[:, 1, :], LP[:, 1, 8, :], -1.0, den[:], ALU.mult, ALU.mult, [b_LP, b_den], [b_inv])
            fre, b_fre = small([64, 32], "fre")
            fim, b_fim = small([64, 32], "fim")
            nr, b_nr = small([64, 32], "nr")
            P.ts("dve", nr[:], LP[:, 0, 1, :], -1.0, ALU.add, [b_LP], [b_nr])
            P.tt("dve", tA[:], lr[:], lr[:], ALU.mult, [b_lr], [b_tA])
            P.tt("dve", tB[:], li[:], li[:], ALU.mult, [b_li], [b_tB])
            P.tt("dve", den[:], tA[:], tB[:], ALU.add, [b_tA, b_tB], [b_den])
            P.recip(den[:], den[:], [b_den], [b_den])
            P.tt("dve", tA[:], nr[:], lr[:], ALU.mult, [b_nr, b_lr], [b_tA])
            P.tt("dve", tB[:], LP[:, 1, 1, :], li[:], ALU.mult, [b_LP, b_li], [b_tB])
            P.tt("dve", fre[:], tA[:], tB[:], ALU.add, [b_tA, b_tB], [b_fre])
            P.tt("dve", fre[:], fre[:], den[:], ALU.mult, [b_fre, b_den], [b_fre])
            P.tt("dve", tA[:], LP[:, 1, 1, :], lr[:], ALU.mult, [b_LP, b_lr], [b_tA])
            P.tt("dve", tB[:], nr[:], li[:], ALU.mult, [b_nr, b_li], [b_tB])
            P.tt("dve", fim[:], tA[:], tB[:], ALU.subtract, [b_tA, b_tB], [b_fim])
            P.tt("dve", fim[:], fim[:], den[:], ALU.mult, [b_fim, b_den], [b_fim])

            def bc(ap2d):
                return ap2d.unsqueeze(2).to_broadcast([64, 32, 16])

            u1, b_u1 = small([64, 32, 16], "u1")
            u2, b_u2 = small([64, 32, 16], "u2")
            utmp = [(u1[:], b_u1), (u2[:], b_u2)]
            bbre, b_bbre = small([64, 32, 16], "bbre")
            bbim, b_bbim = small([64, 32, 16], "bbim")
            cmul(bbre[:], bbim[:], Bre[:], Bim[:], bc(fre[:]), bc(fim[:]), [b_Bre, b_Bim, b_fre, b_fim],
                 [b_bbre, b_bbim], utmp)
            BL, b_BL = small([64, 2, 32, 8, 16], "BL")
            for s in range(8):
                tau = 7 - s
                cmul(BL[:, 0, :, s, :], BL[:, 1, :, s, :], bbre[:], bbim[:], bc(LP[:, 0, tau, :]), bc(LP[:, 1, tau, :]),
                     [b_bbre, b_bbim, b_LP], [b_BL], utmp)
            CL, b_CL = small([64, 2, 32, 8, 16], "CL")
            for t in range(8):
                cmul(CL[:, 0, :, t, :], CL[:, 1, :, t, :], Cre[:], Cim[:], bc(LP[:, 0, t + 1, :]), bc(LP[:, 1, t + 1, :]),
                     [b_Cre, b_Cim, b_LP], [b_CL], utmp)
            P.cp("dve", Vm[:, :, 0, :].rearrange("p g (t j) -> p g t j", t=8), CL[:, 0, :, :, :], [b_CL], [b_Vm])
            P.ts("dve", Vm[:, :, 1, :].rearrange("p g (t j) -> p g t j", t=8), CL[:, 1, :, :, :], -1.0, ALU.mult,
                 [b_CL], [b_Vm])
            CLp, b_CLp = small([64, 2, 32, 8, 16], "CLp")
            v1, b_v1 = small([64, 32, 8, 16], "v1")
            v2, b_v2 = small([64, 32, 8, 16], "v2")

            def bc4(ap2d):
                return ap2d.unsqueeze(2).unsqueeze(3).to_broadcast([64, 32, 8, 16])

            cmul(CLp[:, 0], CLp[:, 1], CL[:, 0], CL[:, 1], bc4(inv[:, 0, :]), bc4(inv[:, 1, :]), [b_CL, b_inv],
                 [b_CLp], [(v1[:], b_v1), (v2[:], b_v2)])
            P.ts("dve", CLp[:, 1], CLp[:, 1], -1.0, ALU.mult, [b_CLp], [b_CLp])
            tmpT, b_tmpT = small([128, 128], "tmpT")
            for g in range(32):
                pt, pb = nb()
                P.mm(pt[:, 0:128], BL[:, 0, g].rearrange("p s h -> p (s h)"), CLp[:, 0, g].rearrange("p t j -> p (t j)"),
                     True, False, [b_BL, b_CLp], [pb])
                P.mm(pt[:, 0:128], BL[:, 1, g].rearrange("p s h -> p (s h)"), CLp[:, 1, g].rearrange("p t j -> p (t j)"),
                     False, True, [b_BL, b_CLp], [pb])
                P.tt("dve", tmpT[:], pt[:, 0:128], s5mask[:], ALU.mult, [pb, b_s5mask], [b_tmpT])
                P.stt("dve", T0[:, g, :], identf[:], dcol[:, g:g + 1], tmpT[:], ALU.mult, ALU.add,
                      [b_identf, b_dcol, b_tmpT], [b_T0])
                for slot in range(2):
                    pt2, pb2 = nb()
                    P.tr(pt2[:, 0:64], BL[:, slot, g].rearrange("p s h -> p (s h)"), identf[0:64, 0:64], [b_BL, b_identf], [pb2])
                    P.cp("act", Wm[:, g, slot, :], pt2[:, 0:64], [pb2], [b_Wm])
        P.fence(fence_t[:], fence_b, prep_bufs)
        main_bufs = []

        def msb(shape, dt=F32, name=None):
            t, b = sb(shape, dt, name)
            main_bufs.append(b)
            return t, b

        TM = 512
        xt, b_xt = msb([128, 4, D], F32, "xt")
        xn, b_xn = msb([128, D], BF16, "xn")
        xnT, b_xnT = msb([128, 8, TM], BF16, "xnT")
        actb, b_actb = msb([128, FT, TM], BF16, "actb")
        wblk = [msb([128, 4096], BF16, "wblk") for _ in range(2)]
        DN = 256
        wdn = [msb([128, FT, DN], BF16, "wdn") for _ in range(1)]
        sgt, b_sgt = msb([128, TM], F32, "sgt")
        junk, b_junk = msb([128, D], BF16, "junk")
        ssq, b_ssq = msb([128, 8], F32, "ssq")
        rstd, b_rstd = msb([128, 8], F32, "rstd")
        wga, b_wga = msb([128, 8, 16], BF16, "wga")
        gaT, b_gaT = msb([16, TM], F32, "gaT")
        Lb, b_Lb = msb([128, 1, 256], F32, "Lb")
        E1, b_E1 = msb([128, 2, TM], F32, "E1")
        E2, b_E2 = msb([128, 2, TM], F32, "E2")
        E3, b_E3 = msb([128, 1, 256], F32, "E3")
        qt, b_qt = msb([128, 2, TM], BF16, "qt")
        qa, b_qa = msb([128, 2, TM], BF16, "qa")
        qb, b_qb = msb([128, 2, TM], BF16, "qb")
        ktl, b_ktl = msb([128, 2, TM], BF16, "ktl")
        khat, b_khat = msb([128, 4, 256], BF16, "khat")
        vb, b_vb = msb([128, 4, 512], BF16, "vb")
        srb, b_srb = msb([128, 4, 512], BF16, "srb")
        gatesA, b_gA = msb([128, 4, 1024], BF16, "gatesA")
        gatesB, b_gB = msb([128, 4, 1024], BF16, "gatesB")
        ATb, b_ATb = msb([128, 4, 128], BF16, "ATb")
        onb, b_onb = msb([128, 512], BF16, "onb")
        mf, b_mf = msb([128, 512], F32, "mf")
        mbf, b_mbf = xn, b_xn
        g5T, b_g5T = msb([128, 4, TM], BF16, "g5T")
        onT, b_onT = g5T, b_g5T
        glu, b_glu = msb([128, 4, TM], BF16, "glu")
        Sf = [msb([128, 2, 128], F32, "Sf") for _ in range(2)]
        Sb = [msb([128, 2, 128], BF16, "Sb") for _ in range(2)]
        So = [msb([128, 2, 128], F32, "So") for _ in range(2)]
        dS, b_dS = msb([128, 2, 2], F32, "dS")
        actraw = actb[:].rearrange("p f t -> p (f t)")
        Uc = actraw[:, 0:4096].rearrange("p (g s h) -> p g s h", g=32, s=8)
        Gc = actraw[:, 0:4096].rearrange("p (t c) -> p t c", t=8)
        Ug = actraw[:, 4096:6144].rearrange("p (g c) -> p g c", g=32)
        Xbf = actraw[:, 6144:10240]
        Bst_full, b_Bst = msb([128, 2 * 32 * 65], F32, "Bst")
        Bst = Bst_full[0:64, :]
        ring4 = [wblk[0], wblk[1], (gatesA[:].rearrange("p a b -> p (a b)"), b_gA), (gatesB[:].rearrange("p a b -> p (a b)"), b_gB)]
        wdn.append((Bst_full[:].bitcast(BF16)[:, 0:FT * DN].rearrange("p (f n) -> p f n", f=FT), b_Bst))
        st1, b_st1 = msb([64, 2, 32], F32, "st1")
        st2, b_st2 = msb([64, 2, 32], F32, "st2")
        xo, b_xo = msb([64, 2, 32], F32, "xo")
        xin, b_xin = msb([64, 2, 32, 2], F32, "xin")
        xo2 = [msb([64, 2, 32], F32, "xo2") for _ in range(2)]

        P.memset("pool", qa[:], 0.0, [b_qa])
        P.memset("pool", qb[:], 0.0, [b_qb])

        def norm_T(T, src_t, src_b):
            NT = T // 128
            for ts_ in range(NT):
                P.act(junk[:], src_t[:, ts_, :], AF.Square, [src_b], [b_junk, b_ssq], accum=ssq[:, ts_:ts_ + 1])
                P.act(rstd[:, ts_:ts_ + 1], ssq[:, ts_:ts_ + 1], AF.Sqrt, [b_ssq], [b_rstd], bias=EPS, scale=1.0 / D)
                P.recip(rstd[:, ts_:ts_ + 1], rstd[:, ts_:ts_ + 1], [b_rstd], [b_rstd])
                P.ts("dve", xn[:], src_t[:, ts_, :], rstd[:, ts_:ts_ + 1], ALU.mult, [src_b, b_rstd], [b_xn])
                bt, bb = nbb()
                for kt in range(8):
                    P.tr(bt[:, kt * 128:(kt + 1) * 128], xn[:, kt * 128:(kt + 1) * 128], identb[:], [b_xn, b_identb], [bb])
                P.cp("dve" if ts_ % 2 else "act", xnT[:, :, ts_ * 128:(ts_ + 1) * 128],
                     bt[:].rearrange("p (k t) -> p k t", k=8), [bb], [b_xnT])

        wi = [0, 0]

        def load_blk(scr_name, view_fn, shape_fn, big=False):
            if big:
                t, b = ring4[wi[1] % 4]
                wi[1] += 1
            else:
                t, b = wblk[wi[0] % 2]
                wi[0] += 1
            v = shape_fn(t)
            P.dma(v, view_fn(ws[scr_name]), reads=[scr_bufs[scr_name]], writes=[b], sbuf=b)
            return v, b

        def ffn(T, h_t, h_b, wg, wu, wdn_name):
            NT = T // 128
            norm_T(T, h_t, h_b)
            gi = 0
            for c0 in range(0, FF, 512):
                cw = min(512, FF - c0)
                gv, gb = load_blk(wg, lambda a: a[:, :, c0:c0 + cw], lambda t: t[:, 0:8 * cw].rearrange("p (k n) -> p k n", k=8), big=True)
                uv, ub = load_blk(wu, lambda a: a[:, :, c0:c0 + cw], lambda t: t[:, 0:8 * cw].rearrange("p (k n) -> p k n", k=8), big=True)
                for f0 in range(0, cw, 128):
                    ft = (c0 + f0) // 128
                    pg, pgb = banks[(gi % 2) * 2]
                    pu, pub = banks[(gi % 2) * 2 + 1]
                    gi += 1
                    for kt in range(8):
                        P.mm(pg[:, 0:T], gv[:, kt, f0:f0 + 128], xnT[:, kt, 0:T], kt == 0, kt == 7, [gb, b_xnT], [pgb])
                    for kt in range(8):
                        P.mm(pu[:, 0:T], uv[:, kt, f0:f0 + 128], xnT[:, kt, 0:T], kt == 0, kt == 7, [ub, b_xnT], [pub])
                    P.act(sgt[:, 0:T], pg[:, 0:T], AF.Silu, [pgb], [b_sgt])
                    P.tt("dve", actb[:, ft, 0:T], sgt[:, 0:T], pu[:, 0:T], ALU.mult, [b_sgt, pub], [b_actb])
            for dq in range(D // DN):
                wt_, wb_ = wdn[dq % 2]
                P.dma(wt_[:], ws[wdn_name][:, :, dq * DN:(dq + 1) * DN], reads=[scr_bufs[wdn_name]], writes=[wb_], sbuf=wb_)
                for ts_ in range(NT):
                    po, pob = banks[4 + (ts_ % 2)]
                    for ft in range(FT):
                        P.mm(po[:, 0:DN], actb[:, ft, ts_ * 128:(ts_ + 1) * 128], wt_[:, ft, :], ft == 0, ft == FT - 1,
                             [b_actb, wb_], [pob])
                    P.stt("dve", h_t[:, ts_, dq * DN:(dq + 1) * DN], po[:, 0:DN], 0.5, h_t[:, ts_, dq * DN:(dq + 1) * DN],
                          ALU.mult, ALU.add, [pob, h_b], [h_b])

        def tok_proj(T, scr_name, c0, cw, consume):
            NT = T // 128
            wv, wb_ = load_blk(scr_name, lambda a: a[:, :, c0:c0 + cw], lambda t: t[:, 0:8 * cw].rearrange("p (k n) -> p k n", k=8))
            for ts_ in range(NT):
                pt, pb = nb()
                for kt in range(8):
                    P.mm(pt[:, 0:cw], xnT[:, kt, ts_ * 128:(ts_ + 1) * 128], wv[:, kt, :], kt == 0, kt == 7, [b_xnT, wb_], [pb])
                consume(ts_, pt[:, 0:cw], pb)

        def mixer(T, h_t, h_b, Q, s5_init, gla_mode, gla_out, s5_out, so=False):
            NT = T // 128
            NC = T // 8
            NCQ = NC // Q
            norm_T(T, h_t, h_b)
            P.dma(wga[:], ws["w_in"][:, :, 1536:1552], reads=[scr_bufs["w_in"]], writes=[b_wga], sbuf=b_wga)
            qkv, qkb = load_blk("w_in", lambda a: a[:, :, 0:512], lambda t: t[:, 0:4096].rearrange("p (k n) -> p k n", k=8))
            pt, pb = nb()
            for kt in range(8):
                P.mm(pt[0:16, 0:T], wga[:, kt, :], xnT[:, kt, 0:T], kt == 0, kt == 7, [b_wga, b_xnT], [pb])
            P.cp("dve", gaT[:, 0:T], pt[0:16, 0:T], [pb], [b_gaT])
            for ts_ in range(NT):
                pt, pb = nb()
                P.mm(pt[:, 0:256], gaT[:, ts_ * 128:(ts_ + 1) * 128], wgu[:], True, False, [b_gaT, b_wgu], [pb])
                P.mm(pt[:, 0:256], ones1[:], bgate[:], False, True, [b_ones1, b_bgate], [pb])
                P.act(mf[:, 0:256], pt[:, 0:256], AF.Exp, [pb], [b_mf], scale=-1.0)
                P.act(Lb[:, 0, :], mf[:, 0:256], AF.Ln, [b_mf], [b_Lb], bias=1.0)
                for pair in range(2):
                    pt2, pb2 = nb()
                    P.mm(pt2[:, 0:128], Lb[:, 0, pair * 128:(pair + 1) * 128], triinc[:], True, True, [b_Lb, b_triinc], [pb2])
                    P.act(E1[:, pair, ts_ * 128:(ts_ + 1) * 128], pt2[:, 0:128], AF.Exp, [pb2], [b_E1])
                    if not so:
                        P.act(E2[:, pair, ts_ * 128:(ts_ + 1) * 128], pt2[:, 0:128], AF.Exp, [pb2], [b_E2], scale=-1.0)
                pt3, pb3 = nb()
                P.mm(pt3[:, 0:256], trirev[:], Lb[:, 0, :], True, True, [b_trirev, b_Lb], [pb3])
                P.act(E3[:, 0, :], pt3[:, 0:256], AF.Exp, [pb3], [b_E3])
                pt4, pb4 = nb()
                for kt in range(8):
                    P.mm(pt4[:, 0:256], xnT[:, kt, ts_ * 128:(ts_ + 1) * 128], qkv[:, kt, 256:512], kt == 0, kt == 7, [b_xnT, qkb], [pb4])
                P.tt("dve", khat[:, ts_, :], pt4[:, 0:256], E3[:, 0, :], ALU.mult, [pb4, b_E3], [b_khat])
            for which in ([] if so else range(2)):
                for pair in range(2):
                    c0 = which * 256 + pair * 128
                    pt, pb = nb()
                    for kt in range(8):
                        P.mm(pt[:, 0:T], qkv[:, kt, c0:c0 + 128], xnT[:, kt, 0:T], kt == 0, kt == 7, [qkb, b_xnT], [pb])
                    if which == 0:
                        P.stt("dve", qt[:, pair, 0:T], E1[:, pair, 0:T], 0.125, pt[:, 0:T], ALU.mult, ALU.mult, [b_E1, pb], [b_qt])
                        qv = qt[:, pair, 0:T].rearrange("p (n c t) -> p n c t", c=2, t=64)
                        P.cp("pool", qa[:, pair, 0:T].rearrange("p (n c t) -> p n c t", c=2, t=64)[:, :, 0, :], qv[:, :, 0, :], [b_qt], [b_qa])
                        P.cp("pool", qb[:, pair, 0:T].rearrange("p (n c t) -> p n c t", c=2, t=64)[:, :, 1, :], qv[:, :, 1, :], [b_qt], [b_qb])
                    else:
                        P.tt("dve", ktl[:, pair, 0:T], E2[:, pair, 0:T], pt[:, 0:T], ALU.mult, [b_E2, pb], [b_ktl])
            tok_proj(T, "w_in", 512, 512, lambda ts_, p, pb: P.cp("act", vb[:, ts_, :], p, [pb], [b_vb]))
            if not so:
                tok_proj(T, "w_in", 1024, 512, lambda ts_, p, pb: P.act(srb[:, ts_, :], p, AF.Silu, [pb], [b_srb]))
            for i in ([] if so else range(4)):
                tok_proj(T, "w_in", 2064 + i * 512, 512,
                         lambda ts_, p, pb, i=i: P.act((gatesA if i < 2 else gatesB)[:, ts_, (i % 2) * 512:(i % 2 + 1) * 512], p,
                                                       AF.Sigmoid, [pb], [b_gA if i < 2 else b_gB]))
            uv_, ub_ = load_blk("w_in", lambda a: a[:, :, 1552:2064], lambda t: t[:, 0:4096].rearrange("p (k n) -> p k n", k=8))
            for s_lo in range(8):
                pt, pb = nb()
                for kt in range(8):
                    P.mm(pt[0:NC, 0:512], xnT[:, kt, s_lo:T:8], uv_[:, kt, :], kt == 0, kt == 7, [b_xnT, ub_], [pb])
                P.cp("act" if s_lo % 2 else "dve", Uc[0:NC, :, s_lo, :], pt[0:NC, 0:512].rearrange("c (g h) -> c g h", g=32), [pb], [b_actb])
            for half in range(2):
                bt, bb = nbb()
                for gg in range(16):
                    g = half * 16 + gg
                    P.tr(bt[:, gg * NC:(gg + 1) * NC], Uc[0:NC, g].rearrange("c s h -> c (s h)"), identb[0:NC, 0:NC], [b_actb, b_identb], [bb])
                P.cp("dve", Ug[:, half * 16:(half + 1) * 16, 0:NC], bt[:, 0:16 * NC].rearrange("p (g c) -> p g c", g=16), [bb], [b_actb])
            Bv = Bst[:, 0:2 * 32 * Q * (NCQ + 1)].rearrange("p (s g q c) -> p s g q c", s=2, g=32, q=Q)
            for q in range(Q):
                P.cp("pool", Bv[:, :, :, q, 0], s5_init(q), [b_xo] if s5_init_reads is None else s5_init_reads, [b_Bst])
            for g0 in range(0, 32, 4):
                pt, pb = nb()
                pv = pt[0:64, 0:2 * 4 * NC].rearrange("p (s g c) -> p s g c", s=2, g=4)
                for gg in range(4):
                    for slot in range(2):
                        P.mm(pv[:, slot, gg, :], Wm[:, g0 + gg, slot, :], Ug[:, g0 + gg, 0:NC], True, True, [b_Wm, b_actb], [pb])
                for q in range(Q):
                    P.cp("dve" if (g0 // 4) % 2 else "act", Bv[:, :, g0:g0 + 4, q, 1:NCQ + 1], pv[:, :, :, q * NCQ:(q + 1) * NCQ], [pb], [b_Bst])
            for c in range(NCQ):
                for q in range(Q):
                    P.tt("pool", st1[:], A1[:], Bv[:, :, :, q, c], ALU.mult, [b_A1, b_Bst], [b_st1])
                    P.tt("pool", st2[:, 0, :], A2[:, 0, :], Bv[:, 1, :, q, c], ALU.mult, [b_A2, b_Bst], [b_st2])
                    P.tt("pool", st2[:, 1, :], A2[:, 1, :], Bv[:, 0, :, q, c], ALU.mult, [b_A2, b_Bst], [b_st2])
                    P.tt("pool", st1[:], st1[:], st2[:], ALU.add, [b_st1, b_st2], [b_st1])
                    P.tt("pool", Bv[:, :, :, q, c + 1], Bv[:, :, :, q, c + 1], st1[:], ALU.add, [b_Bst, b_st1], [b_Bst])
            for q in range(Q):
                s5_out(q, Bv[:, :, :, q, NCQ])
            Xv = Xbf[0:64, 0:2 * 32 * NC].rearrange("p (s g c) -> p s g c", s=2, g=32)
            for q in ([] if so else range(Q)):
                P.cp("dve", Xv[:, :, :, q * NCQ:(q + 1) * NCQ], Bv[:, :, :, q, 0:NCQ], [b_Bst], [b_actb])
            for g0 in ([] if so else range(0, 32, 4)):
                pt, pb = nb()
                for gg in range(4):
                    g = g0 + gg
                    o_ = pt[0:NC, gg * 128:(gg + 1) * 128]
                    P.mm(o_, Ug[:, g, 0:NC], T0[:, g, :], True, False, [b_actb, b_T0], [pb])
                    P.mm(o_, Xv[:, 0, g, :], Vm[:, g, 0, :], False, False, [b_actb, b_Vm], [pb])
                    P.mm(o_, Xv[:, 1, g, :], Vm[:, g, 1, :], False, True, [b_actb, b_Vm], [pb])
                P.act(Gc[0:NC, :, g0 * 16:(g0 + 4) * 16].rearrange("c t (g j) -> c t g j", g=4),
                      pt[0:NC, 0:512].rearrange("c (g t j) -> c t g j", g=4, t=8), AF.Gelu, [pb], [b_actb])
            for th_ in ([] if so else range(2)):
                bt, bb = nbb()
                for tl in range(4):
                    for ct in range(4):
                        i = tl * 4 + ct
                        P.tr(bt[:, i * NC:(i + 1) * NC], Gc[0:NC, th_ * 4 + tl, ct * 128:(ct + 1) * 128], identb[0:NC, 0:NC],
                             [b_actb, b_identb], [bb])
                P.cp("dve", g5T[:, :, 0:T].rearrange("p k (c t) -> p t k c", t=8)[:, th_ * 4:(th_ + 1) * 4],
                     bt[:, 0:16 * NC].rearrange("p (t k c) -> p t k c", t=4, k=4), [bb], [b_g5T])
            if so:
                for ts_ in range(NT):
                    for pair in range(2):
                        P.cp("pool", dS[:, pair, :], E1[:, pair, ts_ * 128 + 63:ts_ * 128 + 128:64], [b_E1], [b_dS])
                    P.tt("pool", Dtot[:], Dtot[:], dS[:, :, 0], ALU.mult, [b_Dtot, b_dS], [b_Dtot])
                    P.tt("pool", Dtot[:], Dtot[:], dS[:, :, 1], ALU.mult, [b_Dtot, b_dS], [b_Dtot])
                    for c2 in range(2):
                        ps_ = slice(c2 * 64, c2 * 64 + 64)
                        src, dst = (Sf[0], Sf[1]) if c2 == 0 else (Sf[1], Sf[0])
                        for pair in range(2):
                            pt, pb = nb()
                            P.mm(pt[:, 0:256], khat[ps_, ts_, pair * 128:(pair + 1) * 128], vb[ps_, ts_, pair * 256:(pair + 1) * 256],
                                 True, True, [b_khat, b_vb], [pb])
                            for hp in range(2):
                                rs = slice(hp * 64, hp * 64 + 64)
                                P.stt("dve", dst[0][rs, pair, :], src[0][rs, pair, :], dS[rs, pair, c2:c2 + 1],
                                      pt[rs, hp * 128:(hp + 1) * 128], ALU.mult, ALU.add, [src[1], b_dS, pb], [dst[1]])
                return
            wa_v, wa_b = load_blk("w_glu_a", lambda a: a, lambda t: t[:, 0:2048].rearrange("p (k n) -> p k n", k=4))
            wb_v, wb_b = load_blk("w_glu_b", lambda a: a, lambda t: t[:, 0:2048].rearrange("p (k n) -> p k n", k=4))
            for nt_ in range(4):
                pa, pab = nb()
                for ct in range(4):
                    P.mm(pa[:, 0:T], wa_v[:, ct, nt_ * 128:(nt_ + 1) * 128], g5T[:, ct, 0:T], ct == 0, ct == 3, [wa_b, b_g5T], [pab])
                pb_, pbb = nb()
                for ct in range(4):
                    P.mm(pb_[:, 0:T], wb_v[:, ct, nt_ * 128:(nt_ + 1) * 128], g5T[:, ct, 0:T], ct == 0, ct == 3, [wb_b, b_g5T], [pbb])
                P.act(sgt[:, 0:T], pb_[:, 0:T], AF.Sigmoid, [pbb], [b_sgt])
                P.tt("dve", glu[:, nt_, 0:T], sgt[:, 0:T], pa[:, 0:T], ALU.mult, [b_sgt, pab], [b_glu])
            for ts_ in range(NT):
                tsl = slice(ts_ * 128, (ts_ + 1) * 128)
                if gla_mode == "chain":
                    s0f, s0b = Sf[0], Sb[0]
                    s1f, s1b = Sf[1], Sb[1]
                else:
                    s0f, s0b = Sf[0], Sb[0]
                    s1f, s1b = Sf[1], Sb[1]
                for pair in range(2):
                    P.cp("pool", dS[:, pair, :], E1[:, pair, ts_ * 128 + 63:ts_ * 128 + 128:64], [b_E1], [b_dS])
                for h in range(4):
                    pair, hp = h // 2, h % 2
                    rs = slice(hp * 64, hp * 64 + 64)
                    pt, pb = nb()
                    P.mm(pt[:, 0:128], ktl[rs, pair, tsl], qt[rs, pair, tsl], True, True, [b_ktl, b_qt], [pb])
                    P.tt("dve", ATb[:, h, :], pt[:, 0:128], cmask[:], ALU.mult, [pb, b_cmask], [b_ATb])
                kv = []
                for c2 in range(2):
                    ps_ = slice(c2 * 64, c2 * 64 + 64)
                    row = []
                    for pair in range(2):
                        pt, pb = nb()
                        P.mm(pt[:, 0:256], khat[ps_, ts_, pair * 128:(pair + 1) * 128], vb[ps_, ts_, pair * 256:(pair + 1) * 256],
                             True, True, [b_khat, b_vb], [pb])
                        row.append((pt, pb))
                    kv.append(row)

                def upd(dst_t, dst_b, src_t, src_b, c2, bf_t=None, bf_b=None):
                    for pair in range(2):
                        pt, pb = kv[c2][pair]
                        for hp in range(2):
                            rs = slice(hp * 64, hp * 64 + 64)
                            P.stt("dve", dst_t[rs, pair, :], src_t[rs, pair, :], dS[rs, pair, c2:c2 + 1],
                                  pt[rs, hp * 128:(hp + 1) * 128], ALU.mult, ALU.add, [src_b, b_dS, pb], [dst_b])
                    if bf_t is not None:
                        P.cp("pool", bf_t[:], dst_t[:], [dst_b], [bf_b])

                if gla_mode == "chain":
                    upd(s1f[0], s1f[1], s0f[0], s0f[1], 0, s1b[0], s1b[1])
                else:
                    upd(So[0][0], So[0][1], s0f[0], s0f[1], 0)
                    upd(So[1][0], So[1][1], s1f[0], s1f[1], 1)
                po, pob = nb()
                for h in range(4):
                    pair, hp = h // 2, h % 2
                    rs = slice(hp * 64, hp * 64 + 64)
                    o_ = po[:, h * 128:(h + 1) * 128]
                    P.mm(o_, ATb[:, h, :], vb[:, ts_, h * 128:(h + 1) * 128], True, False, [b_ATb, b_vb], [pob])
                    P.mm(o_, qa[rs, pair, tsl], s0b[0][rs, pair, :], False, False, [b_qa, s0b[1]], [pob])
                    P.mm(o_, qb[rs, pair, tsl], s1b[0][rs, pair, :], False, True, [b_qb, s1b[1]], [pob])
                if gla_mode == "chain":
                    upd(s0f[0], s0f[1], s1f[0], s1f[1], 1, s0b[0], s0b[1])
                for h in range(4):
                    P.act(junk[:, 0:128], po[:, h * 128:(h + 1) * 128], AF.Square, [pob], [b_junk, b_ssq], accum=ssq[:, 4 + h:5 + h])
                P.act(rstd[:, 4:8], ssq[:, 4:8], AF.Sqrt, [b_ssq], [b_rstd], bias=EPS, scale=1.0 / 128)
                P.recip(rstd[:, 4:8], rstd[:, 4:8], [b_rstd], [b_rstd])
                for h in range(4):
                    P.stt("dve", onb[:, h * 128:(h + 1) * 128], po[:, h * 128:(h + 1) * 128], rstd[:, 4 + h:5 + h],
                          srb[:, ts_, h * 128:(h + 1) * 128], ALU.mult, ALU.mult, [pob, b_rstd, b_srb], [b_onb])
                bt, bb = nbb()
                for ct in range(4):
                    P.tr(bt[:, ct * 128:(ct + 1) * 128], onb[:, ct * 128:(ct + 1) * 128], identb[:], [b_onb, b_identb], [bb])
                P.cp("act", onT[:, :, tsl], bt[:, 0:512].rearrange("p (k t) -> p k t", k=4), [bb], [b_onT])
            if gla_mode == "chain":
                pass
            else:
                for q in range(2):
                    gla_out(q, So[q])
            wgo_v, wgo_b = load_blk("w_gla_out", lambda a: a, lambda t: t[:, 0:4096].rearrange("p (k n) -> p k n", k=4))
            wso_v, wso_b = load_blk("w_s5_out", lambda a: a, lambda t: t[:, 0:4096].rearrange("p (k n) -> p k n", k=4))
            for ts_ in range(NT):
                tsl = slice(ts_ * 128, (ts_ + 1) * 128)
                for half in range(2):
                    hs = slice(half * 512, (half + 1) * 512)
                    pg_, pgb_ = nb()
                    for ct in range(4):
                        P.mm(pg_[:, :], onT[:, ct, tsl], wgo_v[:, ct, hs], ct == 0, ct == 3, [b_onT, wgo_b], [pgb_])
                    ps2, psb2 = nb()
                    for ct in range(4):
                        P.mm(ps2[:, :], glu[:, ct, tsl], wso_v[:, ct, hs], ct == 0, ct == 3, [b_glu, wso_b], [psb2])
                    P.tt("dve", mf[:], gatesA[:, ts_, hs], pg_[:, :], ALU.mult, [b_gA, pgb_], [b_mf])
                    P.tt("dve", sgt[:, 0:512], gatesB[:, ts_, hs], ps2[:, :], ALU.mult,
                         [b_gB, psb2], [b_sgt])
                    P.tt("pool", mbf[:, hs], mf[:], sgt[:, 0:512], ALU.add, [b_mf, b_sgt], [b_mbf])
                bt, bb = nbb()
                for kt in range(8):
                    P.tr(bt[:, kt * 128:(kt + 1) * 128], mbf[:, kt * 128:(kt + 1) * 128], identb[:], [b_mbf, b_identb], [bb])
                P.cp("act", xnT[:, :, tsl], bt[:].rearrange("p (k t) -> p k t", k=8), [bb], [b_xnT])
            for half in range(2):
                hs = slice(half * 512, (half + 1) * 512)
                wo_v, wo_b = load_blk("w_out", lambda a: a[:, :, half * 512:(half + 1) * 512],
                                      lambda t: t[:, 0:4096].rearrange("p (k n) -> p k n", k=8))
                for ts_ in range(NT):
                    pt, pb = nb()
                    for kt in range(8):
                        P.mm(pt[:, :], xnT[:, kt, ts_ * 128:(ts_ + 1) * 128], wo_v[:, kt, :], kt == 0, kt == 7, [b_xnT, wo_b], [pb])
                    P.tt("dve", h_t[:, ts_, hs], h_t[:, ts_, hs], pt[:, :], ALU.add, [h_b, pb], [h_b])

        def final_norm(T, h_t, h_b):
            NT = T // 128
            for ts_ in range(NT):
                P.act(junk[:], h_t[:, ts_, :], AF.Square, [h_b], [b_junk, b_ssq], accum=ssq[:, ts_:ts_ + 1])
                P.act(rstd[:, ts_:ts_ + 1], ssq[:, ts_:ts_ + 1], AF.Sqrt, [b_ssq], [b_rstd], bias=EPS, scale=1.0 / D)
                P.recip(rstd[:, ts_:ts_ + 1], rstd[:, ts_:ts_ + 1], [b_rstd], [b_rstd])
                P.stt("dve", h_t[:, ts_, :], h_t[:, ts_, :], rstd[:, ts_:ts_ + 1], gfin[:], ALU.mult, ALU.mult,
                      [h_b, b_rstd, b_gfin], [h_b])

        s5_init_reads = None
        def s5o(q, view):
            P.cp("pool", xo[:], view, [b_Bst], [b_xo])

        if ntiles > 0:
            P.memset("dve", Sf[0][0][:], 0.0, [Sf[0][1]])
            P.memset("dve", Sb[0][0][:], 0.0, [Sb[0][1]])
            P.memset("dve", xo[:], 0.0, [b_xo])
        hb = [Buf("hscr%d" % ti) for ti in range(ntiles)]
        if balanced is True and ntiles > 0:
            P.memset("dve", Dtot[:], 1.0, [b_Dtot])
            for ti in range(ntiles):
                P.dma(xt[:, :, :], xp[ti * 512:(ti + 1) * 512, :].rearrange("(n p) d -> p n d", p=128), writes=[b_xt], sbuf=b_xt)
                ffn(512, xt, b_xt, "w_ffn1_gate", "w_ffn1_up", "w_ffn1_down")
                P.dma(hscr[ti * 512:(ti + 1) * 512, :].rearrange("(n p) d -> p n d", p=128), xt[:, :, :], reads=[b_xt],
                      writes=[hb[ti]], sbuf=b_xt)
                mixer(512, xt, b_xt, 1, lambda q: xo[:], "chain", None, s5o, so=True)
            P.memset("dve", exs[:], 0.0, [b_exs])
            P.cp("dve", exs[:, 0:256], Sf[0][0][:].rearrange("p a e -> p (a e)"), [Sf[0][1]], [b_exs])
            P.cp("dve", exs[:, 256:258], Dtot[:], [b_Dtot], [b_exs])
            P.cp("dve", exs[0:64, 258:322], xo[:].rearrange("p s g -> p (s g)"), [b_xo], [b_exs])
            b_exsrc, b_exdst, b_cc = Buf("exsrc"), Buf("exdst"), Buf("cc")
            P.dma(exsrc, exs[:], reads=[b_exs], writes=[b_exsrc], sbuf=b_exs)
            if NO_COLL:
                P.dma(exdst[0:128, :], exsrc, reads=[b_exsrc], writes=[b_exdst], sbuf=b_cc)
            else:
                P.coll("AllGather", exsrc, exdst, [list(range(NCORES))], reads=[b_exsrc], writes=[b_exdst], sbuf=b_cc)
            accS, b_accS = So[0]
            tmpS, b_tmpS = So[1]
            accX, b_accX = xo2[0]
            tmpX, b_tmpX = xo2[1]
            P.memset("dve", accS[:], 0.0, [b_accS])
            P.memset("dve", accX[:], 0.0, [b_accX])
            for i in range(NCORES):
                P.dma(exg[:], exdst[i * 128:(i + 1) * 128, :], reads=[b_exdst], writes=[b_exg], sbuf=b_exg)
                fi = flags[:, i:i + 1]
                for pair in range(2):
                    P.stt("dve", tmpS[:, pair, :], accS[:, pair, :], exg[:, 256 + pair:257 + pair], exg[:, pair * 128:(pair + 1) * 128],
                          ALU.mult, ALU.add, [b_accS, b_exg], [b_tmpS])
                P.tt("dve", tmpS[:], tmpS[:], accS[:], ALU.subtract, [b_tmpS, b_accS], [b_tmpS])
                P.stt("dve", accS[:], tmpS[:], fi, accS[:], ALU.mult, ALU.add, [b_tmpS, b_flags, b_accS], [b_accS])
                xi = exg[0:64, 258:322].rearrange("p (s g) -> p s g", s=2)
                P.tt("dve", st1[:], AB1[:], accX[:], ALU.mult, [b_AB1, b_accX], [b_st1])
                P.tt("dve", st2[:, 0, :], AB2[:, 0, :], accX[:, 1, :], ALU.mult, [b_AB2, b_accX], [b_st2])
                P.tt("dve", st2[:, 1, :], AB2[:, 1, :], accX[:, 0, :], ALU.mult, [b_AB2, b_accX], [b_st2])
                P.tt("dve", st1[:], st1[:], st2[:], ALU.add, [b_st1, b_st2], [b_st1])
                P.tt("dve", st1[:], st1[:], xi, ALU.add, [b_st1, b_exg], [b_st1])
                P.tt("dve", tmpX[:], st1[:], accX[:], ALU.subtract, [b_st1, b_accX], [b_tmpX])
                P.stt("dve", accX[:], tmpX[:], fi[0:64, :], accX[:], ALU.mult, ALU.add, [b_tmpX, b_flags, b_accX], [b_accX])
            P.cp("dve", Sf[0][0][:], accS[:], [b_accS], [Sf[0][1]])
            P.cp("dve", Sb[0][0][:], accS[:], [b_accS], [Sb[0][1]])
            P.cp("dve", xo[:], accX[:], [b_accX], [b_xo])
        if balanced == "prefix":
            for ti in range(NPRE):
                P.dma(xt[:, :, :], xp[ti * 512:(ti + 1) * 512, :].rearrange("(n p) d -> p n d", p=128), writes=[b_xt], sbuf=b_xt)
                ffn(512, xt, b_xt, "w_ffn1_gate", "w_ffn1_up", "w_ffn1_down")
                mixer(512, xt, b_xt, 1, lambda q: xo[:], "chain", None, s5o, so=True)
            P.cp("dve", Sb[0][0][:], Sf[0][0][:], [Sf[0][1]], [Sb[0][1]])
        for ti in range(ntiles):
            if balanced is True:
                P.dma(xt[:, :, :], hscr[ti * 512:(ti + 1) * 512, :].rearrange("(n p) d -> p n d", p=128), reads=[hb[ti]],
                      writes=[b_xt], sbuf=b_xt)
            else:
                t0_ = (NPRE + ti) * 512
                P.dma(xt[:, :, :], xp[t0_:t0_ + 512, :].rearrange("(n p) d -> p n d", p=128), writes=[b_xt], sbuf=b_xt)
                ffn(512, xt, b_xt, "w_ffn1_gate", "w_ffn1_up", "w_ffn1_down")
            mixer(512, xt, b_xt, 1, lambda q: xo[:], "chain", None, s5o)
            ffn(512, xt, b_xt, "w_ffn2_gate", "w_ffn2_up", "w_ffn2_down")
            final_norm(512, xt, b_xt)
            P.dma(yp[ti * 512:(ti + 1) * 512, :].rearrange("(n p) d -> p n d", p=128), xt[:, :, :], reads=[b_xt], sbuf=b_xt)
        if ntiles > 0:
            P.dma(glap_o, Sf[0][0][:], reads=[Sf[0][1]], sbuf=Sf[0][1])
            P.dma(s5p_o, xo[:], reads=[b_xo], sbuf=b_xo)
        P.dma(xt[:, 0, :], xs, writes=[b_xt], sbuf=b_xt)
        for q in range(2):
            P.dma(Sf[q][0][:], sgla[q], writes=[Sf[q][1]], sbuf=Sf[q][1])
            P.cp("pool", Sb[q][0][:], Sf[q][0][:], [Sf[q][1]], [Sb[q][1]])
        P.dma(xin[:], ss5, writes=[b_xin], sbuf=b_xin)
        s5_init_reads = [b_xin]
        ffn(128, xt, b_xt, "w_ffn1_gate", "w_ffn1_up", "w_ffn1_down")

        def s5o_s(q, view):
            P.cp("pool", xo2[q][0][:], view, [b_Bst], [xo2[q][1]])
            P.dma(s5s_o[q], xo2[q][0][:], reads=[xo2[q][1]], sbuf=xo2[q][1])

        def glao_s(q, so):
            P.dma(glas_o[q], so[0][:], reads=[so[1]], sbuf=so[1])

        mixer(128, xt, b_xt, 2, lambda q: xin[:, :, :, q], "indep", glao_s, s5o_s)
        ffn(128, xt, b_xt, "w_ffn2_gate", "w_ffn2_up", "w_ffn2_down")
        final_norm(128, xt, b_xt)
        P.dma(ys, xt[:, 0, :], reads=[b_xt], sbuf=b_xt)
        P.emit(st)
    return nc


def _consts():
    idx = np.arange(128)
    same = (idx[:, None] // 64) == (idx[None, :] // 64)
    tri_inc = np.where(same & (idx[:, None] <= idx[None, :]), -1.0 / 16.0, 0.0).astype(np.float32)
    tri_rev = np.where(same & (idx[:, None] > idx[None, :]), -1.0 / 16.0, 0.0).astype(np.float32)
    cmask = np.where(same & (idx[:, None] <= idx[None, :]), 1.0, 0.0).astype(np.float32)
    s5mask = np.where((idx[None, :] // 16) >= (idx[:, None] // 16), 1.0, 0.0).astype(np.float32)
    return dict(ident=np.eye(128, dtype=np.float32), tri_inc=tri_inc, tri_rev=tri_rev, cmask=cmask, s5mask=s5mask)


def _lay_w(w):
    K, N = w.shape
    return np.ascontiguousarray(w.reshape(K // 128, 128, N).transpose(1, 0, 2))


def make_in_maps(inp, ntiles, seq_of_core, seg_of_core=None, prefix_tiles=0):
    if seg_of_core is None:
        seg_of_core = [0] * NCORES
    c = _consts()
    shared = dict(c)
    for name, K, N, _g in W_SPECS:
        shared[name] = _lay_w(np.asarray(inp[name], dtype=np.float32))
    for name, n in GAINS:
        shared[name] = np.ascontiguousarray(np.asarray(inp[name], np.float32).reshape(n, 128).T)
    shared["g_final"] = np.ascontiguousarray(np.broadcast_to(np.asarray(inp["g_final"], np.float32)[None, :], (128, D)))
    shared["w_gate_up"] = np.ascontiguousarray(inp["w_gate_up"], dtype=np.float32)
    shared["b_gate"] = np.ascontiguousarray(np.asarray(inp["b_gate"], np.float32)[None, :])
    shared["lam_re"] = np.ascontiguousarray(np.asarray(inp["s5_lam_re"], np.float32).T)
    shared["lam_im"] = np.ascontiguousarray(np.asarray(inp["s5_lam_im"], np.float32).T)
    shared["log_dt"] = np.ascontiguousarray(np.broadcast_to(np.asarray(inp["s5_log_dt"], np.float32)[None, :], (64, 32)))
    shared["b_re"] = np.ascontiguousarray(np.asarray(inp["s5_b_re"], np.float32).transpose(1, 0, 2))
    shared["b_im"] = np.ascontiguousarray(np.asarray(inp["s5_b_im"], np.float32).transpose(1, 0, 2))
    shared["c_re"] = np.ascontiguousarray(np.asarray(inp["s5_c_re"], np.float32).transpose(2, 0, 1))
    shared["c_im"] = np.ascontiguousarray(np.asarray(inp["s5_c_im"], np.float32).transpose(2, 0, 1))
    d = np.asarray(inp["s5_d"], np.float32).reshape(32, 16).T
    shared["dcol"] = np.ascontiguousarray(np.tile(d, (8, 1)))
    xp = np.asarray(inp["x_prompt"], np.float32)
    xs = np.asarray(inp["x_sample"], np.float32)
    sg = np.asarray(inp["state_gla"], np.float32)
    s5 = np.asarray(inp["state_s5"], np.float32)
    maps = []
    for core in range(NCORES):
        m = dict(shared)
        n0 = seg_of_core[core] * ntiles * 512
        if prefix_tiles:
            npre = prefix_tiles * 512
            buf = np.zeros((npre + ntiles * 512, D), np.float32)
            lo = max(0, n0 - npre)
            buf[npre - (n0 - lo):] = xp[seq_of_core[core]][lo:n0 + ntiles * 512]
            m["xp"] = buf
        else:
            m["xp"] = np.ascontiguousarray(xp[seq_of_core[core]][n0:n0 + ntiles * 512])
        fl = np.zeros((128, 8), np.float32)
        for i in range(NCORES):
            if seq_of_core[i] == seq_of_core[core] and seg_of_core[i] < seg_of_core[core]:
                fl[:, i] = 1.0
        m["flags"] = fl
        m["xs"] = np.ascontiguousarray(xs[2 * core:2 * core + 2].reshape(128, D))
        g2 = sg[2 * core:2 * core + 2]
        m["sgla"] = np.ascontiguousarray(g2.reshape(2, 2, 2, 64, 128).transpose(0, 2, 3, 1, 4).reshape(2, 128, 2, 128))
        s2 = s5[2 * core:2 * core + 2]
        m["ss5"] = np.ascontiguousarray(s2.transpose(2, 3, 1, 0))
        maps.append(m)
    return maps


def _unlay_gla(a):
    return a.reshape(2, 64, 2, 128).transpose(2, 0, 1, 3).reshape(4, 64, 128)


def _unlay_s5(a):
    return a.transpose(2, 0, 1)


_NC_CACHE = {}


def run(inp, ntiles, seq_of_core, seg_of_core=None, balanced=True):
    key = (ntiles, balanced)
    if key not in _NC_CACHE:
        _NC_CACHE[key] = build(ntiles, balanced)
    nc = _NC_CACHE[key]
    maps = make_in_maps(inp, ntiles, seq_of_core, seg_of_core, PREFIX_TILES if balanced == "prefix" else 0)
    res = run_bass_kernel_spmd(nc, maps, core_ids=list(range(NCORES)))
    return res.results


def kernel(**inputs):
    seq_of_core = [c // 4 for c in range(NCORES)]
    seg_of_core = [c % 4 for c in range(NCORES)]
    ntiles = 8
    r = run(inputs, ntiles, seq_of_core, seg_of_core, "prefix")
    y_prompt = np.stack([np.concatenate([r[q * 4 + j]["yp"] for j in range(4)]) for q in range(2)]).astype(np.float32)
    y_sample = np.concatenate([r[c]["ys"].reshape(2, 64, D) for c in range(NCORES)]).astype(np.float32)
    gla_p = np.stack([_unlay_gla(r[3]["glap"]), _unlay_gla(r[7]["glap"])]).astype(np.float32)
    s5_p = np.stack([_unlay_s5(r[3]["s5p"]), _unlay_s5(r[7]["s5p"])]).astype(np.float32)
    gla_s = np.stack([_unlay_gla(r[c]["glas"][q]) for c in range(NCORES) for q in range(2)]).astype(np.float32)
    s5_s = np.stack([_unlay_s5(r[c]["s5s"][q]) for c in range(NCORES) for q in range(2)]).astype(np.float32)
    return (y_prompt, y_sample, gla_p, s5_p, gla_s, s5_s)
```

```python
import math
from contextlib import ExitStack
import numpy as np
import concourse.bass as bass
import concourse.mybir as mybir
from concourse.bass_utils import run_bass_kernel_spmd

F32 = mybir.dt.float32
BF16 = mybir.dt.bfloat16
I32 = mybir.dt.int32
AF = mybir.ActivationFunctionType
ALU = mybir.AluOpType

D = 1024
KT = 8
FF = 2816
FT = 22
INC = 4112
EPS = 1e-6
NCORES = 8
NO_COLL = False
PREFIX_TILES = 24
WARM_COLL = False
TWO_PI = 2.0 * math.pi


class Buf:
    __slots__ = ("name", "last_w", "readers", "sem", "dcount")
    epoch_op = None

    def __init__(self, name):
        self.name = name
        self.last_w = Buf.epoch_op
        self.readers = []
        self.sem = None
        self.dcount = 0


class Op:
    __slots__ = ("eng", "fn", "deps", "is_dma", "needs_inc", "val", "sem", "inc")

    def __init__(self, eng, fn, is_dma):
        self.eng = eng
        self.fn = fn
        self.deps = {}
        self.is_dma = is_dma
        self.needs_inc = False
        self.val = None
        self.sem = None
        self.inc = 16


class Prog:
    ENGS = ("pe", "act", "dve", "pool", "sp")

    def __init__(self, nc):
        self.nc = nc
        self.ops = {e: [] for e in self.ENGS}
        self.all_ops = []
        self.dma_bufs = []

    def _dep(self, op, reads, writes):
        for r in reads:
            if r.last_w is not None:
                op.deps[r.last_w] = True
        for w in writes:
            if w.last_w is not None and w.last_w not in op.deps:
                op.deps[w.last_w] = False
            for rd in w.readers:
                if rd not in op.deps:
                    op.deps[rd] = False
        for r in reads:
            if not op.is_dma:
                r.readers = [x for x in r.readers if x.is_dma or x.eng != op.eng]
            r.readers.append(op)
        for w in writes:
            w.last_w = op
            w.readers = []

    def op(self, eng, fn, reads=(), writes=()):
        o = Op(eng, fn, False)
        self._dep(o, reads, writes)
        self.ops[eng].append(o)
        self.all_ops.append(o)
        return o

    def dma(self, out, in_, reads=(), writes=(), sbuf=None, eng="sp", slow=False):
        if slow:
            fn = lambda e: e.dma_start(out=out, in_=in_, allow_slow_non_contiguous=True)
        else:
            fn = lambda e: e.dma_start(out=out, in_=in_)
        o = Op(eng, fn, True)
        self._dep(o, reads, writes)
        if sbuf not in self.dma_bufs:
            self.dma_bufs.append(sbuf)
        sbuf.dcount += 16
        o.sem = sbuf
        o.val = sbuf.dcount
        self.ops[eng].append(o)
        self.all_ops.append(o)
        return o

    def coll(self, kind, src, dst, groups, reads, writes, sbuf, inc=1):
        fn = lambda e: e.collective_compute(kind, ALU.bypass, replica_groups=groups, ins=[src], outs=[dst])
        o = Op("pool", fn, True)
        self._dep(o, reads, writes)
        if sbuf not in self.dma_bufs:
            self.dma_bufs.append(sbuf)
        sbuf.dcount += inc
        o.sem = sbuf
        o.val = sbuf.dcount
        o.inc = inc
        self.ops["pool"].append(o)
        self.all_ops.append(o)
        return o

    def mm(self, out, lhsT, rhs, start, stop, reads, writes):
        return self.op("pe", lambda e: e.matmul(out, lhsT=lhsT, rhs=rhs, start=start, stop=stop),
                       reads, writes)

    def tr(self, out, in_, ident, reads, writes):
        return self.op("pe", lambda e: e.transpose(out=out, in_=in_, identity=ident), reads, writes)

    def act(self, out, in_, func, reads, writes, bias=None, scale=None, accum=None):
        kw = {}
        if bias is not None:
            kw["bias"] = bias
        if scale is not None:
            kw["scale"] = scale
        if accum is not None:
            kw["accum_out"] = accum
        return self.op("act", lambda e: e.activation(out=out, in_=in_, func=func, **kw), reads, writes)

    def tt(self, eng, out, in0, in1, op, reads, writes):
        return self.op(eng, lambda e: e.tensor_tensor(out=out, in0=in0, in1=in1, op=op), reads, writes)

    def ts(self, eng, out, in0, s1, op0, reads, writes, s2=None, op1=None):
        if op1 is None:
            return self.op(eng, lambda e: e.tensor_scalar(out=out, in0=in0, scalar1=s1, scalar2=None, op0=op0),
                           reads, writes)
        return self.op(eng, lambda e: e.tensor_scalar(out=out, in0=in0, scalar1=s1, scalar2=s2, op0=op0, op1=op1),
                       reads, writes)

    def stt(self, eng, out, in0, scalar, in1, op0, op1, reads, writes):
        return self.op(eng, lambda e: e.scalar_tensor_tensor(out=out, in0=in0, scalar=scalar, in1=in1,
                                                             op0=op0, op1=op1), reads, writes)

    def cp(self, eng, out, in_, reads, writes):
        if eng == "act":
            return self.act(out, in_, AF.Copy, reads, writes)
        return self.op(eng, lambda e: e.tensor_copy(out=out, in_=in_), reads, writes)

    def memset(self, eng, ap, val, writes):
        return self.op(eng, lambda e: e.memset(ap, val), (), writes)

    def fence(self, ap, fbuf, bufs):
        o = self.op("dve", lambda e: e.memset(ap, 0.0), (), list(bufs) + [fbuf])
        Buf.epoch_op = o
        return o

    def recip(self, out, in_, reads, writes):
        return self.op("dve", lambda e: e.reciprocal(out=out, in_=in_), reads, writes)

    @staticmethod
    def _need(o, d, raw):
        if d.is_dma:
            if o.is_dma and not raw:
                return False
            return True
        if d.eng == o.eng:
            if d.eng in ("pe", "pool"):
                return False
            return raw
        return True

    def emit(self, stack):
        nc = self.nc
        for o in self.all_ops:
            for d, raw in o.deps.items():
                if self._need(o, d, raw) and not d.is_dma:
                    d.needs_inc = True
        esem = {}
        for e in self.ENGS:
            esem[e] = stack.enter_context(nc.semaphore("s_" + e))
            c = 0
            for o in self.ops[e]:
                if not o.is_dma:
                    if o.needs_inc:
                        c += 1
                        o.val = c
                    o.sem = e
        for b in self.dma_bufs:
            b.sem = stack.enter_context(nc.semaphore("d_" + b.name))
        block = stack.enter_context(nc.Block())

        def replay(ename, eng):
            waited = {}
            for o in self.ops[ename]:
                for d, raw in o.deps.items():
                    if not self._need(o, d, raw):
                        continue
                    if d.is_dma:
                        key, sem = id(d.sem), d.sem.sem
                    else:
                        key, sem = d.sem, esem[d.sem]
                    if waited.get(key, 0) >= d.val:
                        continue
                    waited[key] = d.val
                    eng.wait_ge(sem, d.val)
                ins = o.fn(eng)
                if o.is_dma:
                    ins.then_inc(o.sem.sem, o.inc)
                elif o.needs_inc:
                    ins.then_inc(esem[ename], 1)
            if ename == "sp":
                for b in self.dma_bufs:
                    if waited.get(id(b), 0) < b.dcount:
                        eng.wait_ge(b.sem, b.dcount)

        @block.tensor
        def _(eng):
            replay("pe", eng)

        @block.scalar
        def _(eng):
            replay("act", eng)

        @block.vector
        def _(eng):
            replay("dve", eng)

        @block.gpsimd
        def _(eng):
            replay("pool", eng)

        @block.sync
        def _(eng):
            replay("sp", eng)


W_SPECS = [
    ("w_ffn1_gate", 1024, FF, "g_ffn1"), ("w_ffn1_up", 1024, FF, "g_ffn1"), ("w_ffn1_down", FF, 1024, None),
    ("w_in", 1024, INC, "g_mix"), ("w_gla_out", 512, 1024, "g_gla_head"), ("w_glu_a", 512, 512, None),
    ("w_glu_b", 512, 512, None), ("w_s5_out", 512, 1024, None), ("w_out", 1024, 1024, None),
    ("w_ffn2_gate", 1024, FF, "g_ffn2"), ("w_ffn2_up", 1024, FF, "g_ffn2"), ("w_ffn2_down", FF, 1024, None),
]
GAINS = [("g_ffn1", 8), ("g_mix", 8), ("g_ffn2", 8), ("g_gla_head", 4)]


def build(ntiles, balanced=True):
    Buf.epoch_op = None
    nc = bass.Bass("TRN2", target_bir_lowering=False)
    NPRE = PREFIX_TILES if balanced == "prefix" else 0
    NP = ntiles * 512
    NPIN = (ntiles + NPRE) * 512

    def din(name, shape, dt=F32):
        return nc.dram_tensor(name, list(shape), dt, kind="ExternalInput").ap()

    def dout(name, shape):
        return nc.dram_tensor(name, list(shape), F32, kind="ExternalOutput").ap()

    xp = din("xp", [NPIN, D])
    xs = din("xs", [128, D])
    sgla = din("sgla", [2, 128, 2, 128])
    ss5 = din("ss5", [64, 2, 32, 2])
    wd = {}
    ws = {}
    for name, K, N, _g in W_SPECS:
        wd[name] = din(name, [128, K // 128, N])
        ws[name] = nc.dram_tensor("scr_" + name, [128, K // 128, N], BF16).ap()
    gd = {name: din(name, [128, n]) for name, n in GAINS}
    g_final_d = din("g_final", [128, D])
    wgu_d = din("w_gate_up", [16, 256])
    bgate_d = din("b_gate", [1, 256])
    lamre_d = din("lam_re", [64, 32])
    lamim_d = din("lam_im", [64, 32])
    logdt_d = din("log_dt", [64, 32])
    bre_d = din("b_re", [64, 32, 16])
    bim_d = din("b_im", [64, 32, 16])
    cre_d = din("c_re", [64, 32, 16])
    cim_d = din("c_im", [64, 32, 16])
    dcol_d = din("dcol", [128, 32])
    ident_d = din("ident", [128, 128])
    triinc_d = din("tri_inc", [128, 128])
    trirev_d = din("tri_rev", [128, 128])
    cmask_d = din("cmask", [128, 128])
    s5mask_d = din("s5mask", [128, 128])
    flags_d = din("flags", [128, 8])
    hscr = nc.dram_tensor("hscr", [max(NP, 128) if balanced is True else 128, D], F32).ap()
    exsrc = nc.dram_tensor("exsrc", [128, 384], F32).ap()
    exdst = nc.dram_tensor("exdst", [8 * 128, 384], F32).ap()

    yp = dout("yp", [NP, D])
    ys = dout("ys", [128, D])
    glap_o = dout("glap", [128, 2, 128])
    s5p_o = dout("s5p", [64, 2, 32])
    glas_o = dout("glas", [2, 128, 2, 128])
    s5s_o = dout("s5s", [2, 64, 2, 32])

    with ExitStack() as st:
        P = Prog(nc)
        cnt = [0]

        def sb(shape, dt=F32, name=None):
            cnt[0] += 1
            nm = (name or "t") + "_%d" % cnt[0]
            return st.enter_context(nc.sbuf_tensor(nm, list(shape), dt)), Buf(nm)

        banks = []
        for i in range(6):
            t = st.enter_context(nc.psum_tensor("pb%d" % i, [128, 512], F32))
            banks.append((t, Buf("pb%d" % i)))
        bbanks = []
        for i in range(2):
            t = st.enter_context(nc.psum_tensor("pbb%d" % i, [128, 1024], BF16))
            bbanks.append((t, Buf("pbb%d" % i)))
        bk = [0]

        def nb():
            bk[0] = (bk[0] + 1) % 6
            return banks[bk[0]]

        bbk = [0]

        def nbb():
            bbk[0] = (bbk[0] + 1) % 2
            return bbanks[bbk[0]]

        identf, b_identf = sb([128, 128], F32, "identf")
        identb, b_identb = sb([128, 128], BF16, "identb")
        triinc, b_triinc = sb([128, 128], F32, "triinc")
        trirev, b_trirev = sb([128, 128], F32, "trirev")
        cmask, b_cmask = sb([128, 128], F32, "cmask")
        s5mask, b_s5mask = sb([128, 128], F32, "s5mask")
        gfin, b_gfin = sb([128, D], F32, "gfin")
        wgu, b_wgu = sb([16, 256], F32, "wgu")
        bgate, b_bgate = sb([1, 256], F32, "bgate")
        ones1, b_ones1 = sb([1, 128], F32, "ones1")
        dcol, b_dcol = sb([128, 32], F32, "dcol")
        T0, b_T0 = sb([128, 32, 128], BF16, "T0")
        Wm, b_Wm = sb([128, 32, 2, 64], BF16, "Wm")
        Vm, b_Vm = sb([64, 32, 2, 128], BF16, "Vm")
        A1, b_A1 = sb([64, 2, 32], F32, "A1")
        A2, b_A2 = sb([64, 2, 32], F32, "A2")
        AB1, b_AB1 = sb([64, 2, 32], F32, "AB1")
        AB2, b_AB2 = sb([64, 2, 32], F32, "AB2")
        flags, b_flags = sb([128, 8], F32, "flags")
        Dtot, b_Dtot = sb([128, 2], F32, "Dtot")
        exs, b_exs = sb([128, 384 if balanced is True else 1], F32, "exs")
        exg, b_exg = sb([128, 384 if balanced is True else 1], F32, "exg")
        gcols = {}
        for name, n in GAINS:
            gcols[name] = sb([128, n], F32, name)

        for t_, b_, d_ in [(identf, b_identf, ident_d), (triinc, b_triinc, triinc_d), (trirev, b_trirev, trirev_d),
                           (cmask, b_cmask, cmask_d), (s5mask, b_s5mask, s5mask_d), (gfin, b_gfin, g_final_d),
                           (wgu, b_wgu, wgu_d), (bgate, b_bgate, bgate_d), (dcol, b_dcol, dcol_d), (flags, b_flags, flags_d)]:
            P.dma(t_[:], d_, writes=[b_], sbuf=b_)
        for name, n in GAINS:
            P.dma(gcols[name][0][:], gd[name], writes=[gcols[name][1]], sbuf=gcols[name][1])
        if balanced is True and not NO_COLL and WARM_COLL:
            wsrc = nc.dram_tensor("wsrc", [128, 64], F32).ap()
            wdst = nc.dram_tensor("wdst", [8 * 128, 64], F32).ap()
            b_wsrc, b_wdst, b_wcc = Buf("wsrc"), Buf("wdst"), Buf("wcc")
            P.dma(wsrc, ident_d[:, 0:64], writes=[b_wsrc], sbuf=b_identf)
            P.coll("AllGather", wsrc, wdst, [list(range(NCORES))], reads=[b_wsrc], writes=[b_wdst], sbuf=b_wcc)
        P.cp("dve", identb[:], identf[:], [b_identf], [b_identb])
        P.memset("dve", ones1[:], 1.0, [b_ones1])

        prep_bufs = []
        with ExitStack() as pst:
            def psb(shape, dt=F32, name=None):
                cnt[0] += 1
                nm = (name or "p") + "_%d" % cnt[0]
                b = Buf(nm)
                prep_bufs.append(b)
                return pst.enter_context(nc.sbuf_tensor(nm, list(shape), dt)), b

            CH = 2816
            stg = [psb([128, CH], F32, "stg") for _ in range(7)]
            stb = [psb([128, CH], BF16, "stb") for _ in range(7)]
            ci = 0
            cast_engs = ["act", "dve", "act", "dve", "act", "pool"]
            scr_bufs = {name: Buf("scr_" + name) for name, _, _, _ in W_SPECS}
            for name, K, N, gk in W_SPECS:
                for kt in range(K // 128):
                    for c0 in range(0, N, CH):
                        cw = min(CH, N - c0)
                        s_t, s_b = stg[ci % 7]
                        o_t, o_b = stb[ci % 7]
                        P.dma(s_t[:, 0:cw], wd[name][:, kt, c0:c0 + cw], writes=[s_b], sbuf=s_b)
                        eng = cast_engs[ci % 6]
                        if gk is None:
                            P.cp(eng, o_t[:, 0:cw], s_t[:, 0:cw], [s_b], [o_b])
                        else:
                            gt, gb = gcols[gk]
                            if eng == "act":
                                P.act(o_t[:, 0:cw], s_t[:, 0:cw], AF.Copy, [s_b, gb], [o_b], scale=gt[:, kt:kt + 1])
                            else:
                                P.ts(eng, o_t[:, 0:cw], s_t[:, 0:cw], gt[:, kt:kt + 1], ALU.mult, [s_b, gb], [o_b])
                        P.dma(ws[name][:, kt, c0:c0 + cw], o_t[:, 0:cw], reads=[o_b], writes=[scr_bufs[name]], sbuf=o_b, eng="act")
                        ci += 1

        fence_t, fence_b = sb([128, 1], F32, "fence")
        P.fence(fence_t[:], fence_b, prep_bufs + list(scr_bufs.values()))
        prep_bufs = []
        with ExitStack() as pst:
            def psb(shape, dt=F32, name=None):
                cnt[0] += 1
                nm = (name or "p") + "_%d" % cnt[0]
                b = Buf(nm)
                prep_bufs.append(b)
                return pst.enter_context(nc.sbuf_tensor(nm, list(shape), dt)), b

            def small(shape, name):
                return psb(shape, F32, name)

            lr, b_lr = small([64, 32], "lr")
            li, b_li = small([64, 32], "li")
            ldt, b_ldt = small([64, 32], "ldt")
            P.dma(lr[:], lamre_d, writes=[b_lr], sbuf=b_lr)
            P.dma(li[:], lamim_d, writes=[b_li], sbuf=b_li)
            P.dma(ldt[:], logdt_d, writes=[b_ldt], sbuf=b_ldt)
            Bre, b_Bre = small([64, 32, 16], "Bre")
            Bim, b_Bim = small([64, 32, 16], "Bim")
            Cre, b_Cre = small([64, 32, 16], "Cre")
            Cim, b_Cim = small([64, 32, 16], "Cim")
            for t_, b_, d_ in [(Bre, b_Bre, bre_d), (Bim, b_Bim, bim_d), (Cre, b_Cre, cre_d), (Cim, b_Cim, cim_d)]:
                P.dma(t_[:], d_, writes=[b_], sbuf=b_)
            dt_, b_dt = small([64, 32], "dt")
            P.act(dt_[:], ldt[:], AF.Exp, [b_ldt], [b_dt])
            aa, b_aa = small([64, 32], "aa")
            th, b_th = small([64, 32], "th")
            P.tt("dve", aa[:], lr[:], dt_[:], ALU.mult, [b_lr, b_dt], [b_aa])
            P.tt("dve", th[:], li[:], dt_[:], ALU.mult, [b_li, b_dt], [b_th])
            mag, b_mag = small([64, 32], "mag")
            P.act(mag[:], aa[:], AF.Exp, [b_aa], [b_mag])
            ki, b_ki = psb([64, 32], I32, "ki")
            kf, b_kf = small([64, 32], "kf")
            P.ts("dve", ki[:], th[:], 1.0 / TWO_PI, ALU.mult, [b_th], [b_ki])
            P.cp("dve", kf[:], ki[:], [b_ki], [b_kf])
            C1 = 6.28125
            C2 = TWO_PI - C1
            thr, b_thr = small([64, 32], "thr")
            P.stt("dve", thr[:], kf[:], -C1, th[:], ALU.mult, ALU.add, [b_kf, b_th], [b_thr])
            P.stt("dve", thr[:], kf[:], -C2, thr[:], ALU.mult, ALU.add, [b_kf, b_thr], [b_thr])
            sn, b_sn = small([64, 32], "sn")
            cs, b_cs = small([64, 32], "cs")
            ab, b_ab = small([64, 32], "ab")
            P.act(sn[:], thr[:], AF.Sin, [b_thr], [b_sn])
            P.act(ab[:], thr[:], AF.Abs, [b_thr], [b_ab])
            halfpi, b_halfpi = small([64, 1], "halfpi")
            P.memset("dve", halfpi[:], math.pi / 2.0, [b_halfpi])
            P.act(cs[:], ab[:], AF.Sin, [b_ab, b_halfpi], [b_cs], bias=halfpi[:], scale=-1.0)
            LP, b_LP = small([64, 2, 9, 32], "LP")
            P.memset("dve", LP[:, 0, 0, :], 1.0, [b_LP])
            P.memset("dve", LP[:, 1, 0, :], 0.0, [b_LP])
            P.tt("dve", LP[:, 0, 1, :], mag[:], cs[:], ALU.mult, [b_mag, b_cs], [b_LP])
            P.tt("dve", LP[:, 1, 1, :], mag[:], sn[:], ALU.mult, [b_mag, b_sn], [b_LP])
            tA, b_tA = small([64, 32], "tA")
            tB, b_tB = small([64, 32], "tB")

            def cmul(o_re, o_im, a_re, a_im, c_re, c_im, rd, wr, shape_t=None):
                t1, bt1 = shape_t[0]
                t2, bt2 = shape_t[1]
                P.tt("dve", t1, a_re, c_re, ALU.mult, rd, [bt1])
                P.tt("dve", t2, a_im, c_im, ALU.mult, rd, [bt2])
                P.tt("dve", o_re, t1, t2, ALU.subtract, [bt1, bt2], wr)
                P.tt("dve", t1, a_re, c_im, ALU.mult, rd, [bt1])
                P.tt("dve", t2, a_im, c_re, ALU.mult, rd, [bt2])
                P.tt("dve", o_im, t1, t2, ALU.add, [bt1, bt2], wr)

            for tau in range(2, 9):
                cmul(LP[:, 0, tau, :], LP[:, 1, tau, :], LP[:, 0, tau - 1, :], LP[:, 1, tau - 1, :],
                     LP[:, 0, 1, :], LP[:, 1, 1, :], [b_LP], [b_LP], [(tA[:], b_tA), (tB[:], b_tB)])
            P.cp("dve", A1[:, 0, :], LP[:, 0, 8, :], [b_LP], [b_A1])
            P.cp("dve", A1[:, 1, :], LP[:, 0, 8, :], [b_LP], [b_A1])
            P.ts("dve", A2[:, 0, :], LP[:, 1, 8, :], -1.0, ALU.mult, [b_LP], [b_A2])
            P.cp("dve", A2[:, 1, :], LP[:, 1, 8, :], [b_LP], [b_A2])
            nsq = int(round(math.log2(max(ntiles, 1) * 64)))
            assert 2 ** nsq == max(ntiles, 1) * 64
            LB, b_LB = small([64, 2, 32], "LB")
            P.cp("dve", LB[:, 0, :], LP[:, 0, 8, :], [b_LP], [b_LB])
            P.cp("dve", LB[:, 1, :], LP[:, 1, 8, :], [b_LP], [b_LB])
            for _ in range(nsq):
                P.tt("dve", tA[:], LB[:, 0, :], LB[:, 0, :], ALU.mult, [b_LB], [b_tA])
                P.tt("dve", tB[:], LB[:, 1, :], LB[:, 1, :], ALU.mult, [b_LB], [b_tB])
                P.stt("dve", LB[:, 1, :], LB[:, 0, :], 2.0, LB[:, 1, :], ALU.mult, ALU.mult, [b_LB], [b_LB])
                P.tt("dve", LB[:, 0, :], tA[:], tB[:], ALU.subtract, [b_tA, b_tB], [b_LB])
            P.cp("dve", AB1[:, 0, :], LB[:, 0, :], [b_LB], [b_AB1])
            P.cp("dve", AB1[:, 1, :], LB[:, 0, :], [b_LB], [b_AB1])
            P.ts("dve", AB2[:, 0, :], LB[:, 1, :], -1.0, ALU.mult, [b_LB], [b_AB2])
            P.cp("dve", AB2[:, 1, :], LB[:, 1, :], [b_LB], [b_AB2])
            inv, b_inv = small([64, 2, 32], "inv")
            den, b_den = small([64, 32], "den")
            P.tt("dve", tA[:], LP[:, 0, 8, :], LP[:, 0, 8, :], ALU.mult, [b_LP], [b_tA])
            P.tt("dve", tB[:], LP[:, 1, 8, :], LP[:, 1, 8, :], ALU.mult, [b_LP], [b_tB])
            P.tt("dve", den[:], tA[:], tB[:], ALU.add, [b_tA, b_tB], [b_den])
            P.recip(den[:], den[:], [b_den], [b_den])
            P.tt("dve", inv[:, 0, :], LP[:, 0, 8, :], den[:], ALU.mult, [b_LP, b_den], [b_inv])
            P.stt("dve", inv[:, 1, :], LP[:, 1, 8, :], -1.0, den[:], ALU.mult, ALU.mult, [b_LP, b_den], [b_inv])
            fre, b_fre = small([64, 32], "fre")
            fim, b_fim = small([64, 32], "fim")
            nr, b_nr = small([64, 32], "nr")
            P.ts("dve", nr[:], LP[:, 0, 1, :], -1.0, ALU.add, [b_LP], [b_nr])
            P.tt("dve", tA[:], lr[:], lr[:], ALU.mult, [b_lr], [b_tA])
            P.tt("dve", tB[:], li[:], li[:], ALU.mult, [b_li], [b_tB])
            P.tt("dve", den[:], tA[:], tB[:], ALU.add, [b_tA, b_tB], [b_den])
            P.recip(den[:], den[:], [b_den], [b_den])
            P.tt("dve", tA[:], nr[:], lr[:], ALU.mult, [b_nr, b_lr], [b_tA])
            P.tt("dve", tB[:], LP[:, 1, 1, :], li[:], ALU.mult, [b_LP, b_li], [b_tB])
            P.tt("dve", fre[:], tA[:], tB[:], ALU.add, [b_tA, b_tB], [b_fre])
            P.tt("dve", fre[:], fre[:], den[:], ALU.mult, [b_fre, b_den], [b_fre])
            P.tt("dve", tA[:], LP[:, 1, 1, :], lr[:], ALU.mult, [b_LP, b_lr], [b_tA])
            P.tt("dve", tB[:], nr[:], li[:], ALU.mult, [b_nr, b_li], [b_tB])
            P.tt("dve", fim[:], tA[:], tB[:], ALU.subtract, [b_tA, b_tB], [b_fim])
            P.tt("dve", fim[:], fim[:], den[:], ALU.mult, [b_fim, b_den], [b_fim])

            def bc(ap2d):
                return ap2d.unsqueeze(2).to_broadcast([64, 32, 16])

            u1, b_u1 = small([64, 32, 16], "u1")
            u2, b_u2 = small([64, 32, 16], "u2")
            utmp = [(u1[:], b_u1), (u2[:], b_u2)]
            bbre, b_bbre = small([64, 32, 16], "bbre")
            bbim, b_bbim = small([64, 32, 16], "bbim")
            cmul(bbre[:], bbim[:], Bre[:], Bim[:], bc(fre[:]), bc(fim[:]), [b_Bre, b_Bim, b_fre, b_fim],
                 [b_bbre, b_bbim], utmp)
            BL, b_BL = small([64, 2, 32, 8, 16], "BL")
            for s in range(8):
                tau = 7 - s
                cmul(BL[:, 0, :, s, :], BL[:, 1, :, s, :], bbre[:], bbim[:], bc(LP[:, 0, tau, :]), bc(LP[:, 1, tau, :]),
                     [b_bbre, b_bbim, b_LP], [b_BL], utmp)
            CL, b_CL = small([64, 2, 32, 8, 16], "CL")
            for t in range(8):
                cmul(CL[:, 0, :, t, :], CL[:, 1, :, t, :], Cre[:], Cim[:], bc(LP[:, 0, t + 1, :]), bc(LP[:, 1, t + 1, :]),
                     [b_Cre, b_Cim, b_LP], [b_CL], utmp)
            P.cp("dve", Vm[:, :, 0, :].rearrange("p g (t j) -> p g t j", t=8), CL[:, 0, :, :, :], [b_CL], [b_Vm])
            P.ts("dve", Vm[:, :, 1, :].rearrange("p g (t j) -> p g t j", t=8), CL[:, 1, :, :, :], -1.0, ALU.mult,
                 [b_CL], [b_Vm])
            CLp, b_CLp = small([64, 2, 32, 8, 16], "CLp")
            v1, b_v1 = small([64, 32, 8, 16], "v1")
            v2, b_v2 = small([64, 32, 8, 16], "v2")

            def bc4(ap2d):
                return ap2d.unsqueeze(2).unsqueeze(3).to_broadcast([64, 32, 8, 16])

            cmul(CLp[:, 0], CLp[:, 1], CL[:, 0], CL[:, 1], bc4(inv[:, 0, :]), bc4(inv[:, 1, :]), [b_CL, b_inv],
                 [b_CLp], [(v1[:], b_v1), (v2[:], b_v2)])
            P.ts("dve", CLp[:, 1], CLp[:, 1], -1.0, ALU.mult, [b_CLp], [b_CLp])
            tmpT, b_tmpT = small([128, 128], "tmpT")
            for g in range(32):
                pt, pb = nb()
                P.mm(pt[:, 0:128], BL[:, 0, g].rearrange("p s h -> p (s h)"), CLp[:, 0, g].rearrange("p t j -> p (t j)"),
                     True, False, [b_BL, b_CLp], [pb])
                P.mm(pt[:, 0:128], BL[:, 1, g].rearrange("p s h -> p (s h)"), CLp[:, 1, g].rearrange("p t j -> p (t j)"),
                     False, True, [b_BL, b_CLp], [pb])
                P.tt("dve", tmpT[:], pt[:, 0:128], s5mask[:], ALU.mult, [pb, b_s5mask], [b_tmpT])
                P.stt("dve", T0[:, g, :], identf[:], dcol[:, g:g + 1], tmpT[:], ALU.mult, ALU.add,
                      [b_identf, b_dcol, b_tmpT], [b_T0])
                for slot in range(2):
                    pt2, pb2 = nb()
                    P.tr(pt2[:, 0:64], BL[:, slot, g].rearrange("p s h -> p (s h)"), identf[0:64, 0:64], [b_BL, b_identf], [pb2])
                    P.cp("act", Wm[:, g, slot, :], pt2[:, 0:64], [pb2], [b_Wm])
        P.fence(fence_t[:], fence_b, prep_bufs)
        main_bufs = []

        def msb(shape, dt=F32, name=None):
            t, b = sb(shape, dt, name)
            main_bufs.append(b)
            return t, b

        TM = 512
        xt, b_xt = msb([128, 4, D], F32, "xt")
        xn, b_xn = msb([128, D], BF16, "xn")
        xnT, b_xnT = msb([128, 8, TM], BF16, "xnT")
        actb, b_actb = msb([128, FT, TM], BF16, "actb")
        wblk = [msb([128, 4096], BF16, "wblk") for _ in range(2)]
        DN = 256
        wdn = [msb([128, FT, DN], BF16, "wdn") for _ in range(1)]
        sgt, b_sgt = msb([128, TM], F32, "sgt")
        junk, b_junk = msb([128, D], BF16, "junk")
        ssq, b_ssq = msb([128, 8], F32, "ssq")
        rstd, b_rstd = msb([128, 8], F32, "rstd")
        wga, b_wga = msb([128, 8, 16], BF16, "wga")
        gaT, b_gaT = msb([16, TM], F32, "gaT")
        Lb, b_Lb = msb([128, 1, 256], F32, "Lb")
        E1, b_E1 = msb([128, 2, TM], F32, "E1")
        E2, b_E2 = msb([128, 2, TM], F32, "E2")
        E3, b_E3 = msb([128, 1, 256], F32, "E3")
        qt, b_qt = msb([128, 2, TM], BF16, "qt")
        qa, b_qa = msb([128, 2, TM], BF16, "qa")
        qb, b_qb = msb([128, 2, TM], BF16, "qb")
        ktl, b_ktl = msb([128, 2, TM], BF16, "ktl")
        khat, b_khat = msb([128, 4, 256], BF16, "khat")
        vb, b_vb = msb([128, 4, 512], BF16, "vb")
        gatesA, b_gA = msb([128, 4, 1024], BF16, "gatesA")
        gatesB, b_gB = msb([128, 4, 1024], BF16, "gatesB")
        ATb, b_ATb = msb([128, 4, 128], BF16, "ATb")
        onb, b_onb = msb([128, 512], BF16, "onb")
        mf, b_mf = msb([128, 512], F32, "mf")
        mbf, b_mbf = xn, b_xn
        mixB, _bm = msb([128, 6144], BF16, "mixB")
        b_srb, b_g5T, b_glu = Buf("srb"), Buf("g5T"), Buf("glu")
        srb = mixB[:, 0:2048].rearrange("p (a b) -> p a b", a=4)
        g5T = mixB[:, 2048:4096].rearrange("p (a b) -> p a b", a=4)
        glu = mixB[:, 4096:6144].rearrange("p (a b) -> p a b", a=4)
        onT, b_onT = msb([128, 4, TM], BF16, "onT")
        Sf = [msb([128, 2, 128], F32, "Sf") for _ in range(2)]
        Sb = [msb([128, 2, 128], BF16, "Sb") for _ in range(2)]
        So = [msb([128, 2, 128], F32, "So") for _ in range(2)]
        dS, b_dS = msb([128, 2, 2], F32, "dS")
        actraw = actb[:].rearrange("p f t -> p (f t)")
        Uc = actraw[:, 0:4096].rearrange("p (g s h) -> p g s h", g=32, s=8)
        Gc = actraw[:, 0:4096].rearrange("p (t c) -> p t c", t=8)
        Ug = actraw[:, 4096:6144].rearrange("p (g c) -> p g c", g=32)
        Xbf = actraw[:, 6144:10240]
        Bst_full, b_Bst = msb([128, 2 * 32 * 65], F32, "Bst")
        Bst = Bst_full[0:64, :]
        ring4 = [wblk[0], wblk[1], (gatesA[:].rearrange("p a b -> p (a b)"), b_gA), (gatesB[:].rearrange("p a b -> p (a b)"), b_gB)]
        wdn.append((mixB[:, 0:FT * DN].rearrange("p (f n) -> p f n", f=FT), [b_srb, b_g5T, b_glu]))
        st1, b_st1 = msb([64, 2, 32], F32, "st1")
        st2, b_st2 = msb([64, 2, 32], F32, "st2")
        xo, b_xo = msb([64, 2, 32], F32, "xo")
        xin, b_xin = msb([64, 2, 32, 2], F32, "xin")
        xo2 = [msb([64, 2, 32], F32, "xo2") for _ in range(2)]

        P.memset("pool", qa[:], 0.0, [b_qa])
        P.memset("pool", qb[:], 0.0, [b_qb])

        def norm_T(T, src_t, src_b):
            NT = T // 128
            xn2 = mf[:].bitcast(BF16)
            for ts_ in range(NT):
                P.act(junk[:], src_t[:, ts_, :], AF.Square, [src_b], [b_junk, b_ssq], accum=ssq[:, ts_:ts_ + 1])
                P.act(rstd[:, ts_:ts_ + 1], ssq[:, ts_:ts_ + 1], AF.Sqrt, [b_ssq], [b_rstd], bias=EPS, scale=1.0 / D)
                P.recip(rstd[:, ts_:ts_ + 1], rstd[:, ts_:ts_ + 1], [b_rstd], [b_rstd])
                xa, xb_ = (xn[:], b_xn) if ts_ % 2 == 0 else (xn2, b_mf)
                P.ts("dve", xa, src_t[:, ts_, :], rstd[:, ts_:ts_ + 1], ALU.mult, [src_b, b_rstd], [xb_])
                bt, bb = nbb()
                for kt in range(8):
                    P.tr(bt[:, kt * 128:(kt + 1) * 128], xa[:, kt * 128:(kt + 1) * 128], identb[:], [xb_, b_identb], [bb])
                P.cp("dve" if ts_ % 2 else "act", xnT[:, :, ts_ * 128:(ts_ + 1) * 128],
                     bt[:].rearrange("p (k t) -> p k t", k=8), [bb], [b_xnT])

        wi = [0, 0]

        def load_blk(scr_name, view_fn, shape_fn, big=False):
            if big:
                t, b = ring4[wi[1] % 4]
                wi[1] += 1
            else:
                t, b = wblk[wi[0] % 2]
                wi[0] += 1
            v = shape_fn(t)
            P.dma(v, view_fn(ws[scr_name]), reads=[scr_bufs[scr_name]], writes=[b], sbuf=b)
            return v, b

        def ffn(T, h_t, h_b, wg, wu, wdn_name):
            NT = T // 128
            norm_T(T, h_t, h_b)
            gi = 0
            for c0 in range(0, FF, 512):
                cw = min(512, FF - c0)
                gv, gb = load_blk(wg, lambda a: a[:, :, c0:c0 + cw], lambda t: t[:, 0:8 * cw].rearrange("p (k n) -> p k n", k=8), big=True)
                uv, ub = load_blk(wu, lambda a: a[:, :, c0:c0 + cw], lambda t: t[:, 0:8 * cw].rearrange("p (k n) -> p k n", k=8), big=True)
                for f0 in range(0, cw, 128):
                    ft = (c0 + f0) // 128
                    pg, pgb = banks[(gi % 2) * 2]
                    pu, pub = banks[(gi % 2) * 2 + 1]
                    gi += 1
                    for kt in range(8):
                        P.mm(pg[:, 0:T], gv[:, kt, f0:f0 + 128], xnT[:, kt, 0:T], kt == 0, kt == 7, [gb, b_xnT], [pgb])
                    for kt in range(8):
                        P.mm(pu[:, 0:T], uv[:, kt, f0:f0 + 128], xnT[:, kt, 0:T], kt == 0, kt == 7, [ub, b_xnT], [pub])
                    P.act(sgt[:, 0:T], pg[:, 0:T], AF.Silu, [pgb], [b_sgt])
                    P.tt("dve", actb[:, ft, 0:T], sgt[:, 0:T], pu[:, 0:T], ALU.mult, [b_sgt, pub], [b_actb])
            for dh in range(2):
                for fh in range(2):
                    wt_, wb_ = wdn[(dh * 2 + fh) % 2]
                    wbl_ = wb_ if isinstance(wb_, list) else [wb_]
                    wv_ = (wt_[:] if not isinstance(wb_, list) else wt_).rearrange("p f n -> p (f n)")[:, 0:11 * 512].rearrange("p (f n) -> p f n", f=11)
                    P.dma(wv_, ws[wdn_name][:, fh * 11:(fh + 1) * 11, dh * 512:(dh + 1) * 512], reads=[scr_bufs[wdn_name]],
                          writes=wbl_, sbuf=wbl_[0])
                    for ts_ in range(NT):
                        po, pob = banks[ts_]
                        for f_ in range(11):
                            ft = fh * 11 + f_
                            P.mm(po[:, :], actb[:, ft, ts_ * 128:(ts_ + 1) * 128], wv_[:, f_, :], ft == 0, ft == FT - 1,
                                 [b_actb] + wbl_, [pob])
                for ts_ in range(NT):
                    po, pob = banks[ts_]
                    P.stt("dve", h_t[:, ts_, dh * 512:(dh + 1) * 512], po[:, :], 0.5, h_t[:, ts_, dh * 512:(dh + 1) * 512],
                          ALU.mult, ALU.add, [pob, h_b], [h_b])

        def tok_proj(T, scr_name, c0, cw, consume):
            NT = T // 128
            wv, wb_ = load_blk(scr_name, lambda a: a[:, :, c0:c0 + cw], lambda t: t[:, 0:8 * cw].rearrange("p (k n) -> p k n", k=8))
            for ts_ in range(NT):
                pt, pb = nb()
                for kt in range(8):
                    P.mm(pt[:, 0:cw], xnT[:, kt, ts_ * 128:(ts_ + 1) * 128], wv[:, kt, :], kt == 0, kt == 7, [b_xnT, wb_], [pb])
                consume(ts_, pt[:, 0:cw], pb)

        def mixer(T, h_t, h_b, Q, s5_init, gla_mode, gla_out, s5_out, so=False):
            NT = T // 128
            NC = T // 8
            NCQ = NC // Q
            norm_T(T, h_t, h_b)
            uv_, ub_ = load_blk("w_in", lambda a: a[:, :, 1552:2064], lambda t: t[:, 0:4096].rearrange("p (k n) -> p k n", k=8))
            for s_lo in range(8):
                pt, pb = nb()
                for kt in range(8):
                    P.mm(pt[0:NC, 0:512], xnT[:, kt, s_lo:T:8], uv_[:, kt, :], kt == 0, kt == 7, [b_xnT, ub_], [pb])
                P.cp("act" if s_lo % 2 else "dve", Uc[0:NC, :, s_lo, :], pt[0:NC, 0:512].rearrange("c (g h) -> c g h", g=32), [pb], [b_actb])
            for half in range(2):
                bt, bb = nbb()
                for gg in range(16):
                    g = half * 16 + gg
                    P.tr(bt[:, gg * NC:(gg + 1) * NC], Uc[0:NC, g].rearrange("c s h -> c (s h)"), identb[0:NC, 0:NC], [b_actb, b_identb], [bb])
                P.cp("dve", Ug[:, half * 16:(half + 1) * 16, 0:NC], bt[:, 0:16 * NC].rearrange("p (g c) -> p g c", g=16), [bb], [b_actb])
            Bv = Bst[:, 0:2 * 32 * Q * (NCQ + 1)].rearrange("p (s g q c) -> p s g q c", s=2, g=32, q=Q)
            for q in range(Q):
                P.cp("pool", Bv[:, :, :, q, 0], s5_init(q), [b_xo] if s5_init_reads is None else s5_init_reads, [b_Bst])
            for g0 in range(0, 32, 4):
                pt, pb = nb()
                pv = pt[0:64, 0:2 * 4 * NC].rearrange("p (s g c) -> p s g c", s=2, g=4)
                for gg in range(4):
                    for slot in range(2):
                        P.mm(pv[:, slot, gg, :], Wm[:, g0 + gg, slot, :], Ug[:, g0 + gg, 0:NC], True, True, [b_Wm, b_actb], [pb])
                for q in range(Q):
                    P.cp("dve" if (g0 // 4) % 2 else "act", Bv[:, :, g0:g0 + 4, q, 1:NCQ + 1], pv[:, :, :, q * NCQ:(q + 1) * NCQ], [pb], [b_Bst])
            for c in range(NCQ):
                for q in range(Q):
                    P.tt("pool", st1[:], A1[:], Bv[:, :, :, q, c], ALU.mult, [b_A1, b_Bst], [b_st1])
                    P.tt("pool", st2[:, 0, :], A2[:, 0, :], Bv[:, 1, :, q, c], ALU.mult, [b_A2, b_Bst], [b_st2])
                    P.tt("pool", st2[:, 1, :], A2[:, 1, :], Bv[:, 0, :, q, c], ALU.mult, [b_A2, b_Bst], [b_st2])
                    P.tt("pool", st1[:], st1[:], st2[:], ALU.add, [b_st1, b_st2], [b_st1])
                    P.tt("pool", Bv[:, :, :, q, c + 1], Bv[:, :, :, q, c + 1], st1[:], ALU.add, [b_Bst, b_st1], [b_Bst])
            for q in range(Q):
                s5_out(q, Bv[:, :, :, q, NCQ])
            P.dma(wga[:], ws["w_in"][:, :, 1536:1552], reads=[scr_bufs["w_in"]], writes=[b_wga], sbuf=b_wga)
            qkv, qkb = load_blk("w_in", lambda a: a[:, :, 0:512], lambda t: t[:, 0:4096].rearrange("p (k n) -> p k n", k=8))
            pt, pb = nb()
            for kt in range(8):
                P.mm(pt[0:16, 0:T], wga[:, kt, :], xnT[:, kt, 0:T], kt == 0, kt == 7, [b_wga, b_xnT], [pb])
            P.cp("dve", gaT[:, 0:T], pt[0:16, 0:T], [pb], [b_gaT])
            for ts_ in range(NT):
                pt, pb = nb()
                P.mm(pt[:, 0:256], gaT[:, ts_ * 128:(ts_ + 1) * 128], wgu[:], True, False, [b_gaT, b_wgu], [pb])
                P.mm(pt[:, 0:256], ones1[:], bgate[:], False, True, [b_ones1, b_bgate], [pb])
                P.act(mf[:, 0:256], pt[:, 0:256], AF.Exp, [pb], [b_mf], scale=-1.0)
                P.act(Lb[:, 0, :], mf[:, 0:256], AF.Ln, [b_mf], [b_Lb], bias=1.0)
                for pair in range(2):
                    pt2, pb2 = nb()
                    P.mm(pt2[:, 0:128], Lb[:, 0, pair * 128:(pair + 1) * 128], triinc[:], True, True, [b_Lb, b_triinc], [pb2])
                    P.act(E1[:, pair, ts_ * 128:(ts_ + 1) * 128], pt2[:, 0:128], AF.Exp, [pb2], [b_E1])
                    if not so:
                        P.act(E2[:, pair, ts_ * 128:(ts_ + 1) * 128], pt2[:, 0:128], AF.Exp, [pb2], [b_E2], scale=-1.0)
                pt3, pb3 = nb()
                P.mm(pt3[:, 0:256], trirev[:], Lb[:, 0, :], True, True, [b_trirev, b_Lb], [pb3])
                P.act(E3[:, 0, :], pt3[:, 0:256], AF.Exp, [pb3], [b_E3])
                pt4, pb4 = nb()
                for kt in range(8):
                    P.mm(pt4[:, 0:256], xnT[:, kt, ts_ * 128:(ts_ + 1) * 128], qkv[:, kt, 256:512], kt == 0, kt == 7, [b_xnT, qkb], [pb4])
                P.tt("dve", khat[:, ts_, :], pt4[:, 0:256], E3[:, 0, :], ALU.mult, [pb4, b_E3], [b_khat])
            for which in ([] if so else range(2)):
                for pair in range(2):
                    c0 = which * 256 + pair * 128
                    pt, pb = nb()
                    for kt in range(8):
                        P.mm(pt[:, 0:T], qkv[:, kt, c0:c0 + 128], xnT[:, kt, 0:T], kt == 0, kt == 7, [qkb, b_xnT], [pb])
                    if which == 0:
                        P.stt("dve", qt[:, pair, 0:T], E1[:, pair, 0:T], 0.125, pt[:, 0:T], ALU.mult, ALU.mult, [b_E1, pb], [b_qt])
                        qv = qt[:, pair, 0:T].rearrange("p (n c t) -> p n c t", c=2, t=64)
                        P.cp("act", qa[:, pair, 0:T].rearrange("p (n c t) -> p n c t", c=2, t=64)[:, :, 0, :], qv[:, :, 0, :], [b_qt], [b_qa])
                        P.cp("act", qb[:, pair, 0:T].rearrange("p (n c t) -> p n c t", c=2, t=64)[:, :, 1, :], qv[:, :, 1, :], [b_qt], [b_qb])
                    else:
                        P.tt("dve", ktl[:, pair, 0:T], E2[:, pair, 0:T], pt[:, 0:T], ALU.mult, [b_E2, pb], [b_ktl])
            tok_proj(T, "w_in", 512, 512, lambda ts_, p, pb: P.cp("act", vb[:, ts_, :], p, [pb], [b_vb]))
            if not so:
                tok_proj(T, "w_in", 1024, 512, lambda ts_, p, pb: P.act(srb[:, ts_, :], p, AF.Silu, [pb], [b_srb]))
            for i in ([] if so else range(4)):
                tok_proj(T, "w_in", 2064 + i * 512, 512,
                         lambda ts_, p, pb, i=i: P.act((gatesA if i < 2 else gatesB)[:, ts_, (i % 2) * 512:(i % 2 + 1) * 512], p,
                                                       AF.Sigmoid, [pb], [b_gA if i < 2 else b_gB]))
            if so:
                for ts_ in range(NT):
                    for pair in range(2):
                        P.cp("act", dS[:, pair, :], E1[:, pair, ts_ * 128 + 63:ts_ * 128 + 128:64], [b_E1], [b_dS])
                    if balanced is True:
                        P.tt("dve", Dtot[:], Dtot[:], dS[:, :, 0], ALU.mult, [b_Dtot, b_dS], [b_Dtot])
                        P.tt("dve", Dtot[:], Dtot[:], dS[:, :, 1], ALU.mult, [b_Dtot, b_dS], [b_Dtot])
                    for c2 in range(2):
                        ps_ = slice(c2 * 64, c2 * 64 + 64)
                        src, dst = (Sf[0], Sf[1]) if c2 == 0 else (Sf[1], Sf[0])
                        for pair in range(2):
                            pt, pb = nb()
                            P.mm(pt[:, 0:256], khat[ps_, ts_, pair * 128:(pair + 1) * 128], vb[ps_, ts_, pair * 256:(pair + 1) * 256],
                                 True, True, [b_khat, b_vb], [pb])
                            for hp in range(2):
                                rs = slice(hp * 64, hp * 64 + 64)
                                P.stt("dve", dst[0][rs, pair, :], src[0][rs, pair, :], dS[rs, pair, c2:c2 + 1],
                                      pt[rs, hp * 128:(hp + 1) * 128], ALU.mult, ALU.add, [src[1], b_dS, pb], [dst[1]])
                return
            for ts_ in range(NT):
                tsl = slice(ts_ * 128, (ts_ + 1) * 128)
                if gla_mode == "chain":
                    s0f, s0b = Sf[0], Sb[0]
                    s1f, s1b = Sf[1], Sb[1]
                else:
                    s0f, s0b = Sf[0], Sb[0]
                    s1f, s1b = Sf[1], Sb[1]
                for pair in range(2):
                    P.cp("act", dS[:, pair, :], E1[:, pair, ts_ * 128 + 63:ts_ * 128 + 128:64], [b_E1], [b_dS])
                for h in range(4):
                    pair, hp = h // 2, h % 2
                    rs = slice(hp * 64, hp * 64 + 64)
                    pt, pb = nb()
                    P.mm(pt[:, 0:128], ktl[rs, pair, tsl], qt[rs, pair, tsl], True, True, [b_ktl, b_qt], [pb])
                    P.tt("dve", ATb[:, h, :], pt[:, 0:128], cmask[:], ALU.mult, [pb, b_cmask], [b_ATb])
                kv = []
                for c2 in range(2):
                    ps_ = slice(c2 * 64, c2 * 64 + 64)
                    row = []
                    for pair in range(2):
                        pt, pb = nb()
                        P.mm(pt[:, 0:256], khat[ps_, ts_, pair * 128:(pair + 1) * 128], vb[ps_, ts_, pair * 256:(pair + 1) * 256],
                             True, True, [b_khat, b_vb], [pb])
                        row.append((pt, pb))
                    kv.append(row)

                def upd(dst_t, dst_b, src_t, src_b, c2, bf_t=None, bf_b=None):
                    for pair in range(2):
                        pt, pb = kv[c2][pair]
                        for hp in range(2):
                            rs = slice(hp * 64, hp * 64 + 64)
                            P.stt("dve", dst_t[rs, pair, :], src_t[rs, pair, :], dS[rs, pair, c2:c2 + 1],
                                  pt[rs, hp * 128:(hp + 1) * 128], ALU.mult, ALU.add, [src_b, b_dS, pb], [dst_b])
                    if bf_t is not None:
                        P.cp("act", bf_t[:], dst_t[:], [dst_b], [bf_b])

                if gla_mode == "chain":
                    upd(s1f[0], s1f[1], s0f[0], s0f[1], 0, s1b[0], s1b[1])
                else:
                    upd(So[0][0], So[0][1], s0f[0], s0f[1], 0)
                    upd(So[1][0], So[1][1], s1f[0], s1f[1], 1)
                po, pob = nb()
                for h in range(4):
                    pair, hp = h // 2, h % 2
                    rs = slice(hp * 64, hp * 64 + 64)
                    o_ = po[:, h * 128:(h + 1) * 128]
                    P.mm(o_, ATb[:, h, :], vb[:, ts_, h * 128:(h + 1) * 128], True, False, [b_ATb, b_vb], [pob])
                    P.mm(o_, qa[rs, pair, tsl], s0b[0][rs, pair, :], False, False, [b_qa, s0b[1]], [pob])
                    P.mm(o_, qb[rs, pair, tsl], s1b[0][rs, pair, :], False, True, [b_qb, s1b[1]], [pob])
                if gla_mode == "chain":
                    upd(s0f[0], s0f[1], s1f[0], s1f[1], 1, s0b[0], s0b[1])
                for h in range(4):
                    P.act(junk[:, 0:128], po[:, h * 128:(h + 1) * 128], AF.Square, [pob], [b_junk, b_ssq], accum=ssq[:, 4 + h:5 + h])
                P.act(rstd[:, 4:8], ssq[:, 4:8], AF.Sqrt, [b_ssq], [b_rstd], bias=EPS, scale=1.0 / 128)
                P.recip(rstd[:, 4:8], rstd[:, 4:8], [b_rstd], [b_rstd])
                for h in range(4):
                    P.stt("dve", onb[:, h * 128:(h + 1) * 128], po[:, h * 128:(h + 1) * 128], rstd[:, 4 + h:5 + h],
                          srb[:, ts_, h * 128:(h + 1) * 128], ALU.mult, ALU.mult, [pob, b_rstd, b_srb], [b_onb])
                bt, bb = nbb()
                for ct in range(4):
                    P.tr(bt[:, ct * 128:(ct + 1) * 128], onb[:, ct * 128:(ct + 1) * 128], identb[:], [b_onb, b_identb], [bb])
                P.cp("act", onT[:, :, tsl], bt[:, 0:512].rearrange("p (k t) -> p k t", k=4), [bb], [b_onT])
            if gla_mode == "chain":
                pass
            else:
                for q in range(2):
                    gla_out(q, So[q])
            Xv = Xbf[0:64, 0:2 * 32 * NC].rearrange("p (s g c) -> p s g c", s=2, g=32)
            for q in ([] if so else range(Q)):
                P.cp("dve", Xv[:, :, :, q * NCQ:(q + 1) * NCQ], Bv[:, :, :, q, 0:NCQ], [b_Bst], [b_actb])
            for g0 in ([] if so else range(0, 32, 4)):
                pt, pb = nb()
                for gg in range(4):
                    g = g0 + gg
                    o_ = pt[0:NC, gg * 128:(gg + 1) * 128]
                    P.mm(o_, Ug[:, g, 0:NC], T0[:, g, :], True, False, [b_actb, b_T0], [pb])
                    P.mm(o_, Xv[:, 0, g, :], Vm[:, g, 0, :], False, False, [b_actb, b_Vm], [pb])
                    P.mm(o_, Xv[:, 1, g, :], Vm[:, g, 1, :], False, True, [b_actb, b_Vm], [pb])
                P.act(Gc[0:NC, :, g0 * 16:(g0 + 4) * 16].rearrange("c t (g j) -> c t g j", g=4),
                      pt[0:NC, 0:512].rearrange("c (g t j) -> c t g j", g=4, t=8), AF.Gelu, [pb], [b_actb])
            for th_ in ([] if so else range(2)):
                bt, bb = nbb()
                for tl in range(4):
                    for ct in range(4):
                        i = tl * 4 + ct
                        P.tr(bt[:, i * NC:(i + 1) * NC], Gc[0:NC, th_ * 4 + tl, ct * 128:(ct + 1) * 128], identb[0:NC, 0:NC],
                             [b_actb, b_identb], [bb])
                P.cp("dve", g5T[:, :, 0:T].rearrange("p k (c t) -> p t k c", t=8)[:, th_ * 4:(th_ + 1) * 4],
                     bt[:, 0:16 * NC].rearrange("p (t k c) -> p t k c", t=4, k=4), [bb], [b_g5T])
            wa_v, wa_b = load_blk("w_glu_a", lambda a: a, lambda t: t[:, 0:2048].rearrange("p (k n) -> p k n", k=4))
            wb_v, wb_b = load_blk("w_glu_b", lambda a: a, lambda t: t[:, 0:2048].rearrange("p (k n) -> p k n", k=4))
            for nt_ in range(4):
                pa, pab = nb()
                for ct in range(4):
                    P.mm(pa[:, 0:T], wa_v[:, ct, nt_ * 128:(nt_ + 1) * 128], g5T[:, ct, 0:T], ct == 0, ct == 3, [wa_b, b_g5T], [pab])
                pb_, pbb = nb()
                for ct in range(4):
                    P.mm(pb_[:, 0:T], wb_v[:, ct, nt_ * 128:(nt_ + 1) * 128], g5T[:, ct, 0:T], ct == 0, ct == 3, [wb_b, b_g5T], [pbb])
                P.act(sgt[:, 0:T], pb_[:, 0:T], AF.Sigmoid, [pbb], [b_sgt])
                P.tt("dve", glu[:, nt_, 0:T], sgt[:, 0:T], pa[:, 0:T], ALU.mult, [b_sgt, pab], [b_glu])
            wgo_v, wgo_b = load_blk("w_gla_out", lambda a: a, lambda t: t[:, 0:4096].rearrange("p (k n) -> p k n", k=4))
            wso_v, wso_b = load_blk("w_s5_out", lambda a: a, lambda t: t[:, 0:4096].rearrange("p (k n) -> p k n", k=4))
            for ts_ in range(NT):
                tsl = slice(ts_ * 128, (ts_ + 1) * 128)
                for half in range(2):
                    hs = slice(half * 512, (half + 1) * 512)
                    pg_, pgb_ = nb()
                    for ct in range(4):
                        P.mm(pg_[:, :], onT[:, ct, tsl], wgo_v[:, ct, hs], ct == 0, ct == 3, [b_onT, wgo_b], [pgb_])
                    ps2, psb2 = nb()
                    for ct in range(4):
                        P.mm(ps2[:, :], glu[:, ct, tsl], wso_v[:, ct, hs], ct == 0, ct == 3, [b_glu, wso_b], [psb2])
                    P.tt("dve", mf[:], gatesA[:, ts_, hs], pg_[:, :], ALU.mult, [b_gA, pgb_], [b_mf])
                    P.tt("dve", sgt[:, 0:512], gatesB[:, ts_, hs], ps2[:, :], ALU.mult,
                         [b_gB, psb2], [b_sgt])
                    P.tt("dve", mbf[:, hs], mf[:], sgt[:, 0:512], ALU.add, [b_mf, b_sgt], [b_mbf])
                bt, bb = nbb()
                for kt in range(8):
                    P.tr(bt[:, kt * 128:(kt + 1) * 128], mbf[:, kt * 128:(kt + 1) * 128], identb[:], [b_mbf, b_identb], [bb])
                P.cp("act", xnT[:, :, tsl], bt[:].rearrange("p (k t) -> p k t", k=8), [bb], [b_xnT])
            for half in range(2):
                hs = slice(half * 512, (half + 1) * 512)
                wo_v, wo_b = load_blk("w_out", lambda a: a[:, :, half * 512:(half + 1) * 512],
                                      lambda t: t[:, 0:4096].rearrange("p (k n) -> p k n", k=8))
                for ts_ in range(NT):
                    pt, pb = nb()
                    for kt in range(8):
                        P.mm(pt[:, :], xnT[:, kt, ts_ * 128:(ts_ + 1) * 128], wo_v[:, kt, :], kt == 0, kt == 7, [b_xnT, wo_b], [pb])
                    P.tt("dve", h_t[:, ts_, hs], h_t[:, ts_, hs], pt[:, :], ALU.add, [h_b, pb], [h_b])

        ystage = actb[:].rearrange("p f t -> p (f t)").bitcast(F32)[:, 0:4 * D].rearrange("p (n d) -> p n d", n=4)

        def final_norm(T, h_t, h_b):
            NT = T // 128
            for ts_ in range(NT):
                P.act(junk[:], h_t[:, ts_, :], AF.Square, [h_b], [b_junk, b_ssq], accum=ssq[:, ts_:ts_ + 1])
                P.act(rstd[:, ts_:ts_ + 1], ssq[:, ts_:ts_ + 1], AF.Sqrt, [b_ssq], [b_rstd], bias=EPS, scale=1.0 / D)
                P.recip(rstd[:, ts_:ts_ + 1], rstd[:, ts_:ts_ + 1], [b_rstd], [b_rstd])
                P.stt("dve", ystage[:, ts_, :], h_t[:, ts_, :], rstd[:, ts_:ts_ + 1], gfin[:], ALU.mult, ALU.mult,
                      [h_b, b_rstd, b_gfin], [b_actb])

        s5_init_reads = None
        def s5o(q, view):
            P.cp("pool", xo[:], view, [b_Bst], [b_xo])

        if ntiles > 0:
            P.memset("dve", Sf[0][0][:], 0.0, [Sf[0][1]])
            P.memset("dve", Sb[0][0][:], 0.0, [Sb[0][1]])
            P.memset("dve", xo[:], 0.0, [b_xo])
        hb = [Buf("hscr%d" % ti) for ti in range(ntiles)]
        if balanced is True and ntiles > 0:
            P.memset("dve", Dtot[:], 1.0, [b_Dtot])
            for ti in range(ntiles):
                P.dma(xt[:, :, :], xp[ti * 512:(ti + 1) * 512, :].rearrange("(n p) d -> p n d", p=128), writes=[b_xt], sbuf=b_xt)
                ffn(512, xt, b_xt, "w_ffn1_gate", "w_ffn1_up", "w_ffn1_down")
                P.dma(hscr[ti * 512:(ti + 1) * 512, :].rearrange("(n p) d -> p n d", p=128), xt[:, :, :], reads=[b_xt],
                      writes=[hb[ti]], sbuf=b_xt)
                mixer(512, xt, b_xt, 1, lambda q: xo[:], "chain", None, s5o, so=True)
            P.memset("dve", exs[:], 0.0, [b_exs])
            P.cp("dve", exs[:, 0:256], Sf[0][0][:].rearrange("p a e -> p (a e)"), [Sf[0][1]], [b_exs])
            P.cp("dve", exs[:, 256:258], Dtot[:], [b_Dtot], [b_exs])
            P.cp("dve", exs[0:64, 258:322], xo[:].rearrange("p s g -> p (s g)"), [b_xo], [b_exs])
            b_exsrc, b_exdst, b_cc = Buf("exsrc"), Buf("exdst"), Buf("cc")
            P.dma(exsrc, exs[:], reads=[b_exs], writes=[b_exsrc], sbuf=b_exs)
            if NO_COLL:
                P.dma(exdst[0:128, :], exsrc, reads=[b_exsrc], writes=[b_exdst], sbuf=b_cc)
            else:
                P.coll("AllGather", exsrc, exdst, [list(range(NCORES))], reads=[b_exsrc], writes=[b_exdst], sbuf=b_cc)
            accS, b_accS = So[0]
            tmpS, b_tmpS = So[1]
            accX, b_accX = xo2[0]
            tmpX, b_tmpX = xo2[1]
            P.memset("dve", accS[:], 0.0, [b_accS])
            P.memset("dve", accX[:], 0.0, [b_accX])
            for i in range(NCORES):
                P.dma(exg[:], exdst[i * 128:(i + 1) * 128, :], reads=[b_exdst], writes=[b_exg], sbuf=b_exg)
                fi = flags[:, i:i + 1]
                for pair in range(2):
                    P.stt("dve", tmpS[:, pair, :], accS[:, pair, :], exg[:, 256 + pair:257 + pair], exg[:, pair * 128:(pair + 1) * 128],
                          ALU.mult, ALU.add, [b_accS, b_exg], [b_tmpS])
                P.tt("dve", tmpS[:], tmpS[:], accS[:], ALU.subtract, [b_tmpS, b_accS], [b_tmpS])
                P.stt("dve", accS[:], tmpS[:], fi, accS[:], ALU.mult, ALU.add, [b_tmpS, b_flags, b_accS], [b_accS])
                xi = exg[0:64, 258:322].rearrange("p (s g) -> p s g", s=2)
                P.tt("dve", st1[:], AB1[:], accX[:], ALU.mult, [b_AB1, b_accX], [b_st1])
                P.tt("dve", st2[:, 0, :], AB2[:, 0, :], accX[:, 1, :], ALU.mult, [b_AB2, b_accX], [b_st2])
                P.tt("dve", st2[:, 1, :], AB2[:, 1, :], accX[:, 0, :], ALU.mult, [b_AB2, b_accX], [b_st2])
                P.tt("dve", st1[:], st1[:], st2[:], ALU.add, [b_st1, b_st2], [b_st1])
                P.tt("dve", st1[:], st1[:], xi, ALU.add, [b_st1, b_exg], [b_st1])
                P.tt("dve", tmpX[:], st1[:], accX[:], ALU.subtract, [b_st1, b_accX], [b_tmpX])
                P.stt("dve", accX[:], tmpX[:], fi[0:64, :], accX[:], ALU.mult, ALU.add, [b_tmpX, b_flags, b_accX], [b_accX])
            P.cp("dve", Sf[0][0][:], accS[:], [b_accS], [Sf[0][1]])
            P.cp("dve", Sb[0][0][:], accS[:], [b_accS], [Sb[0][1]])
            P.cp("dve", xo[:], accX[:], [b_accX], [b_xo])
        if balanced == "prefix":
            for ti in range(NPRE):
                P.dma(xt[:, :, :], xp[ti * 512:(ti + 1) * 512, :].rearrange("(n p) d -> p n d", p=128), writes=[b_xt], sbuf=b_xt)
                ffn(512, xt, b_xt, "w_ffn1_gate", "w_ffn1_up", "w_ffn1_down")
                mixer(512, xt, b_xt, 1, lambda q: xo[:], "chain", None, s5o, so=True)
            P.cp("dve", Sb[0][0][:], Sf[0][0][:], [Sf[0][1]], [Sb[0][1]])
        for ti in range(ntiles):
            if balanced is True:
                P.dma(xt[:, :, :], hscr[ti * 512:(ti + 1) * 512, :].rearrange("(n p) d -> p n d", p=128), reads=[hb[ti]],
                      writes=[b_xt], sbuf=b_xt)
            else:
                t0_ = (NPRE + ti) * 512
                P.dma(xt[:, :, :], xp[t0_:t0_ + 512, :].rearrange("(n p) d -> p n d", p=128), writes=[b_xt], sbuf=b_xt)
                ffn(512, xt, b_xt, "w_ffn1_gate", "w_ffn1_up", "w_ffn1_down")
            mixer(512, xt, b_xt, 1, lambda q: xo[:], "chain", None, s5o)
            ffn(512, xt, b_xt, "w_ffn2_gate", "w_ffn2_up", "w_ffn2_down")
            final_norm(512, xt, b_xt)
            P.dma(yp[ti * 512:(ti + 1) * 512, :].rearrange("(n p) d -> p n d", p=128), ystage[:, :, :], reads=[b_actb], sbuf=b_actb)
        if ntiles > 0:
            P.dma(glap_o, Sf[0][0][:], reads=[Sf[0][1]], sbuf=Sf[0][1])
            P.dma(s5p_o, xo[:], reads=[b_xo], sbuf=b_xo)
        P.dma(xt[:, 0, :], xs, writes=[b_xt], sbuf=b_xt)
        for q in range(2):
            P.dma(Sf[q][0][:], sgla[q], writes=[Sf[q][1]], sbuf=Sf[q][1])
            P.cp("pool", Sb[q][0][:], Sf[q][0][:], [Sf[q][1]], [Sb[q][1]])
        P.dma(xin[:], ss5, writes=[b_xin], sbuf=b_xin)
        s5_init_reads = [b_xin]
        ffn(128, xt, b_xt, "w_ffn1_gate", "w_ffn1_up", "w_ffn1_down")

        def s5o_s(q, view):
            P.cp("pool", xo2[q][0][:], view, [b_Bst], [xo2[q][1]])
            P.dma(s5s_o[q], xo2[q][0][:], reads=[xo2[q][1]], sbuf=xo2[q][1])

        def glao_s(q, so):
            P.dma(glas_o[q], so[0][:], reads=[so[1]], sbuf=so[1])

        mixer(128, xt, b_xt, 2, lambda q: xin[:, :, :, q], "indep", glao_s, s5o_s)
        ffn(128, xt, b_xt, "w_ffn2_gate", "w_ffn2_up", "w_ffn2_down")
        final_norm(128, xt, b_xt)
        P.dma(ys, ystage[:, 0, :], reads=[b_actb], sbuf=b_actb)
        P.emit(st)
    return nc


def _consts():
    idx = np.arange(128)
    same = (idx[:, None] // 64) == (idx[None, :] // 64)
    tri_inc = np.where(same & (idx[:, None] <= idx[None, :]), -1.0 / 16.0, 0.0).astype(np.float32)
    tri_rev = np.where(same & (idx[:, None] > idx[None, :]), -1.0 / 16.0, 0.0).astype(np.float32)
    cmask = np.where(same & (idx[:, None] <= idx[None, :]), 1.0, 0.0).astype(np.float32)
    s5mask = np.where((idx[None, :] // 16) >= (idx[:, None] // 16), 1.0, 0.0).astype(np.float32)
    return dict(ident=np.eye(128, dtype=np.float32), tri_inc=tri_inc, tri_rev=tri_rev, cmask=cmask, s5mask=s5mask)


def _lay_w(w):
    K, N = w.shape
    return np.ascontiguousarray(w.reshape(K // 128, 128, N).transpose(1, 0, 2))


def make_in_maps(inp, ntiles, seq_of_core, seg_of_core=None, prefix_tiles=0):
    if seg_of_core is None:
        seg_of_core = [0] * NCORES
    c = _consts()
    shared = dict(c)
    for name, K, N, _g in W_SPECS:
        shared[name] = _lay_w(np.asarray(inp[name], dtype=np.float32))
    for name, n in GAINS:
        shared[name] = np.ascontiguousarray(np.asarray(inp[name], np.float32).reshape(n, 128).T)
    shared["g_final"] = np.ascontiguousarray(np.broadcast_to(np.asarray(inp["g_final"], np.float32)[None, :], (128, D)))
    shared["w_gate_up"] = np.ascontiguousarray(inp["w_gate_up"], dtype=np.float32)
    shared["b_gate"] = np.ascontiguousarray(np.asarray(inp["b_gate"], np.float32)[None, :])
    shared["lam_re"] = np.ascontiguousarray(np.asarray(inp["s5_lam_re"], np.float32).T)
    shared["lam_im"] = np.ascontiguousarray(np.asarray(inp["s5_lam_im"], np.float32).T)
    shared["log_dt"] = np.ascontiguousarray(np.broadcast_to(np.asarray(inp["s5_log_dt"], np.float32)[None, :], (64, 32)))
    shared["b_re"] = np.ascontiguousarray(np.asarray(inp["s5_b_re"], np.float32).transpose(1, 0, 2))
    shared["b_im"] = np.ascontiguousarray(np.asarray(inp["s5_b_im"], np.float32).transpose(1, 0, 2))
    shared["c_re"] = np.ascontiguousarray(np.asarray(inp["s5_c_re"], np.float32).transpose(2, 0, 1))
    shared["c_im"] = np.ascontiguousarray(np.asarray(inp["s5_c_im"], np.float32).transpose(2, 0, 1))
    d = np.asarray(inp["s5_d"], np.float32).reshape(32, 16).T
    shared["dcol"] = np.ascontiguousarray(np.tile(d, (8, 1)))
    xp = np.asarray(inp["x_prompt"], np.float32)
    xs = np.asarray(inp["x_sample"], np.float32)
    sg = np.asarray(inp["state_gla"], np.float32)
    s5 = np.asarray(inp["state_s5"], np.float32)
    maps = []
    for core in range(NCORES):
        m = dict(shared)
        n0 = seg_of_core[core] * ntiles * 512
        if prefix_tiles:
            npre = prefix_tiles * 512
            buf = np.zeros((npre + ntiles * 512, D), np.float32)
            lo = max(0, n0 - npre)
            buf[npre - (n0 - lo):] = xp[seq_of_core[core]][lo:n0 + ntiles * 512]
            m["xp"] = buf
        else:
            m["xp"] = np.ascontiguousarray(xp[seq_of_core[core]][n0:n0 + ntiles * 512])
        fl = np.zeros((128, 8), np.float32)
        for i in range(NCORES):
            if seq_of_core[i] == seq_of_core[core] and seg_of_core[i] < seg_of_core[core]:
                fl[:, i] = 1.0
        m["flags"] = fl
        m["xs"] = np.ascontiguousarray(xs[2 * core:2 * core + 2].reshape(128, D))
        g2 = sg[2 * core:2 * core + 2]
        m["sgla"] = np.ascontiguousarray(g2.reshape(2, 2, 2, 64, 128).transpose(0, 2, 3, 1, 4).reshape(2, 128, 2, 128))
        s2 = s5[2 * core:2 * core + 2]
        m["ss5"] = np.ascontiguousarray(s2.transpose(2, 3, 1, 0))
        maps.append(m)
    return maps


def _unlay_gla(a):
    return a.reshape(2, 64, 2, 128).transpose(2, 0, 1, 3).reshape(4, 64, 128)


def _unlay_s5(a):
    return a.transpose(2, 0, 1)


_NC_CACHE = {}


def run(inp, ntiles, seq_of_core, seg_of_core=None, balanced=True):
    key = (ntiles, balanced)
    if key not in _NC_CACHE:
        _NC_CACHE[key] = build(ntiles, balanced)
    nc = _NC_CACHE[key]
    maps = make_in_maps(inp, ntiles, seq_of_core, seg_of_core, PREFIX_TILES if balanced == "prefix" else 0)
    res = run_bass_kernel_spmd(nc, maps, core_ids=list(range(NCORES)))
    return res.results


def kernel(**inputs):
    seq_of_core = [c // 4 for c in range(NCORES)]
    seg_of_core = [c % 4 for c in range(NCORES)]
    ntiles = 8
    r = run(inputs, ntiles, seq_of_core, seg_of_core, "prefix")
    y_prompt = np.stack([np.concatenate([r[q * 4 + j]["yp"] for j in range(4)]) for q in range(2)]).astype(np.float32)
    y_sample = np.concatenate([r[c]["ys"].reshape(2, 64, D) for c in range(NCORES)]).astype(np.float32)
    gla_p = np.stack([_unlay_gla(r[3]["glap"]), _unlay_gla(r[7]["glap"])]).astype(np.float32)
    s5_p = np.stack([_unlay_s5(r[3]["s5p"]), _unlay_s5(r[7]["s5p"])]).astype(np.float32)
    gla_s = np.stack([_unlay_gla(r[c]["glas"][q]) for c in range(NCORES) for q in range(2)]).astype(np.float32)
    s5_s = np.stack([_unlay_s5(r[c]["s5s"][q]) for c in range(NCORES) for q in range(2)]).astype(np.float32)
    return (y_prompt, y_sample, gla_p, s5_p, gla_s, s5_s)
```

```python
import math
from contextlib import ExitStack
import numpy as np
import concourse.bass as bass
import concourse.mybir as mybir
from concourse.bass_utils import run_bass_kernel_spmd

F32 = mybir.dt.float32
BF16 = mybir.dt.bfloat16
I32 = mybir.dt.int32
AF = mybir.ActivationFunctionType
ALU = mybir.AluOpType

D = 1024
KT = 8
FF = 2816
FT = 22
INC = 4112
EPS = 1e-6
NCORES = 8
NO_COLL = False
PREFIX_TILES = 24
WARM_COLL = False
TWO_PI = 2.0 * math.pi


class Buf:
    __slots__ = ("name", "last_w", "readers", "sem", "dcount")
    epoch_op = None

    def __init__(self, name):
        self.name = name
        self.last_w = Buf.epoch_op
        self.readers = []
        self.sem = None
        self.dcount = 0


class Op:
    __slots__ = ("eng", "fn", "deps", "is_dma", "needs_inc", "val", "sem", "inc")

    def __init__(self, eng, fn, is_dma):
        self.eng = eng
        self.fn = fn
        self.deps = {}
        self.is_dma = is_dma
        self.needs_inc = False
        self.val = None
        self.sem = None
        self.inc = 16


class Prog:
    ENGS = ("pe", "act", "dve", "pool", "sp")

    def __init__(self, nc):
        self.nc = nc
        self.ops = {e: [] for e in self.ENGS}
        self.all_ops = []
        self.dma_bufs = []

    def _dep(self, op, reads, writes):
        for r in reads:
            if r.last_w is not None:
                op.deps[r.last_w] = True
        for w in writes:
            if w.last_w is not None and w.last_w not in op.deps:
                op.deps[w.last_w] = False
            for rd in w.readers:
                if rd not in op.deps:
                    op.deps[rd] = False
        for r in reads:
            if not op.is_dma:
                r.readers = [x for x in r.readers if x.is_dma or x.eng != op.eng]
            r.readers.append(op)
        for w in writes:
            w.last_w = op
            w.readers = []

    def op(self, eng, fn, reads=(), writes=()):
        o = Op(eng, fn, False)
        self._dep(o, reads, writes)
        self.ops[eng].append(o)
        self.all_ops.append(o)
        return o

    def dma(self, out, in_, reads=(), writes=(), sbuf=None, eng="sp", slow=False):
        if slow:
            fn = lambda e: e.dma_start(out=out, in_=in_, allow_slow_non_contiguous=True)
        else:
            fn = lambda e: e.dma_start(out=out, in_=in_)
        o = Op(eng, fn, True)
        self._dep(o, reads, writes)
        if sbuf not in self.dma_bufs:
            self.dma_bufs.append(sbuf)
        sbuf.dcount += 16
        o.sem = sbuf
        o.val = sbuf.dcount
        self.ops[eng].append(o)
        self.all_ops.append(o)
        return o

    def coll(self, kind, src, dst, groups, reads, writes, sbuf, inc=1):
        fn = lambda e: e.collective_compute(kind, ALU.bypass, replica_groups=groups, ins=[src], outs=[dst])
        o = Op("pool", fn, True)
        self._dep(o, reads, writes)
        if sbuf not in self.dma_bufs:
            self.dma_bufs.append(sbuf)
        sbuf.dcount += inc
        o.sem = sbuf
        o.val = sbuf.dcount
        o.inc = inc
        self.ops["pool"].append(o)
        self.all_ops.append(o)
        return o

    def mm(self, out, lhsT, rhs, start, stop, reads, writes):
        return self.op("pe", lambda e: e.matmul(out, lhsT=lhsT, rhs=rhs, start=start, stop=stop),
                       reads, writes)

    def tr(self, out, in_, ident, reads, writes):
        return self.op("pe", lambda e: e.transpose(out=out, in_=in_, identity=ident), reads, writes)

    def act(self, out, in_, func, reads, writes, bias=None, scale=None, accum=None):
        kw = {}
        if bias is not None:
            kw["bias"] = bias
        if scale is not None:
            kw["scale"] = scale
        if accum is not None:
            kw["accum_out"] = accum
        return self.op("act", lambda e: e.activation(out=out, in_=in_, func=func, **kw), reads, writes)

    def tt(self, eng, out, in0, in1, op, reads, writes):
        return self.op(eng, lambda e: e.tensor_tensor(out=out, in0=in0, in1=in1, op=op), reads, writes)

    def ts(self, eng, out, in0, s1, op0, reads, writes, s2=None, op1=None):
        if op1 is None:
            return self.op(eng, lambda e: e.tensor_scalar(out=out, in0=in0, scalar1=s1, scalar2=None, op0=op0),
                           reads, writes)
        return self.op(eng, lambda e: e.tensor_scalar(out=out, in0=in0, scalar1=s1, scalar2=s2, op0=op0, op1=op1),
                       reads, writes)

    def stt(self, eng, out, in0, scalar, in1, op0, op1, reads, writes):
        return self.op(eng, lambda e: e.scalar_tensor_tensor(out=out, in0=in0, scalar=scalar, in1=in1,
                                                             op0=op0, op1=op1), reads, writes)

    def cp(self, eng, out, in_, reads, writes):
        if eng == "act":
            return self.act(out, in_, AF.Copy, reads, writes)
        return self.op(eng, lambda e: e.tensor_copy(out=out, in_=in_), reads, writes)

    def memset(self, eng, ap, val, writes):
        return self.op(eng, lambda e: e.memset(ap, val), (), writes)

    def fence(self, ap, fbuf, bufs):
        o = self.op("dve", lambda e: e.memset(ap, 0.0), (), list(bufs) + [fbuf])
        Buf.epoch_op = o
        return o

    def recip(self, out, in_, reads, writes):
        return self.op("dve", lambda e: e.reciprocal(out=out, in_=in_), reads, writes)

    @staticmethod
    def _need(o, d, raw):
        if d.is_dma:
            if o.is_dma and not raw:
                return False
            return True
        if d.eng == o.eng:
            if d.eng in ("pe", "pool"):
                return False
            return raw
        return True

    def emit(self, stack):
        nc = self.nc
        for o in self.all_ops:
            for d, raw in o.deps.items():
                if self._need(o, d, raw) and not d.is_dma:
                    d.needs_inc = True
        esem = {}
        for e in self.ENGS:
            esem[e] = stack.enter_context(nc.semaphore("s_" + e))
            c = 0
            for o in self.ops[e]:
                if not o.is_dma:
                    if o.needs_inc:
                        c += 1
                        o.val = c
                    o.sem = e
        for b in self.dma_bufs:
            b.sem = stack.enter_context(nc.semaphore("d_" + b.name))
        block = stack.enter_context(nc.Block())

        def replay(ename, eng):
            waited = {}
            for o in self.ops[ename]:
                for d, raw in o.deps.items():
                    if not self._need(o, d, raw):
                        continue
                    if d.is_dma:
                        key, sem = id(d.sem), d.sem.sem
                    else:
                        key, sem = d.sem, esem[d.sem]
                    if waited.get(key, 0) >= d.val:
                        continue
                    waited[key] = d.val
                    eng.wait_ge(sem, d.val)
                ins = o.fn(eng)
                if o.is_dma:
                    ins.then_inc(o.sem.sem, o.inc)
                elif o.needs_inc:
                    ins.then_inc(esem[ename], 1)
            if ename == "sp":
                for b in self.dma_bufs:
                    if waited.get(id(b), 0) < b.dcount:
                        eng.wait_ge(b.sem, b.dcount)

        @block.tensor
        def _(eng):
            replay("pe", eng)

        @block.scalar
        def _(eng):
            replay("act", eng)

        @block.vector
        def _(eng):
            replay("dve", eng)

        @block.gpsimd
        def _(eng):
            replay("pool", eng)

        @block.sync
        def _(eng):
            replay("sp", eng)


W_SPECS = [
    ("w_ffn1_gate", 1024, FF, "g_ffn1"), ("w_ffn1_up", 1024, FF, "g_ffn1"), ("w_ffn1_down", FF, 1024, None),
    ("w_in", 1024, INC, "g_mix"), ("w_gla_out", 512, 1024, "g_gla_head"), ("w_glu_a", 512, 512, None),
    ("w_glu_b", 512, 512, None), ("w_s5_out", 512, 1024, None), ("w_out", 1024, 1024, None),
    ("w_ffn2_gate", 1024, FF, "g_ffn2"), ("w_ffn2_up", 1024, FF, "g_ffn2"), ("w_ffn2_down", FF, 1024, None),
]
GAINS = [("g_ffn1", 8), ("g_mix", 8), ("g_ffn2", 8), ("g_gla_head", 4)]


def build(ntiles, balanced=True):
    Buf.epoch_op = None
    nc = bass.Bass("TRN2", target_bir_lowering=False)
    NPRE = PREFIX_TILES if balanced == "prefix" else 0
    NP = ntiles * 512
    NPIN = (ntiles + NPRE) * 512

    def din(name, shape, dt=F32):
        return nc.dram_tensor(name, list(shape), dt, kind="ExternalInput").ap()

    def dout(name, shape):
        return nc.dram_tensor(name, list(shape), F32, kind="ExternalOutput").ap()

    xp = din("xp", [NPIN, D])
    xs = din("xs", [128, D])
    sgla = din("sgla", [2, 128, 2, 128])
    ss5 = din("ss5", [64, 2, 32, 2])
    wd = {}
    ws = {}
    for name, K, N, _g in W_SPECS:
        wd[name] = din(name, [128, K // 128, N])
        ws[name] = nc.dram_tensor("scr_" + name, [128, K // 128, N], BF16).ap()
    gd = {name: din(name, [128, n]) for name, n in GAINS}
    g_final_d = din("g_final", [128, D])
    wgu_d = din("w_gate_up", [16, 256])
    bgate_d = din("b_gate", [1, 256])
    lamre_d = din("lam_re", [64, 32])
    lamim_d = din("lam_im", [64, 32])
    logdt_d = din("log_dt", [64, 32])
    bre_d = din("b_re", [64, 32, 16])
    bim_d = din("b_im", [64, 32, 16])
    cre_d = din("c_re", [64, 32, 16])
    cim_d = din("c_im", [64, 32, 16])
    dcol_d = din("dcol", [128, 32])
    ident_d = din("ident", [128, 128])
    triinc_d = din("tri_inc", [128, 128])
    trirev_d = din("tri_rev", [128, 128])
    cmask_d = din("cmask", [128, 128])
    s5mask_d = din("s5mask", [128, 128])
    flags_d = din("flags", [128, 8])
    hscr = nc.dram_tensor("hscr", [max(NP, 128) if balanced is True else 128, D], F32).ap()
    exsrc = nc.dram_tensor("exsrc", [128, 384], F32).ap()
    exdst = nc.dram_tensor("exdst", [8 * 128, 384], F32).ap()

    yp = dout("yp", [NP, D])
    ys = dout("ys", [128, D])
    glap_o = dout("glap", [128, 2, 128])
    s5p_o = dout("s5p", [64, 2, 32])
    glas_o = dout("glas", [2, 128, 2, 128])
    s5s_o = dout("s5s", [2, 64, 2, 32])

    with ExitStack() as st:
        P = Prog(nc)
        cnt = [0]

        def sb(shape, dt=F32, name=None):
            cnt[0] += 1
            nm = (name or "t") + "_%d" % cnt[0]
            return st.enter_context(nc.sbuf_tensor(nm, list(shape), dt)), Buf(nm)

        banks = []
        for i in range(6):
            t = st.enter_context(nc.psum_tensor("pb%d" % i, [128, 512], F32))
            banks.append((t, Buf("pb%d" % i)))
        bbanks = []
        for i in range(2):
            t = st.enter_context(nc.psum_tensor("pbb%d" % i, [128, 1024], BF16))
            bbanks.append((t, Buf("pbb%d" % i)))
        bk = [0]

        def nb():
            bk[0] = (bk[0] + 1) % 6
            return banks[bk[0]]

        bbk = [0]

        def nbb():
            bbk[0] = (bbk[0] + 1) % 2
            return bbanks[bbk[0]]

        identf, b_identf = sb([128, 128], F32, "identf")
        identb, b_identb = sb([128, 128], BF16, "identb")
        triinc, b_triinc = sb([128, 128], F32, "triinc")
        trirev, b_trirev = sb([128, 128], F32, "trirev")
        cmask, b_cmask = sb([128, 128], F32, "cmask")
        s5mask, b_s5mask = sb([128, 128], F32, "s5mask")
        gfin, b_gfin = sb([128, D], F32, "gfin")
        wgu, b_wgu = sb([16, 256], F32, "wgu")
        bgate, b_bgate = sb([1, 256], F32, "bgate")
        ones1, b_ones1 = sb([1, 128], F32, "ones1")
        dcol, b_dcol = sb([128, 32], F32, "dcol")
        T0, b_T0 = sb([128, 32, 128], BF16, "T0")
        Wm, b_Wm = sb([128, 32, 2, 64], BF16, "Wm")
        Vm, b_Vm = sb([64, 32, 2, 128], BF16, "Vm")
        A1, b_A1 = sb([64, 2, 32], F32, "A1")
        A2, b_A2 = sb([64, 2, 32], F32, "A2")
        AB1, b_AB1 = sb([64, 2, 32], F32, "AB1")
        AB2, b_AB2 = sb([64, 2, 32], F32, "AB2")
        flags, b_flags = sb([128, 8], F32, "flags")
        Dtot, b_Dtot = sb([128, 2], F32, "Dtot")
        exs, b_exs = sb([128, 384 if balanced is True else 1], F32, "exs")
        exg, b_exg = sb([128, 384 if balanced is True else 1], F32, "exg")
        gcols = {}
        for name, n in GAINS:
            gcols[name] = sb([128, n], F32, name)

        for t_, b_, d_ in [(identf, b_identf, ident_d), (triinc, b_triinc, triinc_d), (trirev, b_trirev, trirev_d),
                           (cmask, b_cmask, cmask_d), (s5mask, b_s5mask, s5mask_d), (gfin, b_gfin, g_final_d),
                           (wgu, b_wgu, wgu_d), (bgate, b_bgate, bgate_d), (dcol, b_dcol, dcol_d), (flags, b_flags, flags_d)]:
            P.dma(t_[:], d_, writes=[b_], sbuf=b_)
        for name, n in GAINS:
            P.dma(gcols[name][0][:], gd[name], writes=[gcols[name][1]], sbuf=gcols[name][1])
        if balanced is True and not NO_COLL and WARM_COLL:
            wsrc = nc.dram_tensor("wsrc", [128, 64], F32).ap()
            wdst = nc.dram_tensor("wdst", [8 * 128, 64], F32).ap()
            b_wsrc, b_wdst, b_wcc = Buf("wsrc"), Buf("wdst"), Buf("wcc")
            P.dma(wsrc, ident_d[:, 0:64], writes=[b_wsrc], sbuf=b_identf)
            P.coll("AllGather", wsrc, wdst, [list(range(NCORES))], reads=[b_wsrc], writes=[b_wdst], sbuf=b_wcc)
        P.cp("dve", identb[:], identf[:], [b_identf], [b_identb])
        P.memset("dve", ones1[:], 1.0, [b_ones1])

        prep_bufs = []
        with ExitStack() as pst:
            def psb(shape, dt=F32, name=None):
                cnt[0] += 1
                nm = (name or "p") + "_%d" % cnt[0]
                b = Buf(nm)
                prep_bufs.append(b)
                return pst.enter_context(nc.sbuf_tensor(nm, list(shape), dt)), b

            CH = 2816
            stg = [psb([128, CH], F32, "stg") for _ in range(7)]
            stb = [psb([128, CH], BF16, "stb") for _ in range(7)]
            ci = 0
            cast_engs = ["act", "dve", "act", "dve", "act", "pool"]
            scr_bufs = {name: Buf("scr_" + name) for name, _, _, _ in W_SPECS}
            for name, K, N, gk in W_SPECS:
                for kt in range(K // 128):
                    for c0 in range(0, N, CH):
                        cw = min(CH, N - c0)
                        s_t, s_b = stg[ci % 7]
                        o_t, o_b = stb[ci % 7]
                        P.dma(s_t[:, 0:cw], wd[name][:, kt, c0:c0 + cw], writes=[s_b], sbuf=s_b)
                        eng = cast_engs[ci % 6]
                        if gk is None:
                            P.cp(eng, o_t[:, 0:cw], s_t[:, 0:cw], [s_b], [o_b])
                        else:
                            gt, gb = gcols[gk]
                            if eng == "act":
                                P.act(o_t[:, 0:cw], s_t[:, 0:cw], AF.Copy, [s_b, gb], [o_b], scale=gt[:, kt:kt + 1])
                            else:
                                P.ts(eng, o_t[:, 0:cw], s_t[:, 0:cw], gt[:, kt:kt + 1], ALU.mult, [s_b, gb], [o_b])
                        P.dma(ws[name][:, kt, c0:c0 + cw], o_t[:, 0:cw], reads=[o_b], writes=[scr_bufs[name]], sbuf=o_b, eng="act")
                        ci += 1

        fence_t, fence_b = sb([128, 1], F32, "fence")
        P.fence(fence_t[:], fence_b, prep_bufs + list(scr_bufs.values()))
        prep_bufs = []
        with ExitStack() as pst:
            def psb(shape, dt=F32, name=None):
                cnt[0] += 1
                nm = (name or "p") + "_%d" % cnt[0]
                b = Buf(nm)
                prep_bufs.append(b)
                return pst.enter_context(nc.sbuf_tensor(nm, list(shape), dt)), b

            def small(shape, name):
                return psb(shape, F32, name)

            lr, b_lr = small([64, 32], "lr")
            li, b_li = small([64, 32], "li")
            ldt, b_ldt = small([64, 32], "ldt")
            P.dma(lr[:], lamre_d, writes=[b_lr], sbuf=b_lr)
            P.dma(li[:], lamim_d, writes=[b_li], sbuf=b_li)
            P.dma(ldt[:], logdt_d, writes=[b_ldt], sbuf=b_ldt)
            Bre, b_Bre = small([64, 32, 16], "Bre")
            Bim, b_Bim = small([64, 32, 16], "Bim")
            Cre, b_Cre = small([64, 32, 16], "Cre")
            Cim, b_Cim = small([64, 32, 16], "Cim")
            for t_, b_, d_ in [(Bre, b_Bre, bre_d), (Bim, b_Bim, bim_d), (Cre, b_Cre, cre_d), (Cim, b_Cim, cim_d)]:
                P.dma(t_[:], d_, writes=[b_], sbuf=b_)
            dt_, b_dt = small([64, 32], "dt")
            P.act(dt_[:], ldt[:], AF.Exp, [b_ldt], [b_dt])
            aa, b_aa = small([64, 32], "aa")
            th, b_th = small([64, 32], "th")
            P.tt("dve", aa[:], lr[:], dt_[:], ALU.mult, [b_lr, b_dt], [b_aa])
            P.tt("dve", th[:], li[:], dt_[:], ALU.mult, [b_li, b_dt], [b_th])
            mag, b_mag = small([64, 32], "mag")
            P.act(mag[:], aa[:], AF.Exp, [b_aa], [b_mag])
            ki, b_ki = psb([64, 32], I32, "ki")
            kf, b_kf = small([64, 32], "kf")
            P.ts("dve", ki[:], th[:], 1.0 / TWO_PI, ALU.mult, [b_th], [b_ki])
            P.cp("dve", kf[:], ki[:], [b_ki], [b_kf])
            C1 = 6.28125
            C2 = TWO_PI - C1
            thr, b_thr = small([64, 32], "thr")
            P.stt("dve", thr[:], kf[:], -C1, th[:], ALU.mult, ALU.add, [b_kf, b_th], [b_thr])
            P.stt("dve", thr[:], kf[:], -C2, thr[:], ALU.mult, ALU.add, [b_kf, b_thr], [b_thr])
            sn, b_sn = small([64, 32], "sn")
            cs, b_cs = small([64, 32], "cs")
            ab, b_ab = small([64, 32], "ab")
            P.act(sn[:], thr[:], AF.Sin, [b_thr], [b_sn])
            P.act(ab[:], thr[:], AF.Abs, [b_thr], [b_ab])
            halfpi, b_halfpi = small([64, 1], "halfpi")
            P.memset("dve", halfpi[:], math.pi / 2.0, [b_halfpi])
            P.act(cs[:], ab[:], AF.Sin, [b_ab, b_halfpi], [b_cs], bias=halfpi[:], scale=-1.0)
            LP, b_LP = small([64, 2, 9, 32], "LP")
            P.memset("dve", LP[:, 0, 0, :], 1.0, [b_LP])
            P.memset("dve", LP[:, 1, 0, :], 0.0, [b_LP])
            P.tt("dve", LP[:, 0, 1, :], mag[:], cs[:], ALU.mult, [b_mag, b_cs], [b_LP])
            P.tt("dve", LP[:, 1, 1, :], mag[:], sn[:], ALU.mult, [b_mag, b_sn], [b_LP])
            tA, b_tA = small([64, 32], "tA")
            tB, b_tB = small([64, 32], "tB")

            def cmul(o_re, o_im, a_re, a_im, c_re, c_im, rd, wr, shape_t=None):
                t1, bt1 = shape_t[0]
                t2, bt2 = shape_t[1]
                P.tt("dve", t1, a_re, c_re, ALU.mult, rd, [bt1])
                P.tt("dve", t2, a_im, c_im, ALU.mult, rd, [bt2])
                P.tt("dve", o_re, t1, t2, ALU.subtract, [bt1, bt2], wr)
                P.tt("dve", t1, a_re, c_im, ALU.mult, rd, [bt1])
                P.tt("dve", t2, a_im, c_re, ALU.mult, rd, [bt2])
                P.tt("dve", o_im, t1, t2, ALU.add, [bt1, bt2], wr)

            for tau in range(2, 9):
                cmul(LP[:, 0, tau, :], LP[:, 1, tau, :], LP[:, 0, tau - 1, :], LP[:, 1, tau - 1, :],
                     LP[:, 0, 1, :], LP[:, 1, 1, :], [b_LP], [b_LP], [(tA[:], b_tA), (tB[:], b_tB)])
            P.cp("dve", A1[:, 0, :], LP[:, 0, 8, :], [b_LP], [b_A1])
            P.cp("dve", A1[:, 1, :], LP[:, 0, 8, :], [b_LP], [b_A1])
            P.ts("dve", A2[:, 0, :], LP[:, 1, 8, :], -1.0, ALU.mult, [b_LP], [b_A2])
            P.cp("dve", A2[:, 1, :], LP[:, 1, 8, :], [b_LP], [b_A2])
            nsq = int(round(math.log2(max(ntiles, 1) * 64)))
            assert 2 ** nsq == max(ntiles, 1) * 64
            LB, b_LB = small([64, 2, 32], "LB")
            P.cp("dve", LB[:, 0, :], LP[:, 0, 8, :], [b_LP], [b_LB])
            P.cp("dve", LB[:, 1, :], LP[:, 1, 8, :], [b_LP], [b_LB])
            for _ in range(nsq):
                P.tt("dve", tA[:], LB[:, 0, :], LB[:, 0, :], ALU.mult, [b_LB], [b_tA])
                P.tt("dve", tB[:], LB[:, 1, :], LB[:, 1, :], ALU.mult, [b_LB], [b_tB])
                P.stt("dve", LB[:, 1, :], LB[:, 0, :], 2.0, LB[:, 1, :], ALU.mult, ALU.mult, [b_LB], [b_LB])
                P.tt("dve", LB[:, 0, :], tA[:], tB[:], ALU.subtract, [b_tA, b_tB], [b_LB])
            P.cp("dve", AB1[:, 0, :], LB[:, 0, :], [b_LB], [b_AB1])
            P.cp("dve", AB1[:, 1, :], LB[:, 0, :], [b_LB], [b_AB1])
            P.ts("dve", AB2[:, 0, :], LB[:, 1, :], -1.0, ALU.mult, [b_LB], [b_AB2])
            P.cp("dve", AB2[:, 1, :], LB[:, 1, :], [b_LB], [b_AB2])
            inv, b_inv = small([64, 2, 32], "inv")
            den, b_den = small([64, 32], "den")
            P.tt("dve", tA[:], LP[:, 0, 8, :], LP[:, 0, 8, :], ALU.mult, [b_LP], [b_tA])
            P.tt("dve", tB[:], LP[:, 1, 8, :], LP[:, 1, 8, :], ALU.mult, [b_LP], [b_tB])
            P.tt("dve", den[:], tA[:], tB[:], ALU.add, [b_tA, b_tB], [b_den])
            P.recip(den[:], den[:], [b_den], [b_den])
            P.tt("dve", inv[:, 0, :], LP[:, 0, 8, :], den[:], ALU.mult, [b_LP, b_den], [b_inv])
            P.stt("dve", inv[:, 1, :], LP[:, 1, 8, :], -1.0, den[:], ALU.mult, ALU.mult, [b_LP, b_den], [b_inv])
            fre, b_fre = small([64, 32], "fre")
            fim, b_fim = small([64, 32], "fim")
            nr, b_nr = small([64, 32], "nr")
            P.ts("dve", nr[:], LP[:, 0, 1, :], -1.0, ALU.add, [b_LP], [b_nr])
            P.tt("dve", tA[:], lr[:], lr[:], ALU.mult, [b_lr], [b_tA])
            P.tt("dve", tB[:], li[:], li[:], ALU.mult, [b_li], [b_tB])
            P.tt("dve", den[:], tA[:], tB[:], ALU.add, [b_tA, b_tB], [b_den])
            P.recip(den[:], den[:], [b_den], [b_den])
            P.tt("dve", tA[:], nr[:], lr[:], ALU.mult, [b_nr, b_lr], [b_tA])
            P.tt("dve", tB[:], LP[:, 1, 1, :], li[:], ALU.mult, [b_LP, b_li], [b_tB])
            P.tt("dve", fre[:], tA[:], tB[:], ALU.add, [b_tA, b_tB], [b_fre])
            P.tt("dve", fre[:], fre[:], den[:], ALU.mult, [b_fre, b_den], [b_fre])
            P.tt("dve", tA[:], LP[:, 1, 1, :], lr[:], ALU.mult, [b_LP, b_lr], [b_tA])
            P.tt("dve", tB[:], nr[:], li[:], ALU.mult, [b_nr, b_li], [b_tB])
            P.tt("dve", fim[:], tA[:], tB[:], ALU.subtract, [b_tA, b_tB], [b_fim])
            P.tt("dve", fim[:], fim[:], den[:], ALU.mult, [b_fim, b_den], [b_fim])

            def bc(ap2d):
                return ap2d.unsqueeze(2).to_broadcast([64, 32, 16])

            u1, b_u1 = small([64, 32, 16], "u1")
            u2, b_u2 = small([64, 32, 16], "u2")
            utmp = [(u1[:], b_u1), (u2[:], b_u2)]
            bbre, b_bbre = small([64, 32, 16], "bbre")
            bbim, b_bbim = small([64, 32, 16], "bbim")
            cmul(bbre[:], bbim[:], Bre[:], Bim[:], bc(fre[:]), bc(fim[:]), [b_Bre, b_Bim, b_fre, b_fim],
                 [b_bbre, b_bbim], utmp)
            BL, b_BL = small([64, 2, 32, 8, 16], "BL")
            for s in range(8):
                tau = 7 - s
                cmul(BL[:, 0, :, s, :], BL[:, 1, :, s, :], bbre[:], bbim[:], bc(LP[:, 0, tau, :]), bc(LP[:, 1, tau, :]),
                     [b_bbre, b_bbim, b_LP], [b_BL], utmp)
            CL, b_CL = small([64, 2, 32, 8, 16], "CL")
            for t in range(8):
                cmul(CL[:, 0, :, t, :], CL[:, 1, :, t, :], Cre[:], Cim[:], bc(LP[:, 0, t + 1, :]), bc(LP[:, 1, t + 1, :]),
                     [b_Cre, b_Cim, b_LP], [b_CL], utmp)
            P.cp("dve", Vm[:, :, 0, :].rearrange("p g (t j) -> p g t j", t=8), CL[:, 0, :, :, :], [b_CL], [b_Vm])
            P.ts("dve", Vm[:, :, 1, :].rearrange("p g (t j) -> p g t j", t=8), CL[:, 1, :, :, :], -1.0, ALU.mult,
                 [b_CL], [b_Vm])
            CLp, b_CLp = small([64, 2, 32, 8, 16], "CLp")
            v1, b_v1 = small([64, 32, 8, 16], "v1")
            v2, b_v2 = small([64, 32, 8, 16], "v2")

            def bc4(ap2d):
                return ap2d.unsqueeze(2).unsqueeze(3).to_broadcast([64, 32, 8, 16])

            cmul(CLp[:, 0], CLp[:, 1], CL[:, 0], CL[:, 1], bc4(inv[:, 0, :]), bc4(inv[:, 1, :]), [b_CL, b_inv],
                 [b_CLp], [(v1[:], b_v1), (v2[:], b_v2)])
            P.ts("dve", CLp[:, 1], CLp[:, 1], -1.0, ALU.mult, [b_CLp], [b_CLp])
            tmpT, b_tmpT = small([128, 128], "tmpT")
            for g in range(32):
                pt, pb = nb()
                P.mm(pt[:, 0:128], BL[:, 0, g].rearrange("p s h -> p (s h)"), CLp[:, 0, g].rearrange("p t j -> p (t j)"),
                     True, False, [b_BL, b_CLp], [pb])
                P.mm(pt[:, 0:128], BL[:, 1, g].rearrange("p s h -> p (s h)"), CLp[:, 1, g].rearrange("p t j -> p (t j)"),
                     False, True, [b_BL, b_CLp], [pb])
                P.tt("dve", tmpT[:], pt[:, 0:128], s5mask[:], ALU.mult, [pb, b_s5mask], [b_tmpT])
                P.stt("dve", T0[:, g, :], identf[:], dcol[:, g:g + 1], tmpT[:], ALU.mult, ALU.add,
                      [b_identf, b_dcol, b_tmpT], [b_T0])
                for slot in range(2):
                    pt2, pb2 = nb()
                    P.tr(pt2[:, 0:64], BL[:, slot, g].rearrange("p s h -> p (s h)"), identf[0:64, 0:64], [b_BL, b_identf], [pb2])
                    P.cp("act", Wm[:, g, slot, :], pt2[:, 0:64], [pb2], [b_Wm])
        P.fence(fence_t[:], fence_b, prep_bufs)
        main_bufs = []

        def msb(shape, dt=F32, name=None):
            t, b = sb(shape, dt, name)
            main_bufs.append(b)
            return t, b

        TM = 512
        xt, b_xt = msb([128, 4, D], F32, "xt")
        xn, b_xn = msb([128, D], BF16, "xn")
        xnT, b_xnT = msb([128, 8, TM], BF16, "xnT")
        actb, b_actb = msb([128, FT, TM], BF16, "actb")
        wblk = [msb([128, 4096], BF16, "wblk") for _ in range(2)]
        DN = 256
        wdn = [msb([128, FT, DN], BF16, "wdn") for _ in range(1)]
        sgt, b_sgt = msb([128, TM], F32, "sgt")
        junk, b_junk = msb([128, D], BF16, "junk")
        ssq, b_ssq = msb([128, 8], F32, "ssq")
        rstd, b_rstd = msb([128, 8], F32, "rstd")
        wga, b_wga = msb([128, 8, 16], BF16, "wga")
        gaT, b_gaT = msb([16, TM], F32, "gaT")
        Lb, b_Lb = msb([128, 1, 256], F32, "Lb")
        E1, b_E1 = msb([128, 2, TM], F32, "E1")
        E2, b_E2 = msb([128, 2, TM], F32, "E2")
        E3, b_E3 = msb([128, 1, 256], F32, "E3")
        qt, b_qt = msb([128, 2, TM], BF16, "qt")
        qa, b_qa = msb([128, 2, TM], BF16, "qa")
        qb, b_qb = msb([128, 2, TM], BF16, "qb")
        ktl, b_ktl = msb([128, 2, TM], BF16, "ktl")
        khat, b_khat = msb([128, 4, 256], BF16, "khat")
        vb, b_vb = msb([128, 4, 512], BF16, "vb")
        gatesA, b_gA = msb([128, 4, 1024], BF16, "gatesA")
        gatesB, b_gB = msb([128, 4, 1024], BF16, "gatesB")
        ATb, b_ATb = msb([128, 4, 128], BF16, "ATb")
        onb, b_onb = msb([128, 512], BF16, "onb")
        mf, b_mf = msb([128, 512], F32, "mf")
        mbf, b_mbf = xn, b_xn
        mixB, _bm = msb([128, 6144], BF16, "mixB")
        b_srb, b_g5T, b_glu = Buf("srb"), Buf("g5T"), Buf("glu")
        srb = mixB[:, 0:2048].rearrange("p (a b) -> p a b", a=4)
        g5T = mixB[:, 2048:4096].rearrange("p (a b) -> p a b", a=4)
        glu = mixB[:, 4096:6144].rearrange("p (a b) -> p a b", a=4)
        onT, b_onT = msb([128, 4, TM], BF16, "onT")
        Sf = [msb([128, 2, 128], F32, "Sf") for _ in range(2)]
        Sb = [msb([128, 2, 128], BF16, "Sb") for _ in range(2)]
        So = [msb([128, 2, 128], F32, "So") for _ in range(2)]
        dS, b_dS = msb([128, 2, 2], F32, "dS")
        actraw = actb[:].rearrange("p f t -> p (f t)")
        Uc = actraw[:, 0:4096].rearrange("p (g s h) -> p g s h", g=32, s=8)
        Gc = actraw[:, 0:4096].rearrange("p (t c) -> p t c", t=8)
        Ug = actraw[:, 4096:6144].rearrange("p (g c) -> p g c", g=32)
        Xbf = actraw[:, 6144:10240]
        Bst_full, b_Bst = msb([128, 2 * 32 * 65], F32, "Bst")
        Bst = Bst_full[0:64, :]
        ring4 = [wblk[0], wblk[1], (gatesA[:].rearrange("p a b -> p (a b)"), b_gA), (gatesB[:].rearrange("p a b -> p (a b)"), b_gB)]
        wdn.append((mixB[:, 0:FT * DN].rearrange("p (f n) -> p f n", f=FT), [b_srb, b_g5T, b_glu]))
        st1, b_st1 = msb([64, 2, 32], F32, "st1")
        st2, b_st2 = msb([64, 2, 32], F32, "st2")
        xo, b_xo = msb([64, 2, 32], F32, "xo")
        xin, b_xin = msb([64, 2, 32, 2], F32, "xin")
        xo2 = [msb([64, 2, 32], F32, "xo2") for _ in range(2)]

        P.memset("pool", qa[:], 0.0, [b_qa])
        P.memset("pool", qb[:], 0.0, [b_qb])

        def norm_T(T, src_t, src_b):
            NT = T // 128
            xn2 = mf[:].bitcast(BF16)
            for ts_ in range(NT):
                P.act(junk[:], src_t[:, ts_, :], AF.Square, [src_b], [b_junk, b_ssq], accum=ssq[:, ts_:ts_ + 1])
            P.act(rstd[:, 0:NT], ssq[:, 0:NT], AF.Sqrt, [b_ssq], [b_rstd], bias=EPS, scale=1.0 / D)
            P.recip(rstd[:, 0:NT], rstd[:, 0:NT], [b_rstd], [b_rstd])
            for ts_ in range(NT):
                xa, xb_ = (xn[:], b_xn) if ts_ % 2 == 0 else (xn2, b_mf)
                P.ts("dve", xa, src_t[:, ts_, :], rstd[:, ts_:ts_ + 1], ALU.mult, [src_b, b_rstd], [xb_])
                bt, bb = nbb()
                for kt in range(8):
                    P.tr(bt[:, kt * 128:(kt + 1) * 128], xa[:, kt * 128:(kt + 1) * 128], identb[:], [xb_, b_identb], [bb])
                P.cp("dve" if ts_ % 2 else "act", xnT[:, :, ts_ * 128:(ts_ + 1) * 128],
                     bt[:].rearrange("p (k t) -> p k t", k=8), [bb], [b_xnT])

        wi = [0, 0]

        def load_blk(scr_name, view_fn, shape_fn, big=False):
            if big:
                t, b = ring4[wi[1] % 4]
                wi[1] += 1
            else:
                t, b = wblk[wi[0] % 2]
                wi[0] += 1
            v = shape_fn(t)
            P.dma(v, view_fn(ws[scr_name]), reads=[scr_bufs[scr_name]], writes=[b], sbuf=b)
            return v, b

        def ffn(T, h_t, h_b, wg, wu, wdn_name):
            NT = T // 128
            norm_T(T, h_t, h_b)
            gi = 0
            for c0 in range(0, FF, 512):
                cw = min(512, FF - c0)
                gv, gb = load_blk(wg, lambda a: a[:, :, c0:c0 + cw], lambda t: t[:, 0:8 * cw].rearrange("p (k n) -> p k n", k=8), big=True)
                uv, ub = load_blk(wu, lambda a: a[:, :, c0:c0 + cw], lambda t: t[:, 0:8 * cw].rearrange("p (k n) -> p k n", k=8), big=True)
                for f0 in range(0, cw, 128):
                    ft = (c0 + f0) // 128
                    pg, pgb = banks[(gi % 2) * 2]
                    pu, pub = banks[(gi % 2) * 2 + 1]
                    gi += 1
                    for kt in range(8):
                        P.mm(pg[:, 0:T], gv[:, kt, f0:f0 + 128], xnT[:, kt, 0:T], kt == 0, kt == 7, [gb, b_xnT], [pgb])
                    for kt in range(8):
                        P.mm(pu[:, 0:T], uv[:, kt, f0:f0 + 128], xnT[:, kt, 0:T], kt == 0, kt == 7, [ub, b_xnT], [pub])
                    P.act(sgt[:, 0:T], pg[:, 0:T], AF.Silu, [pgb], [b_sgt])
                    P.tt("dve", actb[:, ft, 0:T], sgt[:, 0:T], pu[:, 0:T], ALU.mult, [b_sgt, pub], [b_actb])
            for dh in range(2):
                for fh in range(2):
                    wt_, wb_ = wdn[(dh * 2 + fh) % 2]
                    wbl_ = wb_ if isinstance(wb_, list) else [wb_]
                    wv_ = (wt_[:] if not isinstance(wb_, list) else wt_).rearrange("p f n -> p (f n)")[:, 0:11 * 512].rearrange("p (f n) -> p f n", f=11)
                    P.dma(wv_, ws[wdn_name][:, fh * 11:(fh + 1) * 11, dh * 512:(dh + 1) * 512], reads=[scr_bufs[wdn_name]],
                          writes=wbl_, sbuf=wbl_[0])
                    for ts_ in range(NT):
                        po, pob = banks[ts_]
                        for f_ in range(11):
                            ft = fh * 11 + f_
                            P.mm(po[:, :], actb[:, ft, ts_ * 128:(ts_ + 1) * 128], wv_[:, f_, :], ft == 0, ft == FT - 1,
                                 [b_actb] + wbl_, [pob])
                for ts_ in range(NT):
                    po, pob = banks[ts_]
                    P.stt("dve", h_t[:, ts_, dh * 512:(dh + 1) * 512], po[:, :], 0.5, h_t[:, ts_, dh * 512:(dh + 1) * 512],
                          ALU.mult, ALU.add, [pob, h_b], [h_b])

        def tok_proj(T, scr_name, c0, cw, consume):
            NT = T // 128
            wv, wb_ = load_blk(scr_name, lambda a: a[:, :, c0:c0 + cw], lambda t: t[:, 0:8 * cw].rearrange("p (k n) -> p k n", k=8))
            for ts_ in range(NT):
                pt, pb = nb()
                for kt in range(8):
                    P.mm(pt[:, 0:cw], xnT[:, kt, ts_ * 128:(ts_ + 1) * 128], wv[:, kt, :], kt == 0, kt == 7, [b_xnT, wb_], [pb])
                consume(ts_, pt[:, 0:cw], pb)

        def mixer(T, h_t, h_b, Q, s5_init, gla_mode, gla_out, s5_out, so=False):
            NT = T // 128
            NC = T // 8
            NCQ = NC // Q
            norm_T(T, h_t, h_b)
            uv_, ub_ = load_blk("w_in", lambda a: a[:, :, 1552:2064], lambda t: t[:, 0:4096].rearrange("p (k n) -> p k n", k=8))
            for s_lo in range(8):
                pt, pb = nb()
                for kt in range(8):
                    P.mm(pt[0:NC, 0:512], xnT[:, kt, s_lo:T:8], uv_[:, kt, :], kt == 0, kt == 7, [b_xnT, ub_], [pb])
                P.cp("act" if s_lo % 2 else "dve", Uc[0:NC, :, s_lo, :], pt[0:NC, 0:512].rearrange("c (g h) -> c g h", g=32), [pb], [b_actb])
            for half in range(2):
                bt, bb = nbb()
                for gg in range(16):
                    g = half * 16 + gg
                    P.tr(bt[:, gg * NC:(gg + 1) * NC], Uc[0:NC, g].rearrange("c s h -> c (s h)"), identb[0:NC, 0:NC], [b_actb, b_identb], [bb])
                P.cp("dve", Ug[:, half * 16:(half + 1) * 16, 0:NC], bt[:, 0:16 * NC].rearrange("p (g c) -> p g c", g=16), [bb], [b_actb])
            Bv = Bst[:, 0:2 * 32 * Q * (NCQ + 1)].rearrange("p (s g q c) -> p s g q c", s=2, g=32, q=Q)
            for q in range(Q):
                P.cp("pool", Bv[:, :, :, q, 0], s5_init(q), [b_xo] if s5_init_reads is None else s5_init_reads, [b_Bst])
            for g0 in range(0, 32, 4):
                pt, pb = nb()
                pv = pt[0:64, 0:2 * 4 * NC].rearrange("p (s g c) -> p s g c", s=2, g=4)
                for gg in range(4):
                    for slot in range(2):
                        P.mm(pv[:, slot, gg, :], Wm[:, g0 + gg, slot, :], Ug[:, g0 + gg, 0:NC], True, True, [b_Wm, b_actb], [pb])
                for q in range(Q):
                    P.cp("dve" if (g0 // 4) % 2 else "act", Bv[:, :, g0:g0 + 4, q, 1:NCQ + 1], pv[:, :, :, q * NCQ:(q + 1) * NCQ], [pb], [b_Bst])
            for c in range(NCQ):
                for q in range(Q):
                    P.tt("pool", st1[:], A1[:], Bv[:, :, :, q, c], ALU.mult, [b_A1, b_Bst], [b_st1])
                    P.tt("pool", st2[:, 0, :], A2[:, 0, :], Bv[:, 1, :, q, c], ALU.mult, [b_A2, b_Bst], [b_st2])
                    P.tt("pool", st2[:, 1, :], A2[:, 1, :], Bv[:, 0, :, q, c], ALU.mult, [b_A2, b_Bst], [b_st2])
                    P.tt("pool", st1[:], st1[:], st2[:], ALU.add, [b_st1, b_st2], [b_st1])
                    P.tt("pool", Bv[:, :, :, q, c + 1], Bv[:, :, :, q, c + 1], st1[:], ALU.add, [b_Bst, b_st1], [b_Bst])
            for q in range(Q):
                s5_out(q, Bv[:, :, :, q, NCQ])
            P.dma(wga[:], ws["w_in"][:, :, 1536:1552], reads=[scr_bufs["w_in"]], writes=[b_wga], sbuf=b_wga)
            qkv, qkb = load_blk("w_in", lambda a: a[:, :, 0:512], lambda t: t[:, 0:4096].rearrange("p (k n) -> p k n", k=8))
            pt, pb = nb()
            for kt in range(8):
                P.mm(pt[0:16, 0:T], wga[:, kt, :], xnT[:, kt, 0:T], kt == 0, kt == 7, [b_wga, b_xnT], [pb])
            P.cp("dve", gaT[:, 0:T], pt[0:16, 0:T], [pb], [b_gaT])
            for ts_ in range(NT):
                pt, pb = nb()
                P.mm(pt[:, 0:256], gaT[:, ts_ * 128:(ts_ + 1) * 128], wgu[:], True, False, [b_gaT, b_wgu], [pb])
                P.mm(pt[:, 0:256], ones1[:], bgate[:], False, True, [b_ones1, b_bgate], [pb])
                P.act(mf[:, 0:256], pt[:, 0:256], AF.Exp, [pb], [b_mf], scale=-1.0)
                P.act(Lb[:, 0, :], mf[:, 0:256], AF.Ln, [b_mf], [b_Lb], bias=1.0)
                for pair in range(2):
                    pt2, pb2 = nb()
                    P.mm(pt2[:, 0:128], Lb[:, 0, pair * 128:(pair + 1) * 128], triinc[:], True, True, [b_Lb, b_triinc], [pb2])
                    P.act(E1[:, pair, ts_ * 128:(ts_ + 1) * 128], pt2[:, 0:128], AF.Exp, [pb2], [b_E1])
                    if not so:
                        P.act(E2[:, pair, ts_ * 128:(ts_ + 1) * 128], pt2[:, 0:128], AF.Exp, [pb2], [b_E2], scale=-1.0)
                pt3, pb3 = nb()
                P.mm(pt3[:, 0:256], trirev[:], Lb[:, 0, :], True, True, [b_trirev, b_Lb], [pb3])
                P.act(E3[:, 0, :], pt3[:, 0:256], AF.Exp, [pb3], [b_E3])
                pt4, pb4 = nb()
                for kt in range(8):
                    P.mm(pt4[:, 0:256], xnT[:, kt, ts_ * 128:(ts_ + 1) * 128], qkv[:, kt, 256:512], kt == 0, kt == 7, [b_xnT, qkb], [pb4])
                P.tt("dve", khat[:, ts_, :], pt4[:, 0:256], E3[:, 0, :], ALU.mult, [pb4, b_E3], [b_khat])
            for which in ([] if so else range(2)):
                for pair in range(2):
                    c0 = which * 256 + pair * 128
                    pt, pb = nb()
                    for kt in range(8):
                        P.mm(pt[:, 0:T], qkv[:, kt, c0:c0 + 128], xnT[:, kt, 0:T], kt == 0, kt == 7, [qkb, b_xnT], [pb])
                    if which == 0:
                        P.stt("dve", qt[:, pair, 0:T], E1[:, pair, 0:T], 0.125, pt[:, 0:T], ALU.mult, ALU.mult, [b_E1, pb], [b_qt])
                        qv = qt[:, pair, 0:T].rearrange("p (n c t) -> p n c t", c=2, t=64)
                        P.cp("act", qa[:, pair, 0:T].rearrange("p (n c t) -> p n c t", c=2, t=64)[:, :, 0, :], qv[:, :, 0, :], [b_qt], [b_qa])
                        P.cp("act", qb[:, pair, 0:T].rearrange("p (n c t) -> p n c t", c=2, t=64)[:, :, 1, :], qv[:, :, 1, :], [b_qt], [b_qb])
                    else:
                        P.tt("dve", ktl[:, pair, 0:T], E2[:, pair, 0:T], pt[:, 0:T], ALU.mult, [b_E2, pb], [b_ktl])
            tok_proj(T, "w_in", 512, 512, lambda ts_, p, pb: P.cp("act", vb[:, ts_, :], p, [pb], [b_vb]))
            if not so:
                tok_proj(T, "w_in", 1024, 512, lambda ts_, p, pb: P.act(srb[:, ts_, :], p, AF.Silu, [pb], [b_srb]))
            for i in ([] if so else range(4)):
                tok_proj(T, "w_in", 2064 + i * 512, 512,
                         lambda ts_, p, pb, i=i: P.act((gatesA if i < 2 else gatesB)[:, ts_, (i % 2) * 512:(i % 2 + 1) * 512], p,
                                                       AF.Sigmoid, [pb], [b_gA if i < 2 else b_gB]))
            if so:
                for ts_ in range(NT):
                    for pair in range(2):
                        P.cp("act", dS[:, pair, :], E1[:, pair, ts_ * 128 + 63:ts_ * 128 + 128:64], [b_E1], [b_dS])
                    if balanced is True:
                        P.tt("dve", Dtot[:], Dtot[:], dS[:, :, 0], ALU.mult, [b_Dtot, b_dS], [b_Dtot])
                        P.tt("dve", Dtot[:], Dtot[:], dS[:, :, 1], ALU.mult, [b_Dtot, b_dS], [b_Dtot])
                    for c2 in range(2):
                        ps_ = slice(c2 * 64, c2 * 64 + 64)
                        src, dst = (Sf[0], Sf[1]) if c2 == 0 else (Sf[1], Sf[0])
                        for pair in range(2):
                            pt, pb = nb()
                            P.mm(pt[:, 0:256], khat[ps_, ts_, pair * 128:(pair + 1) * 128], vb[ps_, ts_, pair * 256:(pair + 1) * 256],
                                 True, True, [b_khat, b_vb], [pb])
                            for hp in range(2):
                                rs = slice(hp * 64, hp * 64 + 64)
                                P.stt("dve", dst[0][rs, pair, :], src[0][rs, pair, :], dS[rs, pair, c2:c2 + 1],
                                      pt[rs, hp * 128:(hp + 1) * 128], ALU.mult, ALU.add, [src[1], b_dS, pb], [dst[1]])
                return
            for ts_ in range(NT):
                tsl = slice(ts_ * 128, (ts_ + 1) * 128)
                if gla_mode == "chain":
                    s0f, s0b = Sf[0], Sb[0]
                    s1f, s1b = Sf[1], Sb[1]
                else:
                    s0f, s0b = Sf[0], Sb[0]
                    s1f, s1b = Sf[1], Sb[1]
                for pair in range(2):
                    P.cp("act", dS[:, pair, :], E1[:, pair, ts_ * 128 + 63:ts_ * 128 + 128:64], [b_E1], [b_dS])
                for h in range(4):
                    pair, hp = h // 2, h % 2
                    rs = slice(hp * 64, hp * 64 + 64)
                    pt, pb = nb()
                    P.mm(pt[:, 0:128], ktl[rs, pair, tsl], qt[rs, pair, tsl], True, True, [b_ktl, b_qt], [pb])
                    P.tt("dve", ATb[:, h, :], pt[:, 0:128], cmask[:], ALU.mult, [pb, b_cmask], [b_ATb])
                kv = []
                for c2 in range(2):
                    ps_ = slice(c2 * 64, c2 * 64 + 64)
                    row = []
                    for pair in range(2):
                        pt, pb = nb()
                        P.mm(pt[:, 0:256], khat[ps_, ts_, pair * 128:(pair + 1) * 128], vb[ps_, ts_, pair * 256:(pair + 1) * 256],
                             True, True, [b_khat, b_vb], [pb])
                        row.append((pt, pb))
                    kv.append(row)

                def upd(dst_t, dst_b, src_t, src_b, c2, bf_t=None, bf_b=None):
                    for pair in range(2):
                        pt, pb = kv[c2][pair]
                        for hp in range(2):
                            rs = slice(hp * 64, hp * 64 + 64)
                            P.stt("dve", dst_t[rs, pair, :], src_t[rs, pair, :], dS[rs, pair, c2:c2 + 1],
                                  pt[rs, hp * 128:(hp + 1) * 128], ALU.mult, ALU.add, [src_b, b_dS, pb], [dst_b])
                    if bf_t is not None:
                        P.cp("act", bf_t[:], dst_t[:], [dst_b], [bf_b])

                if gla_mode == "chain":
                    upd(s1f[0], s1f[1], s0f[0], s0f[1], 0, s1b[0], s1b[1])
                else:
                    upd(So[0][0], So[0][1], s0f[0], s0f[1], 0)
                    upd(So[1][0], So[1][1], s1f[0], s1f[1], 1)
                po, pob = nb()
                for h in range(4):
                    pair, hp = h // 2, h % 2
                    rs = slice(hp * 64, hp * 64 + 64)
                    o_ = po[:, h * 128:(h + 1) * 128]
                    P.mm(o_, ATb[:, h, :], vb[:, ts_, h * 128:(h + 1) * 128], True, False, [b_ATb, b_vb], [pob])
                    P.mm(o_, qa[rs, pair, tsl], s0b[0][rs, pair, :], False, False, [b_qa, s0b[1]], [pob])
                    P.mm(o_, qb[rs, pair, tsl], s1b[0][rs, pair, :], False, True, [b_qb, s1b[1]], [pob])
                if gla_mode == "chain":
                    upd(s0f[0], s0f[1], s1f[0], s1f[1], 1, s0b[0], s0b[1])
                for h in range(4):
                    P.act(junk[:, 0:128], po[:, h * 128:(h + 1) * 128], AF.Square, [pob], [b_junk, b_ssq], accum=ssq[:, 4 + h:5 + h])
                P.act(rstd[:, 4:8], ssq[:, 4:8], AF.Sqrt, [b_ssq], [b_rstd], bias=EPS, scale=1.0 / 128)
                P.recip(rstd[:, 4:8], rstd[:, 4:8], [b_rstd], [b_rstd])
                for h in range(4):
                    P.stt("dve", onb[:, h * 128:(h + 1) * 128], po[:, h * 128:(h + 1) * 128], rstd[:, 4 + h:5 + h],
                          srb[:, ts_, h * 128:(h + 1) * 128], ALU.mult, ALU.mult, [pob, b_rstd, b_srb], [b_onb])
                bt, bb = nbb()
                for ct in range(4):
                    P.tr(bt[:, ct * 128:(ct + 1) * 128], onb[:, ct * 128:(ct + 1) * 128], identb[:], [b_onb, b_identb], [bb])
                P.cp("act", onT[:, :, tsl], bt[:, 0:512].rearrange("p (k t) -> p k t", k=4), [bb], [b_onT])
            if gla_mode == "chain":
                pass
            else:
                for q in range(2):
                    gla_out(q, So[q])
            Xv = Xbf[0:64, 0:2 * 32 * NC].rearrange("p (s g c) -> p s g c", s=2, g=32)
            for q in ([] if so else range(Q)):
                P.cp("dve", Xv[:, :, :, q * NCQ:(q + 1) * NCQ], Bv[:, :, :, q, 0:NCQ], [b_Bst], [b_actb])
            for g0 in ([] if so else range(0, 32, 4)):
                pt, pb = nb()
                for gg in range(4):
                    g = g0 + gg
                    o_ = pt[0:NC, gg * 128:(gg + 1) * 128]
                    P.mm(o_, Ug[:, g, 0:NC], T0[:, g, :], True, False, [b_actb, b_T0], [pb])
                    P.mm(o_, Xv[:, 0, g, :], Vm[:, g, 0, :], False, False, [b_actb, b_Vm], [pb])
                    P.mm(o_, Xv[:, 1, g, :], Vm[:, g, 1, :], False, True, [b_actb, b_Vm], [pb])
                P.act(Gc[0:NC, :, g0 * 16:(g0 + 4) * 16].rearrange("c t (g j) -> c t g j", g=4),
                      pt[0:NC, 0:512].rearrange("c (g t j) -> c t g j", g=4, t=8), AF.Gelu, [pb], [b_actb])
            for th_ in ([] if so else range(2)):
                bt, bb = nbb()
                for tl in range(4):
                    for ct in range(4):
                        i = tl * 4 + ct
                        P.tr(bt[:, i * NC:(i + 1) * NC], Gc[0:NC, th_ * 4 + tl, ct * 128:(ct + 1) * 128], identb[0:NC, 0:NC],
                             [b_actb, b_identb], [bb])
                P.cp("dve", g5T[:, :, 0:T].rearrange("p k (c t) -> p t k c", t=8)[:, th_ * 4:(th_ + 1) * 4],
                     bt[:, 0:16 * NC].rearrange("p (t k c) -> p t k c", t=4, k=4), [bb], [b_g5T])
            wa_v, wa_b = load_blk("w_glu_a", lambda a: a, lambda t: t[:, 0:2048].rearrange("p (k n) -> p k n", k=4))
            wb_v, wb_b = load_blk("w_glu_b", lambda a: a, lambda t: t[:, 0:2048].rearrange("p (k n) -> p k n", k=4))
            for nt_ in range(4):
                pa, pab = nb()
                for ct in range(4):
                    P.mm(pa[:, 0:T], wa_v[:, ct, nt_ * 128:(nt_ + 1) * 128], g5T[:, ct, 0:T], ct == 0, ct == 3, [wa_b, b_g5T], [pab])
                pb_, pbb = nb()
                for ct in range(4):
                    P.mm(pb_[:, 0:T], wb_v[:, ct, nt_ * 128:(nt_ + 1) * 128], g5T[:, ct, 0:T], ct == 0, ct == 3, [wb_b, b_g5T], [pbb])
                P.act(sgt[:, 0:T], pb_[:, 0:T], AF.Sigmoid, [pbb], [b_sgt])
                P.tt("dve", glu[:, nt_, 0:T], sgt[:, 0:T], pa[:, 0:T], ALU.mult, [b_sgt, pab], [b_glu])
            wgo_v, wgo_b = load_blk("w_gla_out", lambda a: a, lambda t: t[:, 0:4096].rearrange("p (k n) -> p k n", k=4))
            wso_v, wso_b = load_blk("w_s5_out", lambda a: a, lambda t: t[:, 0:4096].rearrange("p (k n) -> p k n", k=4))
            for ts_ in range(NT):
                tsl = slice(ts_ * 128, (ts_ + 1) * 128)
                for half in range(2):
                    hs = slice(half * 512, (half + 1) * 512)
                    pg_, pgb_ = nb()
                    for ct in range(4):
                        P.mm(pg_[:, :], onT[:, ct, tsl], wgo_v[:, ct, hs], ct == 0, ct == 3, [b_onT, wgo_b], [pgb_])
                    ps2, psb2 = nb()
                    for ct in range(4):
                        P.mm(ps2[:, :], glu[:, ct, tsl], wso_v[:, ct, hs], ct == 0, ct == 3, [b_glu, wso_b], [psb2])
                    P.tt("dve", mf[:], gatesA[:, ts_, hs], pg_[:, :], ALU.mult, [b_gA, pgb_], [b_mf])
                    P.tt("dve", sgt[:, 0:512], gatesB[:, ts_, hs], ps2[:, :], ALU.mult,
                         [b_gB, psb2], [b_sgt])
                    P.tt("dve", mbf[:, hs], mf[:], sgt[:, 0:512], ALU.add, [b_mf, b_sgt], [b_mbf])
                bt, bb = nbb()
                for kt in range(8):
                    P.tr(bt[:, kt * 128:(kt + 1) * 128], mbf[:, kt * 128:(kt + 1) * 128], identb[:], [b_mbf, b_identb], [bb])
                P.cp("act", xnT[:, :, tsl], bt[:].rearrange("p (k t) -> p k t", k=8), [bb], [b_xnT])
            for half in range(2):
                hs = slice(half * 512, (half + 1) * 512)
                wo_v, wo_b = load_blk("w_out", lambda a: a[:, :, half * 512:(half + 1) * 512],
                                      lambda t: t[:, 0:4096].rearrange("p (k n) -> p k n", k=8))
                for ts_ in range(NT):
                    pt, pb = nb()
                    for kt in range(8):
                        P.mm(pt[:, :], xnT[:, kt, ts_ * 128:(ts_ + 1) * 128], wo_v[:, kt, :], kt == 0, kt == 7, [b_xnT, wo_b], [pb])
                    P.tt("dve", h_t[:, ts_, hs], h_t[:, ts_, hs], pt[:, :], ALU.add, [h_b, pb], [h_b])

        ystage = actb[:].rearrange("p f t -> p (f t)").bitcast(F32)[:, 0:4 * D].rearrange("p (n d) -> p n d", n=4)

        def final_norm(T, h_t, h_b):
            NT = T // 128
            for ts_ in range(NT):
                P.act(junk[:], h_t[:, ts_, :], AF.Square, [h_b], [b_junk, b_ssq], accum=ssq[:, ts_:ts_ + 1])
                P.act(rstd[:, ts_:ts_ + 1], ssq[:, ts_:ts_ + 1], AF.Sqrt, [b_ssq], [b_rstd], bias=EPS, scale=1.0 / D)
                P.recip(rstd[:, ts_:ts_ + 1], rstd[:, ts_:ts_ + 1], [b_rstd], [b_rstd])
                P.stt("dve", ystage[:, ts_, :], h_t[:, ts_, :], rstd[:, ts_:ts_ + 1], gfin[:], ALU.mult, ALU.mult,
                      [h_b, b_rstd, b_gfin], [b_actb])

        s5_init_reads = None
        def s5o(q, view):
            P.cp("pool", xo[:], view, [b_Bst], [b_xo])

        if ntiles > 0:
            P.memset("dve", Sf[0][0][:], 0.0, [Sf[0][1]])
            P.memset("dve", Sb[0][0][:], 0.0, [Sb[0][1]])
            P.memset("dve", xo[:], 0.0, [b_xo])
        hb = [Buf("hscr%d" % ti) for ti in range(ntiles)]
        if balanced is True and ntiles > 0:
            P.memset("dve", Dtot[:], 1.0, [b_Dtot])
            for ti in range(ntiles):
                P.dma(xt[:, :, :], xp[ti * 512:(ti + 1) * 512, :].rearrange("(n p) d -> p n d", p=128), writes=[b_xt], sbuf=b_xt)
                ffn(512, xt, b_xt, "w_ffn1_gate", "w_ffn1_up", "w_ffn1_down")
                P.dma(hscr[ti * 512:(ti + 1) * 512, :].rearrange("(n p) d -> p n d", p=128), xt[:, :, :], reads=[b_xt],
                      writes=[hb[ti]], sbuf=b_xt)
                mixer(512, xt, b_xt, 1, lambda q: xo[:], "chain", None, s5o, so=True)
            P.memset("dve", exs[:], 0.0, [b_exs])
            P.cp("dve", exs[:, 0:256], Sf[0][0][:].rearrange("p a e -> p (a e)"), [Sf[0][1]], [b_exs])
            P.cp("dve", exs[:, 256:258], Dtot[:], [b_Dtot], [b_exs])
            P.cp("dve", exs[0:64, 258:322], xo[:].rearrange("p s g -> p (s g)"), [b_xo], [b_exs])
            b_exsrc, b_exdst, b_cc = Buf("exsrc"), Buf("exdst"), Buf("cc")
            P.dma(exsrc, exs[:], reads=[b_exs], writes=[b_exsrc], sbuf=b_exs)
            if NO_COLL:
                P.dma(exdst[0:128, :], exsrc, reads=[b_exsrc], writes=[b_exdst], sbuf=b_cc)
            else:
                P.coll("AllGather", exsrc, exdst, [list(range(NCORES))], reads=[b_exsrc], writes=[b_exdst], sbuf=b_cc)
            accS, b_accS = So[0]
            tmpS, b_tmpS = So[1]
            accX, b_accX = xo2[0]
            tmpX, b_tmpX = xo2[1]
            P.memset("dve", accS[:], 0.0, [b_accS])
            P.memset("dve", accX[:], 0.0, [b_accX])
            for i in range(NCORES):
                P.dma(exg[:], exdst[i * 128:(i + 1) * 128, :], reads=[b_exdst], writes=[b_exg], sbuf=b_exg)
                fi = flags[:, i:i + 1]
                for pair in range(2):
                    P.stt("dve", tmpS[:, pair, :], accS[:, pair, :], exg[:, 256 + pair:257 + pair], exg[:, pair * 128:(pair + 1) * 128],
                          ALU.mult, ALU.add, [b_accS, b_exg], [b_tmpS])
                P.tt("dve", tmpS[:], tmpS[:], accS[:], ALU.subtract, [b_tmpS, b_accS], [b_tmpS])
                P.stt("dve", accS[:], tmpS[:], fi, accS[:], ALU.mult, ALU.add, [b_tmpS, b_flags, b_accS], [b_accS])
                xi = exg[0:64, 258:322].rearrange("p (s g) -> p s g", s=2)
                P.tt("dve", st1[:], AB1[:], accX[:], ALU.mult, [b_AB1, b_accX], [b_st1])
                P.tt("dve", st2[:, 0, :], AB2[:, 0, :], accX[:, 1, :], ALU.mult, [b_AB2, b_accX], [b_st2])
                P.tt("dve", st2[:, 1, :], AB2[:, 1, :], accX[:, 0, :], ALU.mult, [b_AB2, b_accX], [b_st2])
                P.tt("dve", st1[:], st1[:], st2[:], ALU.add, [b_st1, b_st2], [b_st1])
                P.tt("dve", st1[:], st1[:], xi, ALU.add, [b_st1, b_exg], [b_st1])
                P.tt("dve", tmpX[:], st1[:], accX[:], ALU.subtract, [b_st1, b_accX], [b_tmpX])
                P.stt("dve", accX[:], tmpX[:], fi[0:64, :], accX[:], ALU.mult, ALU.add, [b_tmpX, b_flags, b_accX], [b_accX])
            P.cp("dve", Sf[0][0][:], accS[:], [b_accS], [Sf[0][1]])
            P.cp("dve", Sb[0][0][:], accS[:], [b_accS], [Sb[0][1]])
            P.cp("dve", xo[:], accX[:], [b_accX], [b_xo])
        if balanced == "prefix":
            for ti in range(NPRE):
                P.dma(xt[:, :, :], xp[ti * 512:(ti + 1) * 512, :].rearrange("(n p) d -> p n d", p=128), writes=[b_xt], sbuf=b_xt)
                ffn(512, xt, b_xt, "w_ffn1_gate", "w_ffn1_up", "w_ffn1_down")
                mixer(512, xt, b_xt, 1, lambda q: xo[:], "chain", None, s5o, so=True)
            P.cp("dve", Sb[0][0][:], Sf[0][0][:], [Sf[0][1]], [Sb[0][1]])
        for ti in range(ntiles):
            if balanced is True:
                P.dma(xt[:, :, :], hscr[ti * 512:(ti + 1) * 512, :].rearrange("(n p) d -> p n d", p=128), reads=[hb[ti]],
                      writes=[b_xt], sbuf=b_xt)
            else:
                t0_ = (NPRE + ti) * 512
                P.dma(xt[:, :, :], xp[t0_:t0_ + 512, :].rearrange("(n p) d -> p n d", p=128), writes=[b_xt], sbuf=b_xt)
                ffn(512, xt, b_xt, "w_ffn1_gate", "w_ffn1_up", "w_ffn1_down")
            mixer(512, xt, b_xt, 1, lambda q: xo[:], "chain", None, s5o)
            ffn(512, xt, b_xt, "w_ffn2_gate", "w_ffn2_up", "w_ffn2_down")
            final_norm(512, xt, b_xt)
            P.dma(yp[ti * 512:(ti + 1) * 512, :].rearrange("(n p) d -> p n d", p=128), ystage[:, :, :], reads=[b_actb], sbuf=b_actb)
        if ntiles > 0:
            P.dma(glap_o, Sf[0][0][:], reads=[Sf[0][1]], sbuf=Sf[0][1])
            P.dma(s5p_o, xo[:], reads=[b_xo], sbuf=b_xo)
        P.dma(xt[:, 0, :], xs, writes=[b_xt], sbuf=b_xt)
        for q in range(2):
            P.dma(Sf[q][0][:], sgla[q], writes=[Sf[q][1]], sbuf=Sf[q][1])
            P.cp("pool", Sb[q][0][:], Sf[q][0][:], [Sf[q][1]], [Sb[q][1]])
        P.dma(xin[:], ss5, writes=[b_xin], sbuf=b_xin)
        s5_init_reads = [b_xin]
        ffn(128, xt, b_xt, "w_ffn1_gate", "w_ffn1_up", "w_ffn1_down")

        def s5o_s(q, view):
            P.cp("pool", xo2[q][0][:], view, [b_Bst], [xo2[q][1]])
            P.dma(s5s_o[q], xo2[q][0][:], reads=[xo2[q][1]], sbuf=xo2[q][1])

        def glao_s(q, so):
            P.dma(glas_o[q], so[0][:], reads=[so[1]], sbuf=so[1])

        mixer(128, xt, b_xt, 2, lambda q: xin[:, :, :, q], "indep", glao_s, s5o_s)
        ffn(128, xt, b_xt, "w_ffn2_gate", "w_ffn2_up", "w_ffn2_down")
        final_norm(128, xt, b_xt)
        P.dma(ys, ystage[:, 0, :], reads=[b_actb], sbuf=b_actb)
        P.emit(st)
    return nc


def _consts():
    idx = np.arange(128)
    same = (idx[:, None] // 64) == (idx[None, :] // 64)
    tri_inc = np.where(same & (idx[:, None] <= idx[None, :]), -1.0 / 16.0, 0.0).astype(np.float32)
    tri_rev = np.where(same & (idx[:, None] > idx[None, :]), -1.0 / 16.0, 0.0).astype(np.float32)
    cmask = np.where(same & (idx[:, None] <= idx[None, :]), 1.0, 0.0).astype(np.float32)
    s5mask = np.where((idx[None, :] // 16) >= (idx[:, None] // 16), 1.0, 0.0).astype(np.float32)
    return dict(ident=np.eye(128, dtype=np.float32), tri_inc=tri_inc, tri_rev=tri_rev, cmask=cmask, s5mask=s5mask)


def _lay_w(w):
    K, N = w.shape
    return np.ascontiguousarray(w.reshape(K // 128, 128, N).transpose(1, 0, 2))


def make_in_maps(inp, ntiles, seq_of_core, seg_of_core=None, prefix_tiles=0):
    if seg_of_core is None:
        seg_of_core = [0] * NCORES
    c = _consts()
    shared = dict(c)
    for name, K, N, _g in W_SPECS:
        shared[name] = _lay_w(np.asarray(inp[name], dtype=np.float32))
    for name, n in GAINS:
        shared[name] = np.ascontiguousarray(np.asarray(inp[name], np.float32).reshape(n, 128).T)
    shared["g_final"] = np.ascontiguousarray(np.broadcast_to(np.asarray(inp["g_final"], np.float32)[None, :], (128, D)))
    shared["w_gate_up"] = np.ascontiguousarray(inp["w_gate_up"], dtype=np.float32)
    shared["b_gate"] = np.ascontiguousarray(np.asarray(inp["b_gate"], np.float32)[None, :])
    shared["lam_re"] = np.ascontiguousarray(np.asarray(inp["s5_lam_re"], np.float32).T)
    shared["lam_im"] = np.ascontiguousarray(np.asarray(inp["s5_lam_im"], np.float32).T)
    shared["log_dt"] = np.ascontiguousarray(np.broadcast_to(np.asarray(inp["s5_log_dt"], np.float32)[None, :], (64, 32)))
    shared["b_re"] = np.ascontiguousarray(np.asarray(inp["s5_b_re"], np.float32).transpose(1, 0, 2))
    shared["b_im"] = np.ascontiguousarray(np.asarray(inp["s5_b_im"], np.float32).transpose(1, 0, 2))
    shared["c_re"] = np.ascontiguousarray(np.asarray(inp["s5_c_re"], np.float32).transpose(2, 0, 1))
    shared["c_im"] = np.ascontiguousarray(np.asarray(inp["s5_c_im"], np.float32).transpose(2, 0, 1))
    d = np.asarray(inp["s5_d"], np.float32).reshape(32, 16).T
    shared["dcol"] = np.ascontiguousarray(np.tile(d, (8, 1)))
    xp = np.asarray(inp["x_prompt"], np.float32)
    xs = np.asarray(inp["x_sample"], np.float32)
    sg = np.asarray(inp["state_gla"], np.float32)
    s5 = np.asarray(inp["state_s5"], np.float32)
    maps = []
    for core in range(NCORES):
        m = dict(shared)
        n0 = seg_of_core[core] * ntiles * 512
        if prefix_tiles:
            npre = prefix_tiles * 512
            buf = np.zeros((npre + ntiles * 512, D), np.float32)
            lo = max(0, n0 - npre)
            buf[npre - (n0 - lo):] = xp[seq_of_core[core]][lo:n0 + ntiles * 512]
            m["xp"] = buf
        else:
            m["xp"] = np.ascontiguousarray(xp[seq_of_core[core]][n0:n0 + ntiles * 512])
        fl = np.zeros((128, 8), np.float32)
        for i in range(NCORES):
            if seq_of_core[i] == seq_of_core[core] and seg_of_core[i] < seg_of_core[core]:
                fl[:, i] = 1.0
        m["flags"] = fl
        m["xs"] = np.ascontiguousarray(xs[2 * core:2 * core + 2].reshape(128, D))
        g2 = sg[2 * core:2 * core + 2]
        m["sgla"] = np.ascontiguousarray(g2.reshape(2, 2, 2, 64, 128).transpose(0, 2, 3, 1, 4).reshape(2, 128, 2, 128))
        s2 = s5[2 * core:2 * core + 2]
        m["ss5"] = np.ascontiguousarray(s2.transpose(2, 3, 1, 0))
        maps.append(m)
    return maps


def _unlay_gla(a):
    return a.reshape(2, 64, 2, 128).transpose(2, 0, 1, 3).reshape(4, 64, 128)


def _unlay_s5(a):
    return a.transpose(2, 0, 1)


_NC_CACHE = {}


def run(inp, ntiles, seq_of_core, seg_of_core=None, balanced=True):
    key = (ntiles, balanced)
    if key not in _NC_CACHE:
        _NC_CACHE[key] = build(ntiles, balanced)
    nc = _NC_CACHE[key]
    maps = make_in_maps(inp, ntiles, seq_of_core, seg_of_core, PREFIX_TILES if balanced == "prefix" else 0)
    res = run_bass_kernel_spmd(nc, maps, core_ids=list(range(NCORES)))
    return res.results


def kernel(**inputs):
    seq_of_core = [c // 4 for c in range(NCORES)]
    seg_of_core = [c % 4 for c in range(NCORES)]
    ntiles = 8
    r = run(inputs, ntiles, seq_of_core, seg_of_core, "prefix")
    y_prompt = np.stack([np.concatenate([r[q * 4 + j]["yp"] for j in range(4)]) for q in range(2)]).astype(np.float32)
    y_sample = np.concatenate([r[c]["ys"].reshape(2, 64, D) for c in range(NCORES)]).astype(np.float32)
    gla_p = np.stack([_unlay_gla(r[3]["glap"]), _unlay_gla(r[7]["glap"])]).astype(np.float32)
    s5_p = np.stack([_unlay_s5(r[3]["s5p"]), _unlay_s5(r[7]["s5p"])]).astype(np.float32)
    gla_s = np.stack([_unlay_gla(r[c]["glas"][q]) for c in range(NCORES) for q in range(2)]).astype(np.float32)
    s5_s = np.stack([_unlay_s5(r[c]["s5s"][q]) for c in range(NCORES) for q in range(2)]).astype(np.float32)
    return (y_prompt, y_sample, gla_p, s5_p, gla_s, s5_s)
```
